# Optimizing a Trainium2 kernel written in Bass

```python
import jax, jax.numpy as jnp
from jax import lax
import numpy as np

D_MODEL = 1024
BATCH = 8
SEQ = 4096
DEPTH = 2

N_EVEN = (DEPTH + 1) // 2
N_ODD = DEPTH // 2
PLE_DIM = 256
NORM_EPS = 1e-6
NEG_INF = -1e30

CONV_CH = D_MODEL // 2
CONV_WIDTH = 31

HEAD_DIM = 64
NSA_HEADS = (D_MODEL // 2) // HEAD_DIM
NSA_KV_HEADS = 2
NSA_HPG = NSA_HEADS // NSA_KV_HEADS
KV_WIDTH = NSA_KV_HEADS * HEAD_DIM
CMP_LEN = 32
CMP_STRIDE = 16
CMP_HIDDEN = 256
SLC_BLOCK = 64
SLC_TOP = 16
WINDOW = 512
Q_BLOCK = 128
FORCE_SCORE = 1e9

MIX_WIDTH = CONV_CH + NSA_HEADS * HEAD_DIM
IN_WIDTHS = (CONV_CH, CONV_CH, NSA_HEADS * HEAD_DIM) + (KV_WIDTH,) * 6 + (3 * NSA_HEADS,)
IN_COLS = sum(IN_WIDTHS)
IN_SPLITS = [int(v) for v in np.cumsum(IN_WIDTHS)[:-1]]

POOL_WINDOWS = (2, 4, 8, 16)
POOL_GROUP = D_MODEL // len(POOL_WINDOWS)

PEER_HEADS = 8
PEER_QDIM = 128
N_KEYS = 128
N_EXPERTS = N_KEYS * N_KEYS
PEER_TOPK = 16
PEER_CHUNK = 128

kernel_name = "hybrid_conv_nsa_pool_peer"


def rmsnorm(x, g):
    xf = x.astype(jnp.float32)
    y = xf * lax.rsqrt(jnp.mean(xf * xf, axis=-1, keepdims=True) + NORM_EPS)
    return (y * g.astype(jnp.float32)).astype(x.dtype)


def masked_softmax(s, mask):
    s = jnp.where(mask, s.astype(jnp.float32), NEG_INF)
    p = jax.nn.softmax(s, axis=-1)
    return jnp.where(mask, p, 0.0)


def conformer_conv(val, gate, conv_w, conv_b, ln_g, ln_b):
    a = val * jax.nn.sigmoid(gate)
    y = lax.conv_general_dilated(
        a, conv_w[:, None, :].astype(a.dtype), window_strides=(1,),
        padding=[(CONV_WIDTH - 1, 0)], dimension_numbers=('NWC', 'WIO', 'NWC'),
        feature_group_count=CONV_CH)
    y = (y + conv_b).astype(jnp.float32)
    mu = jnp.mean(y, axis=-1, keepdims=True)
    var = jnp.mean(jnp.square(y - mu), axis=-1, keepdims=True)
    y = (y - mu) * lax.rsqrt(var + NORM_EPS) * ln_g.astype(jnp.float32) + ln_b.astype(jnp.float32)
    return jax.nn.silu(y).astype(val.dtype)


def compress_blocks(src, pos, w1, w2):
    B, T = src.shape[:2]
    n_cmp = (T - CMP_LEN) // CMP_STRIDE + 1
    idx = jnp.arange(n_cmp)[:, None] * CMP_STRIDE + jnp.arange(CMP_LEN)[None, :]
    blk = src[:, idx] + pos[None, None, :, None, :]
    blk = blk.transpose(0, 3, 1, 2, 4).reshape(B, NSA_KV_HEADS, n_cmp, CMP_LEN * HEAD_DIM)
    return jax.nn.gelu(blk @ w1) @ w2


def nsa_attention(q, kc, vc, ks, vs, kw, vw, gates, cmp_pos, cmp_w1, cmp_w2):
    B, T = q.shape[:2]
    n_qb = T // Q_BLOCK
    n_slc = T // SLC_BLOCK
    n_top = min(SLC_TOP, n_slc)
    n_cmp = (T - CMP_LEN) // CMP_STRIDE + 1
    scale = HEAD_DIM ** -0.5

    k_cmp = compress_blocks(kc, cmp_pos[0], cmp_w1[0], cmp_w2[0])
    v_cmp = compress_blocks(vc, cmp_pos[1], cmp_w1[1], cmp_w2[1])
    cmp_start = jnp.arange(n_cmp) * CMP_STRIDE
    cmp_end = cmp_start + CMP_LEN - 1
    slc_idx = jnp.arange(n_slc)
    slc_start = slc_idx * SLC_BLOCK
    overlap = ((cmp_start[:, None] < slc_start[None, :] + SLC_BLOCK)
               & (cmp_start[:, None] + CMP_LEN > slc_start[None, :])).astype(jnp.float32)

    ks_blk = ks.transpose(0, 2, 1, 3).reshape(B, NSA_KV_HEADS, n_slc, SLC_BLOCK, HEAD_DIM)
    vs_blk = vs.transpose(0, 2, 1, 3).reshape(B, NSA_KV_HEADS, n_slc, SLC_BLOCK, HEAD_DIM)
    pad = ((0, 0), (0, 0), (WINDOW, 0), (0, 0))
    kw_pad = jnp.pad(kw.transpose(0, 2, 1, 3), pad)
    vw_pad = jnp.pad(vw.transpose(0, 2, 1, 3), pad)
    bi = jnp.arange(B)[:, None, None, None]
    gi = jnp.arange(NSA_KV_HEADS)[None, :, None, None]
    in_blk = jnp.arange(SLC_BLOCK)
    win_off = jnp.arange(WINDOW + Q_BLOCK) - WINDOW

    q_blocks = q.reshape(B, n_qb, Q_BLOCK, NSA_KV_HEADS, NSA_HPG, HEAD_DIM).transpose(1, 0, 3, 4, 2, 5)
    g_blocks = gates.reshape(B, n_qb, Q_BLOCK, NSA_KV_HEADS, NSA_HPG, 3).transpose(1, 0, 3, 4, 2, 5)

    def block_fn(args):
        c, qc, gc = args
        t = c * Q_BLOCK + jnp.arange(Q_BLOCK)
        s = jnp.einsum('bghqd,bgnd->bghqn', qc, k_cmp) * scale
        p_cmp = masked_softmax(s, cmp_end[None, :] <= t[:, None])
        o_cmp = jnp.einsum('bghqn,bgnd->bghqd', p_cmp.astype(qc.dtype), v_cmp)
        imp = jnp.einsum('bghqn,ns->bgqs', p_cmp, overlap)
        cur = t // SLC_BLOCK
        forced = ((slc_idx[None, :] == 0) | (slc_idx[None, :] == cur[:, None])
                  | (slc_idx[None, :] == cur[:, None] - 1))
        imp = jnp.where(forced, FORCE_SCORE, imp)
        imp = jnp.where(slc_start[None, :] <= t[:, None], imp, NEG_INF)
        _, sel = lax.top_k(imp, n_top)
        k_sel = ks_blk[bi, gi, sel]
        v_sel = vs_blk[bi, gi, sel]
        kpos = sel[..., None] * SLC_BLOCK + in_blk
        mask = (kpos <= t[:, None, None]).reshape(B, NSA_KV_HEADS, 1, Q_BLOCK, n_top * SLC_BLOCK)
        s = jnp.einsum('bghqd,bgqnkd->bghqnk', qc, k_sel) * scale
        p = masked_softmax(s.reshape(B, NSA_KV_HEADS, NSA_HPG, Q_BLOCK, n_top * SLC_BLOCK), mask)
        p = p.reshape(B, NSA_KV_HEADS, NSA_HPG, Q_BLOCK, n_top, SLC_BLOCK).astype(qc.dtype)
        o_slc = jnp.einsum('bghqnk,bgqnkd->bghqd', p, v_sel)
        kwin = lax.dynamic_slice_in_dim(kw_pad, c * Q_BLOCK, WINDOW + Q_BLOCK, axis=2)
        vwin = lax.dynamic_slice_in_dim(vw_pad, c * Q_BLOCK, WINDOW + Q_BLOCK, axis=2)
        wpos = c * Q_BLOCK + win_off
        dist = t[:, None] - wpos[None, :]
        wmask = (dist >= 0) & (dist < WINDOW) & (wpos[None, :] >= 0)
        s = jnp.einsum('bghqd,bgkd->bghqk', qc, kwin) * scale
        p = masked_softmax(s, wmask).astype(qc.dtype)
        o_win = jnp.einsum('bghqk,bgkd->bghqd', p, vwin)
        gt = jax.nn.sigmoid(gc.astype(jnp.float32)).astype(qc.dtype)
        return gt[..., 0:1] * o_cmp + gt[..., 1:2] * o_slc + gt[..., 2:3] * o_win

    o = lax.map(block_fn, (jnp.arange(n_qb), q_blocks, g_blocks))
    return o.transpose(1, 0, 4, 2, 3, 5).reshape(B, T, NSA_HEADS * HEAD_DIM)


def mixer_conv_nsa(hn, w_in, conv_w, conv_b, ln_g, ln_b, cmp_pos, cmp_w1, cmp_w2, w_out):
    B, T, _ = hn.shape
    z = hn @ w_in
    val, gate, q, kc, vc, ks, vs, kw, vw, g = jnp.split(z, IN_SPLITS, axis=-1)
    a_out = conformer_conv(val, gate, conv_w, conv_b, ln_g, ln_b)
    kv = lambda u: u.reshape(B, T, NSA_KV_HEADS, HEAD_DIM)
    b_out = nsa_attention(q, kv(kc), kv(vc), kv(ks), kv(vs), kv(kw), kv(vw), g,
                          cmp_pos, cmp_w1, cmp_w2)
    return jnp.concatenate([a_out, b_out], axis=-1) @ w_out


def pool_mixer(hn, pool_w, pool_scale):
    B, T, D = hn.shape
    hf = hn.astype(jnp.float32).reshape(B, T, len(POOL_WINDOWS), POOL_GROUP)
    cs = jnp.pad(jnp.cumsum(hf, axis=1), ((0, 0), (1, 0), (0, 0), (0, 0)))
    pos = jnp.arange(T)
    outs = []
    for gi, w in enumerate(POOL_WINDOWS):
        lo = jnp.maximum(pos + 1 - w, 0)
        cnt = (pos + 1 - lo).astype(jnp.float32)
        mean = (cs[:, pos + 1, gi] - cs[:, lo, gi]) / cnt[None, :, None]
        outs.append(mean - hf[:, :, gi])
    d = jnp.stack(outs, axis=2)
    y = jnp.einsum('btgc,gce->btge', d, pool_w.astype(jnp.float32)).reshape(B, T, D)
    return (y * pool_scale.astype(jnp.float32)).astype(hn.dtype)


def peer_ffn(hn, wq, subkeys, u_tab, v_tab):
    B, T, D = hn.shape
    q = (hn @ wq).reshape(B, T, PEER_HEADS, 2, PEER_QDIM // 2)
    s = jnp.einsum('bthsd,hskd->bthsk', q, subkeys).astype(jnp.float32)
    s_top, i_top = lax.top_k(s, PEER_TOPK)
    cand = s_top[..., 0, :, None] + s_top[..., 1, None, :]
    c_top, c_idx = lax.top_k(cand.reshape(B, T, PEER_HEADS, PEER_TOPK * PEER_TOPK), PEER_TOPK)
    i1 = jnp.take_along_axis(i_top[..., 0, :], c_idx // PEER_TOPK, axis=-1)
    i2 = jnp.take_along_axis(i_top[..., 1, :], c_idx % PEER_TOPK, axis=-1)
    expert = i1 * N_KEYS + i2
    gate = jax.nn.softmax(c_top, axis=-1).astype(hn.dtype)
    n_ch = (B * T) // PEER_CHUNK
    kk = PEER_HEADS * PEER_TOPK
    xs = (hn.reshape(n_ch, PEER_CHUNK, D), expert.reshape(n_ch, PEER_CHUNK, kk),
          gate.reshape(n_ch, PEER_CHUNK, kk))

    def chunk_fn(args):
        xc, ec, gc = args
        u = u_tab[ec]
        v = v_tab[ec]
        act = jax.nn.gelu(jnp.einsum('cd,ckd->ck', xc, u))
        return jnp.einsum('ck,ckd->cd', gc * act, v)

    return lax.map(chunk_fn, xs).reshape(B, T, D)


def setup_inputs(seed: int = 0) -> dict:
    key = jax.random.key(seed)
    ks = jax.random.split(key, 24)
    f32 = jnp.float32
    nrm = lambda k, shape, sc: jax.random.normal(k, shape, f32) * sc
    return {
        "x": nrm(ks[0], (BATCH, SEQ, D_MODEL), 1.0),
        "p": nrm(ks[1], (DEPTH, BATCH, SEQ, PLE_DIM), 1.0),
        "mix_norm": 1.0 + nrm(ks[2], (DEPTH, D_MODEL), 0.02),
        "ab_w_in": nrm(ks[3], (N_EVEN, D_MODEL, IN_COLS), D_MODEL ** -0.5),
        "ab_conv_w": nrm(ks[4], (N_EVEN, CONV_WIDTH, CONV_CH), CONV_WIDTH ** -0.5),
        "ab_conv_b": nrm(ks[5], (N_EVEN, CONV_CH), 0.02),
        "ab_conv_ln_g": 1.0 + nrm(ks[6], (N_EVEN, CONV_CH), 0.02),
        "ab_conv_ln_b": nrm(ks[7], (N_EVEN, CONV_CH), 0.02),
        "ab_cmp_pos": nrm(ks[8], (N_EVEN, 2, CMP_LEN, HEAD_DIM), 0.02),
        "ab_cmp_w1": nrm(ks[9], (N_EVEN, 2, CMP_LEN * HEAD_DIM, CMP_HIDDEN), (CMP_LEN * HEAD_DIM) ** -0.5),
        "ab_cmp_w2": nrm(ks[10], (N_EVEN, 2, CMP_HIDDEN, HEAD_DIM), CMP_HIDDEN ** -0.5),
        "ab_w_out": nrm(ks[11], (N_EVEN, MIX_WIDTH, D_MODEL), MIX_WIDTH ** -0.5),
        "pool_w": nrm(ks[12], (N_ODD, len(POOL_WINDOWS), POOL_GROUP, POOL_GROUP), POOL_GROUP ** -0.5),
        "pool_scale": 1.0 + nrm(ks[13], (N_ODD, D_MODEL), 0.1),
        "ffn_norm": 1.0 + nrm(ks[14], (DEPTH, D_MODEL), 0.02),
        "peer_wq": nrm(ks[15], (DEPTH, D_MODEL, PEER_HEADS * PEER_QDIM), D_MODEL ** -0.5),
        "peer_subkeys": nrm(ks[16], (DEPTH, PEER_HEADS, 2, N_KEYS, PEER_QDIM // 2), (PEER_QDIM // 2) ** -0.5),
        "peer_u": nrm(ks[17], (DEPTH, N_EXPERTS, D_MODEL), D_MODEL ** -0.5),
        "peer_v": nrm(ks[18], (DEPTH, N_EXPERTS, D_MODEL), 0.1),
        "ple_norm": 1.0 + nrm(ks[19], (DEPTH, D_MODEL), 0.02),
        "ple_gate_w": nrm(ks[20], (DEPTH, D_MODEL, D_MODEL), D_MODEL ** -0.5),
        "ple_proj": nrm(ks[21], (DEPTH, PLE_DIM, D_MODEL), PLE_DIM ** -0.5),
        "final_norm": 1.0 + nrm(ks[22], (D_MODEL,), 0.02),
    }


def reference(x, p, mix_norm, ab_w_in, ab_conv_w, ab_conv_b, ab_conv_ln_g, ab_conv_ln_b,
              ab_cmp_pos, ab_cmp_w1, ab_cmp_w2, ab_w_out, pool_w, pool_scale, ffn_norm,
              peer_wq, peer_subkeys, peer_u, peer_v, ple_norm, ple_gate_w, ple_proj, final_norm):
    h = x
    for i in range(DEPTH):
        j = i // 2
        hn = rmsnorm(h, mix_norm[i])
        if i % 2 == 0:
            h = h + mixer_conv_nsa(hn, ab_w_in[j], ab_conv_w[j], ab_conv_b[j], ab_conv_ln_g[j],
                                   ab_conv_ln_b[j], ab_cmp_pos[j], ab_cmp_w1[j], ab_cmp_w2[j],
                                   ab_w_out[j])
        else:
            h = h + pool_mixer(hn, pool_w[j], pool_scale[j])
        hn = rmsnorm(h, ffn_norm[i])
        h = h + peer_ffn(hn, peer_wq[i], peer_subkeys[i], peer_u[i], peer_v[i])
        gate = jax.nn.sigmoid(rmsnorm(h, ple_norm[i]) @ ple_gate_w[i])
        h = h + (p[i] @ ple_proj[i]) * gate
    return rmsnorm(h, final_norm)
```

```python
import numpy as np
from contextlib import ExitStack
import concourse.bass as bass
import concourse.mybir as mybir
from concourse.bass_utils import run_bass_kernel_spmd
from concourse.ap import AP

F32 = mybir.dt.float32
BF16 = mybir.dt.bfloat16
U32 = mybir.dt.uint32
ALU = mybir.AluOpType
AF = mybir.ActivationFunctionType
AX = mybir.AxisListType

T = 4096
D = 1024
NT = 32
NEG = -30000.0
EPS = 1e-6
INC = 2328
NG_ = 2
DEBUG_STOP = None


class Sched:
    def __init__(self, nc, n_dma_sems=12):
        self.nc = nc
        self.eng = {"pe": nc.tensor, "dve": nc.vector, "act": nc.scalar, "pool": nc.gpsimd, "sp": nc.sync}
        self.sem = {k: nc.alloc_semaphore("sem_" + k) for k in ("pe", "dve", "act", "pool")}
        self.cnt = {k: 0 for k in self.sem}
        self.waited = {}
        self.dsem = [nc.alloc_semaphore("dsem%d" % i) for i in range(n_dma_sems)]
        self.dcnt = [0] * n_dma_sems
        self.dnext = 0
        self.last_w = {}
        self.readers = {}
        self.ninst = 0

    def _need(self, engine, tok):
        if tok is None:
            return
        kind, idx, val = tok
        if kind == "c" and idx == "pe" and engine == "pe":
            return
        key = (engine, kind, idx)
        if self.waited.get(key, 0) >= val:
            return
        self.waited[key] = val
        sem = self.sem[idx] if kind == "c" else self.dsem[idx]
        self.eng[engine].wait_ge(sem, val)
        self.ninst += 1

    @staticmethod
    def _flat(xs):
        o = []
        for x in xs:
            if isinstance(x, (list, tuple)):
                o.extend(Sched._flat(x))
            else:
                o.append(x)
        return o

    def _deps(self, engine, reads, writes):
        for r in reads:
            self._need(engine, self.last_w.get(r))
        for w in writes:
            self._need(engine, self.last_w.get(w))
            for t in self.readers.get(w, ()):
                self._need(engine, t)

    def _record(self, tok, reads, writes):
        for r in reads:
            self.readers.setdefault(r, []).append(tok)
        for w in writes:
            self.last_w[w] = tok
            self.readers[w] = []

    def op(self, engine, fn, reads=(), writes=()):
        reads, writes = self._flat(reads), self._flat(writes)
        self._deps(engine, reads, writes)
        inst = fn(self.eng[engine])
        self.cnt[engine] += 1
        inst.then_inc(self.sem[engine], 1)
        tok = ("c", engine, self.cnt[engine])
        self._record(tok, reads, writes)
        self.ninst += 1
        return tok

    def dma(self, queue, out, in_, reads=(), writes=(), **kw):
        reads, writes = self._flat(reads), self._flat(writes)
        self._deps(queue, reads, writes)
        i = self.dnext
        self.dnext = (self.dnext + 1) % len(self.dsem)
        if self.dcnt[i] > 0:
            self._need(queue, ("d", i, self.dcnt[i]))
        self.dcnt[i] += 16
        self.eng[queue].dma_start(out=out, in_=in_, **kw).then_inc(self.dsem[i], 16)
        tok = ("d", i, self.dcnt[i])
        self._record(tok, reads, writes)
        self.ninst += 1
        return tok

    def barrier(self):
        for e in ("pe", "dve", "act", "pool", "sp"):
            self.finish(e)

    def finish(self, engine="sp"):
        for i, c in enumerate(self.dcnt):
            if c:
                self._need(engine, ("d", i, c))
        for k, c in self.cnt.items():
            if c:
                self._need(engine, ("c", k, c))


def fap(ap, dims, off=0):
    return AP(ap.tensor, ap.offset + off, [list(ap.ap[0])] + [list(d) for d in dims])


def make_consts():
    c = {}
    n = np.arange(256)[:, None]
    s = np.arange(64)[None, :]
    ov = ((16 * n < 64 * s + 64) & (16 * n + 32 > 64 * s) & (n < 255)).astype(np.float32)
    c["c_overlap"] = ov
    cc = np.arange(32)[:, None, None, None]
    nl = np.arange(128)[None, :, None, None]
    kch = np.arange(2)[None, None, :, None]
    tl = np.arange(128)[None, None, None, :]
    nn = kch * 128 + nl
    vis = (16 * nn + 31 <= 128 * cc + tl) & (nn < 255)
    c["c_cmpbias"] = np.where(vis, 1.0, 0.0).astype(np.float32)
    cc = np.arange(32)[:, None, None]
    tl = np.arange(128)[None, :, None]
    blk = np.arange(64)[None, None, :]
    t = 128 * cc + tl
    cur = t // 64
    F = np.zeros((32, 128, 64), np.float32)
    F = np.where((blk == 0) | (blk == cur) | (blk == cur - 1), 1e9, F)
    F = np.where(blk > cur, -1e30, F)
    c["c_F"] = F.astype(np.float32)
    key = np.arange(4096)[None, :]
    c["c_E"] = (key // 64 == np.arange(64)[:, None]).astype(np.float32)
    kl = np.arange(128)[:, None]
    tl = np.arange(128)[None, :]
    c["c_mask2"] = np.stack([np.where(kl <= tl, 1.0, 0.0), np.where(kl > tl, 1.0, 0.0)]).astype(np.float32)
    A = np.zeros((4, 3, 128, 128), np.float32)
    tp = np.arange(128)[:, None]
    tt = np.arange(128)[None, :]
    for wi, w in enumerate((2, 4, 8, 16)):
        A[wi, 0] = np.where((tp <= tt) & (tp >= tt - w + 1), 1.0 / w, 0.0) - (tp == tt)
        A[wi, 1] = np.where(tp - 128 >= tt - w + 1, 1.0 / w, 0.0)
        cnt = np.minimum(w, tt + 1)
        A[wi, 2] = np.where((tp <= tt) & (tp >= tt - w + 1), 1.0 / cnt, 0.0) - (tp == tt)
    c["c_poolA"] = A
    c["c_iota128"] = np.tile(np.arange(128, dtype=np.float32)[None, :], (128, 1))
    c["c_iota16"] = np.tile(np.arange(16, dtype=np.float32)[None, :], (128, 1))
    c["c_ident"] = np.eye(128, dtype=np.float32)
    return c


CONST_SHAPES = {"c_overlap": [256, 64], "c_cmpbias": [32, 128, 2, 128], "c_F": [32, 128, 64], "c_E": [64, 4096],
                "c_mask2": [2, 128, 128], "c_poolA": [4, 3, 128, 128], "c_iota128": [128, 128],
                "c_iota16": [128, 16], "c_ident": [128, 128]}

IN_SHAPES = {
    "x": [T, D], "p": [2, T, 256], "mix_norm": [2, D], "ab_w_in": [D, INC], "ab_conv_w": [124, 128],
    "ab_conv_b": [4, 128], "ab_conv_ln_g": [4, 128], "ab_conv_ln_b": [4, 128], "ab_cmp_pos": [2, 32, 64],
    "ab_cmp_w1": [2, 2048, 256], "ab_cmp_w2": [2, 256, 64], "ab_w_out": [D, D], "pool_w": [4, 256, 256],
    "pool_scale": [D], "ffn_norm": [2, D], "peer_wq": [2, D, D], "peer_subkeys": [2, 8, 2, 128, 64],
    "uP": [2, D, 16384], "vP": [2, 16384, D], "ple_norm": [2, D], "ple_gate_w": [2, D, D],
    "ple_proj": [2, 256, D], "final_norm": [D],
}


def build_program(debug_stop=None):
    nc = bass.Bass("TRN2", target_bir_lowering=False)
    I = {k: nc.dram_tensor(k, v, F32, kind="ExternalInput").ap() for k, v in IN_SHAPES.items()}
    C = {k: nc.dram_tensor(k, v, F32, kind="ExternalInput").ap() for k, v in CONST_SHAPES.items()}
    out = nc.dram_tensor("out", [T, D], F32, kind="ExternalOutput").ap()
    dbg = None
    if debug_stop is not None:
        dbg = nc.dram_tensor("dbg", [3 * T, D], F32, kind="ExternalOutput").ap()
    h_dram = nc.dram_tensor("h_dram", [T, D], F32).ap()
    aoT_dram = nc.dram_tensor("aoT_dram", [512, T], BF16).ap()
    u16 = nc.dram_tensor("u16", [2, D, 16384], BF16).ap()
    v16 = nc.dram_tensor("v16", [2, 16384, D], BF16).ap()
    wq16 = nc.dram_tensor("wq16", [2, D, D], BF16).ap()
    gw16 = nc.dram_tensor("gw16", [2, D, D], BF16).ap()

    S = Sched(nc)
    PE = lambda fn, r=(), w=(): S.op("pe", fn, r, w)
    DVE = lambda fn, r=(), w=(): S.op("dve", fn, r, w)
    ACT = lambda fn, r=(), w=(): S.op("act", fn, r, w)
    POOL = lambda fn, r=(), w=(): S.op("pool", fn, r, w)

    with ExitStack() as es0:
        def sbt(es, name, shape, dt):
            return es.enter_context(nc.sbuf_tensor(name, shape, dt))
        pb = [es0.enter_context(nc.psum_tensor("pb%d" % i, [128, 512], F32)) for i in range(8)]
        pbn = ["pb%d" % i for i in range(8)]
        pbT = [pb[i][:].bitcast(BF16) for i in range(8)]

        ident_f = sbt(es0, "ident_f", [128, 128], F32)
        ident_b = sbt(es0, "ident_b", [128, 128], BF16)
        ones_f = sbt(es0, "ones_f", [128, 128], F32)
        zeros_b = sbt(es0, "zeros_b", [128, 512], BF16)
        junk = sbt(es0, "junk", [128, 1024], BF16)
        ss = sbt(es0, "ss", [128, 8], F32)
        rs = sbt(es0, "rs", [128, 8], F32)
        S.dma("sp", ident_f[:], C["c_ident"][:, :], writes=["ident_f"])
        S.dma("pool", ident_b[:], C["c_ident"][:, :], writes=["ident_b"])
        POOL(lambda e: e.memset(ones_f[:], 1.0), w=["ones_f"])
        POOL(lambda e: e.memset(zeros_b[:], 0.0), w=["zeros_b"])

        def rmsnorm_tile(x_ap, xres, g_ap, gres, out_ap, outres, col):
            ACT(lambda e: e.activation(junk[:], x_ap, AF.Square, accum_out=ss[:, col:col + 1]), r=[xres], w=["junk", "ss%d" % col])
            DVE(lambda e: e.tensor_scalar(rs[:, col:col + 1], ss[:, col:col + 1], 1.0 / D, EPS, ALU.mult, ALU.add),
                r=["ss%d" % col], w=["rs%d" % col])
            ACT(lambda e: e.activation(rs[:, col:col + 1], rs[:, col:col + 1], AF.Sqrt), w=["rs%d" % col])
            DVE(lambda e: e.reciprocal(rs[:, col:col + 1], rs[:, col:col + 1]), w=["rs%d" % col])
            DVE(lambda e: e.scalar_tensor_tensor(out_ap, x_ap, rs[:, col:col + 1], g_ap, ALU.mult, ALU.mult),
                r=[xres, "rs%d" % col, gres], w=(outres if isinstance(outres, list) else [outres]))

        def transpose_to(hn_ap, hnres, bank, dst_ap, dstres, nk=8, eng="act"):
            for kc in range(nk):
                PE(lambda e, kc=kc: e.transpose(pbT[bank][:, kc * 128:(kc + 1) * 128], hn_ap[:, kc * 128:(kc + 1) * 128], ident_b[:]),
                   r=[hnres, "ident_b"], w=[pbn[bank]])
            src = fap(pbT[bank][:, 0:1], [[128, nk], [1, 128]])
            if eng == "act":
                ACT(lambda e: e.copy(dst_ap, src), w=[dstres, pbn[bank]])
            else:
                DVE(lambda e: e.tensor_copy(dst_ap, src), w=[dstres, pbn[bank]])

        esL0 = es0.enter_context(ExitStack())
        qT_all = sbt(esL0, "qT_all", [128, 4 * T], BF16)
        ksT = sbt(esL0, "ksT", [128, T], BF16)
        kwT = sbt(esL0, "kwT", [128, T], BF16)
        vs_aug = sbt(esL0, "vs_aug", [128, NT * 130], BF16)
        vw_aug = sbt(esL0, "vw_aug", [128, NT * 130], BF16)
        gsig = sbt(esL0, "gsig", [128, NT * 24], F32)
        kcmpT = sbt(esL0, "kcmpT", [128, 256], BF16)
        vc_aug = sbt(esL0, "vc_aug", [128, 2 * 2 * 129], BF16)
        POOL(lambda e: e.memset(vs_aug[:], 1.0), w=["vs_aug"])
        POOL(lambda e: e.memset(vw_aug[:], 1.0), w=["vw_aug"])
        POOL(lambda e: e.memset(vc_aug[:], 1.0), w=["vc_aug"])

        esAB = es0.enter_context(ExitStack())
        kcT = sbt(esAB, "kcT", [128, T], BF16)
        vcT = sbt(esAB, "vcT", [128, T], BF16)

        with ExitStack() as esA:
            w_in_sb = sbt(esA, "w_in_sb", [128, 8 * INC], BF16)
            wqr = sbt(esA, "wqr", [128, 8 * 512], BF16)
            g0 = sbt(esA, "g0", [128, D], F32)
            cw = sbt(esA, "cw", [128, 124], F32)
            cp = sbt(esA, "cp", [128, 12], F32)
            stg = sbt(esA, "stg", [128, 128], F32)
            xt = [sbt(esA, "xt%d" % i, [128, D], F32) for i in range(2)]
            hn = [sbt(esA, "hn%d" % i, [128, D], BF16) for i in range(2)]
            hnT = sbt(esA, "hnT", [128, 8 * 512], BF16)
            a_pad = sbt(esA, "a_pad", [128, 4 * 542], F32)
            y = sbt(esA, "y", [128, 4 * 512], F32)
            ysq = [sbt(esA, "ysq%d" % i, [128, 512], F32) for i in range(2)]
            sig = [sbt(esA, "sig%d" % i, [128, 512], F32) for i in range(2)]
            mean_sb = sbt(esA, "mean_sb", [128, 512], F32)
            msq = sbt(esA, "msq", [128, 512], F32)
            rstd = sbt(esA, "rstd", [128, 512], F32)
            ao = sbt(esA, "ao", [128, 4 * 512], BF16)

            for kc in range(8):
                S.dma("pool", w_in_sb[:, kc * INC:(kc + 1) * INC], I["ab_w_in"][kc * 128:(kc + 1) * 128, :], writes=["w_in_sb"])
                for g_ in range(2):
                    src = AP(I["ab_w_in"].tensor, kc * 128 * INC + 1024 + g_ * 256, [[INC, 128], [64, 4], [1, 64]])
                    dst = fap(wqr[:, kc * 512 + g_ * 64:kc * 512 + g_ * 64 + 1], [[128, 4], [1, 64]])
                    S.dma("pool", dst, src, writes=["wqr"])
            S.dma("sp", g0[:], I["mix_norm"][0, :].partition_broadcast(128), writes=["g0"])
            S.dma("sp", stg[0:124, :], I["ab_conv_w"][:, :], writes=["stg"])
            PE(lambda e: e.transpose(pb[7][:, 0:124], stg[0:124, :], ident_f[0:124, 0:124]), r=["stg", "ident_f"], w=[pbn[7]])
            DVE(lambda e: e.tensor_copy(cw[:], pb[7][:, 0:124]), w=["cw", pbn[7]])
            S.dma("sp", stg[0:4, :], I["ab_conv_b"][:, :], writes=["stg"], reads=[])
            S.dma("sp", stg[4:8, :], I["ab_conv_ln_g"][:, :], writes=["stg"])
            S.dma("sp", stg[8:12, :], I["ab_conv_ln_b"][:, :], writes=["stg"])
            PE(lambda e: e.transpose(pb[7][:, 0:12], stg[0:12, :], ident_f[0:12, 0:12]), r=["stg", "ident_f"], w=[pbn[7]])
            DVE(lambda e: e.tensor_copy(cp[:], pb[7][:, 0:12]), w=["cp", pbn[7]])
            for j in range(4):
                DVE(lambda e, j=j: e.memset(a_pad[:, j * 542:j * 542 + 30], 0.0), w=["a_pad%d" % j])

            for l in range(2):
                for r in range(8):
                    S.dma("pool", u16[l, r * 128:(r + 1) * 128, :], I["uP"][l, r * 128:(r + 1) * 128, :], writes=["u16_%d" % l])
                for r in range(8):
                    S.dma("pool", v16[l, r * 2048:(r + 1) * 2048, :], I["vP"][l, r * 2048:(r + 1) * 2048, :], writes=["v16_%d" % l])
                S.dma("pool", wq16[l, :, :], I["peer_wq"][l, :, :], writes=["wq16_%d" % l])
                S.dma("pool", gw16[l, :, :], I["ple_gate_w"][l, :, :], writes=["gw16_%d" % l])
            bank_rr = [0]

            def nb():
                b = bank_rr[0]
                bank_rr[0] = (b + 1) % 6
                return b

            for st in range(8):
                t0 = st * 512
                for j in range(4):
                    tile = st * 4 + j
                    b2 = j % 2
                    S.dma("sp", xt[b2][:], I["x"][tile * 128:(tile + 1) * 128, :], writes=["xt%d" % b2])
                    rmsnorm_tile(xt[b2][:], "xt%d" % b2, g0[:], "g0", hn[b2][:], "hn%d" % b2, b2)
                    transpose_to(hn[b2], "hn%d" % b2, 6 + b2, fap(hnT[:, j * 128:j * 128 + 1], [[512, 8], [1, 128]]), "hnT")

                def proj_T(lhs_tile, col_fn, bank):
                    for kc in range(8):
                        PE(lambda e, kc=kc: e.matmul(pb[bank][:], lhsT=col_fn(kc), rhs=hnT[:, kc * 512:(kc + 1) * 512],
                                                     start=(kc == 0), stop=(kc == 7)),
                           r=[lhs_tile, "hnT"], w=[pbn[bank]])

                for j in range(4):
                    bv, bg = nb(), nb()
                    proj_T("w_in_sb", lambda kc, j=j: w_in_sb[:, kc * INC + j * 128:kc * INC + (j + 1) * 128], bv)
                    proj_T("w_in_sb", lambda kc, j=j: w_in_sb[:, kc * INC + 512 + j * 128:kc * INC + 512 + (j + 1) * 128], bg)
                    ACT(lambda e, j=j, bg=bg: e.activation(sig[j % 2][:], pb[bg][:], AF.Sigmoid), w=["sig%d" % (j % 2), pbn[bg]])
                    DVE(lambda e, j=j, bv=bv: e.tensor_tensor(a_pad[:, j * 542 + 30:j * 542 + 542], pb[bv][:], sig[j % 2][:], ALU.mult),
                        r=["sig%d" % (j % 2)], w=["a_pad%d" % j, pbn[bv]])
                for h in range(4):
                    b = nb()
                    proj_T("wqr", lambda kc, h=h: wqr[:, kc * 512 + h * 128:kc * 512 + (h + 1) * 128], b)
                    ACT(lambda e, h=h, b=b: e.mul(qT_all[:, h * T + t0:h * T + t0 + 512], pb[b][:], 0.125), w=["qT_all", pbn[b]])
                for (tl_, tn, col0) in ((kcT, "kcT", 1536), (vcT, "vcT", 1664), (ksT, "ksT", 1792), (kwT, "kwT", 2048)):
                    b = nb()
                    proj_T("w_in_sb", lambda kc, col0=col0: w_in_sb[:, kc * INC + col0:kc * INC + col0 + 128], b)
                    ACT(lambda e, tl_=tl_, b=b: e.copy(tl_[:, t0:t0 + 512], pb[b][:]), w=[tn, pbn[b]])
                for j in range(4):
                    tile = st * 4 + j
                    b = nb()
                    for kc in range(8):
                        PE(lambda e, kc=kc, j=j, b=b: e.matmul(pb[b][:, 0:128], lhsT=hnT[:, kc * 512 + j * 128:kc * 512 + (j + 1) * 128],
                                                               rhs=w_in_sb[:, kc * INC + 1920:kc * INC + 2048], start=(kc == 0), stop=(kc == 7)),
                           r=["hnT", "w_in_sb"], w=[pbn[b]])
                    for kc in range(8):
                        PE(lambda e, kc=kc, j=j, b=b: e.matmul(pb[b][:, 128:280], lhsT=hnT[:, kc * 512 + j * 128:kc * 512 + (j + 1) * 128],
                                                               rhs=w_in_sb[:, kc * INC + 2176:kc * INC + 2328], start=(kc == 0), stop=(kc == 7)),
                           r=["hnT", "w_in_sb"], w=[pbn[b]])
                    ACT(lambda e, b=b, tile=tile: e.copy(fap(vs_aug[:, tile * 130:tile * 130 + 1], [[65, 2], [1, 64]]),
                                                         fap(pb[b][:, 0:1], [[64, 2], [1, 64]])), w=["vs_aug", pbn[b]])
                    ACT(lambda e, b=b, tile=tile: e.copy(fap(vw_aug[:, tile * 130:tile * 130 + 1], [[65, 2], [1, 64]]),
                                                         fap(pb[b][:, 128:129], [[64, 2], [1, 64]])), w=["vw_aug", pbn[b]])
                    ACT(lambda e, b=b, tile=tile: e.activation(gsig[:, tile * 24:(tile + 1) * 24], pb[b][:, 256:280], AF.Sigmoid),
                        w=["gsig", pbn[b]])
                for j in range(4):
                    engn = "dve"
                    yj = y[:, j * 512:(j + 1) * 512]
                    S.op(engn, lambda e, j=j, yj=yj: e.tensor_scalar(yj, a_pad[:, j * 542:j * 542 + 512], cw[:, j:j + 1], cp[:, j:j + 1],
                                                                     ALU.mult, ALU.add), ["a_pad%d" % j, "cw", "cp"], ["y%d" % j])
                    for k in range(1, 31):
                        S.op(engn, lambda e, j=j, k=k, yj=yj: e.scalar_tensor_tensor(yj, a_pad[:, j * 542 + k:j * 542 + k + 512],
                                                                                      cw[:, k * 4 + j:k * 4 + j + 1], yj, ALU.mult, ALU.add),
                             ["a_pad%d" % j, "cw"], ["y%d" % j])
                    S.op(engn, lambda e, j=j: e.tensor_copy(a_pad[:, j * 542:j * 542 + 30], a_pad[:, j * 542 + 512:j * 542 + 542]),
                         [], ["a_pad%d" % j])
                b1, b2_ = nb(), nb()
                for j in range(4):
                    PE(lambda e, j=j: e.matmul(pb[b1][:], lhsT=ones_f[:], rhs=y[:, j * 512:(j + 1) * 512], start=(j == 0), stop=(j == 3)),
                       r=["ones_f", "y%d" % j], w=[pbn[b1]])
                for j in range(4):
                    ACT(lambda e, j=j: e.activation(ysq[j % 2][:], y[:, j * 512:(j + 1) * 512], AF.Square), r=["y%d" % j], w=["ysq%d" % (j % 2)])
                    PE(lambda e, j=j: e.matmul(pb[b2_][:], lhsT=ones_f[:], rhs=ysq[j % 2][:], start=(j == 0), stop=(j == 3)),
                       r=["ones_f", "ysq%d" % (j % 2)], w=[pbn[b2_]])
                DVE(lambda e: e.tensor_scalar(mean_sb[:], pb[b1][:], 1.0 / 512, None, ALU.mult), w=["mean_sb", pbn[b1]])
                DVE(lambda e: e.tensor_tensor(msq[:], mean_sb[:], mean_sb[:], ALU.mult), r=["mean_sb"], w=["msq"])
                DVE(lambda e: e.scalar_tensor_tensor(rstd[:], pb[b2_][:], 1.0 / 512, msq[:], ALU.mult, ALU.subtract), r=["msq"], w=["rstd", pbn[b2_]])
                DVE(lambda e: e.tensor_scalar(rstd[:], rstd[:], EPS, None, ALU.add), r=[], w=["rstd"])
                ACT(lambda e: e.activation(rstd[:], rstd[:], AF.Sqrt), w=["rstd"])
                DVE(lambda e: e.reciprocal(rstd[:], rstd[:]), w=["rstd"])
                for j in range(4):
                    yj = y[:, j * 512:(j + 1) * 512]
                    DVE(lambda e, yj=yj: e.tensor_tensor(yj, yj, mean_sb[:], ALU.subtract), r=["mean_sb"], w=["y%d" % j])
                    DVE(lambda e, yj=yj: e.tensor_tensor(yj, yj, rstd[:], ALU.mult), r=["rstd"], w=["y%d" % j])
                    ACT(lambda e, j=j, yj=yj: e.activation(ao[:, j * 512:(j + 1) * 512], yj, AF.Silu, bias=cp[:, 8 + j:9 + j], scale=cp[:, 4 + j:5 + j]),
                        r=["y%d" % j, "cp"], w=["ao%d" % j])
                    S.dma("sp", aoT_dram[j * 128:(j + 1) * 128, t0:t0 + 512], ao[:, j * 512:(j + 1) * 512], reads=["ao%d" % j], writes=["aoT_dram"])
        S.barrier()
        if debug_stop == "A":
            S.dma("pool", AP(dbg.tensor, 0, [[4096, 512], [1, 4096]]), aoT_dram[:, :], reads=["aoT_dram"])
            S.finish("sp")
            return nc
        with ExitStack() as esB:
            w1_sb = sbt(esB, "w1_sb", [128, 32 * 256], BF16)
            w2_sb = sbt(esB, "w2_sb", [128, 128], BF16)
            w2pad = sbt(esB, "w2pad", [128, 4 * 128], BF16)
            posr = sbt(esB, "posr", [32, 128], F32)
            posT = sbt(esB, "posT", [128, 32], BF16)
            lo = sbt(esB, "lo", [128, T], BF16)
            hi = sbt(esB, "hi", [128, T], BF16)
            hid = [sbt(esB, "hid%d" % g, [128, 512], BF16) for g in range(2)]
            for kch in range(2):
                for g in range(2):
                    S.dma("pool", vc_aug[:, (kch * 2 + g) * 129 + 65:(kch * 2 + g) * 129 + 129],
                          C["c_overlap"][kch * 128:(kch + 1) * 128, :], writes=["vc_aug"])
            for src_i, (srcT, srcname) in enumerate(((kcT, "kcT"), (vcT, "vcT"))):
                for dup in range(2):
                    src_ap = AP(I["ab_cmp_w1"].tensor, src_i * 2048 * 256, [[256, 64], [64 * 256, 32], [1, 256]])
                    S.dma("pool", fap(w1_sb[dup * 64:(dup + 1) * 64, 0:1], [[256, 32], [1, 256]]), src_ap, writes=["w1_sb"])
                S.dma("pool", fap(w2_sb[:, 0:1], [[64, 2], [1, 64]]),
                      AP(I["ab_cmp_w2"].tensor, src_i * 256 * 64, [[64, 128], [128 * 64, 2], [1, 64]]), writes=["w2_sb"])
                for dup in range(2):
                    S.dma("sp", posr[:, dup * 64:(dup + 1) * 64], I["ab_cmp_pos"][src_i, :, :], writes=["posr"])
                PE(lambda e: e.transpose(pb[7][:, 0:32], posr[0:32, :], ident_f[0:32, 0:32]), r=["posr", "ident_f"], w=[pbn[7]])
                DVE(lambda e: e.tensor_copy(posT[:], pb[7][:, 0:32]), w=["posT", pbn[7]])
                DVE(lambda e, srcT=srcT: e.tensor_tensor(fap(lo[:, 0:1], [[16, 256], [1, 16]]), fap(srcT[:, 0:1], [[16, 256], [1, 16]]),
                                                         fap(posT[:, 0:1], [[0, 256], [1, 16]]), ALU.add), r=[srcname, "posT"], w=["lo"])
                DVE(lambda e, srcT=srcT: e.tensor_tensor(fap(hi[:, 0:1], [[16, 256], [1, 16]]), fap(srcT[:, 0:1], [[16, 256], [1, 16]]),
                                                         fap(posT[:, 16:17], [[0, 256], [1, 16]]), ALU.add), r=[srcname, "posT"], w=["hi"])
                for g in range(2):
                    for jc in range(2):
                        bk = g * 2 + jc
                        for l in range(32):
                            src_t = lo if l < 16 else hi
                            PE(lambda e, g=g, jc=jc, l=l, bk=bk, src_t=src_t: e.matmul(
                                pb[bk][:, 0:255], lhsT=w1_sb[g * 64:(g + 1) * 64, l * 256 + jc * 128:l * 256 + (jc + 1) * 128],
                                rhs=fap(src_t[g * 64:(g + 1) * 64, l:l + 1], [[16, 255]]), start=(l == 0), stop=(l == 31)),
                               r=["w1_sb", "lo", "hi"], w=[pbn[bk]])
                        ACT(lambda e, g=g, jc=jc, bk=bk: e.activation(hid[g][:, jc * 256:jc * 256 + 255], pb[bk][:, 0:255], AF.Gelu_apprx_tanh),
                            w=["hid%d" % g, pbn[bk]])
                if src_i == 0:
                    DVE(lambda e: e.memset(w2pad[:], 0.0), w=["w2pad"])
                    for g in range(2):
                        for jc in range(2):
                            DVE(lambda e, g=g, jc=jc: e.tensor_copy(w2pad[:, (g * 2 + jc) * 128 + g * 64:(g * 2 + jc) * 128 + g * 64 + 64],
                                                                    w2_sb[:, jc * 64:(jc + 1) * 64]), r=["w2_sb"], w=["w2pad"])
                    n_ = 0
                    for g in range(2):
                        for jc in range(2):
                            PE(lambda e, g=g, jc=jc, n_=n_: e.matmul(pb[4][:, 0:255], lhsT=w2pad[:, (g * 2 + jc) * 128:(g * 2 + jc + 1) * 128],
                                                                      rhs=hid[g][:, jc * 256:jc * 256 + 255], start=(n_ == 0), stop=(n_ == 3)),
                               r=["w2pad", "hid%d" % g], w=[pbn[4]])
                            n_ += 1
                    DVE(lambda e: e.tensor_copy(kcmpT[:, 0:255], pb[4][:, 0:255]), w=["kcmpT", pbn[4]])
                    DVE(lambda e: e.memset(kcmpT[:, 255:256], 0.0), w=["kcmpT"])
                else:
                    for g in range(2):
                        for kch in range(2):
                            nr = 128 if kch == 0 else 127
                            bk = 4 + (g * 2 + kch) % 2
                            for jc in range(2):
                                PE(lambda e, g=g, kch=kch, jc=jc, nr=nr, bk=bk: e.matmul(
                                    pb[bk][0:nr, 0:64], lhsT=hid[g][:, jc * 256 + kch * 128:jc * 256 + kch * 128 + nr],
                                    rhs=w2_sb[:, jc * 64:(jc + 1) * 64], start=(jc == 0), stop=(jc == 1)),
                                   r=["hid%d" % g, "w2_sb"], w=[pbn[bk]])
                            DVE(lambda e, g=g, kch=kch, nr=nr, bk=bk: e.tensor_copy(vc_aug[0:nr, (kch * 2 + g) * 129:(kch * 2 + g) * 129 + 64],
                                                                                   pb[bk][0:nr, 0:64]), w=["vc_aug", pbn[bk]])
        esAB.close()
        S.barrier()
        if debug_stop == "B":
            dtmp = es0.enter_context(nc.sbuf_tensor("dtmp", [128, 256 + 516], F32))
            DVE(lambda e: e.tensor_copy(dtmp[:, 0:256], kcmpT[:]), r=["kcmpT"], w=["dtmp"])
            DVE(lambda e: e.tensor_copy(dtmp[:, 256:772], vc_aug[:]), r=["vc_aug"], w=["dtmp"])
            S.dma("sp", AP(dbg.tensor, 0, [[256, 128], [1, 256]]), dtmp[:, 0:256], reads=["dtmp"])
            S.dma("sp", AP(dbg.tensor, 128 * 256, [[516, 128], [1, 516]]), dtmp[:, 256:772], reads=["dtmp"])
            S.finish("sp")
            return nc

        with ExitStack() as esC:
            E_sb = sbt(esC, "E_sb", [128, T], BF16)
            mask2 = sbt(esC, "mask2", [128, 256], BF16)
            w_out_sb = sbt(esC, "w_out_sb", [128, 8 * D], BF16)
            cmpb = [sbt(esC, "cmpb%d" % i, [128, 256], BF16) for i in range(2)]
            F_sb = [sbt(esC, "F_sb%d" % i, [128, 64], F32) for i in range(2)]
            xc = [sbt(esC, "xc%d" % i, [128, D], F32) for i in range(2)]
            aot = [sbt(esC, "aot%d" % i, [128, 512], BF16) for i in range(2)]
            pT = [sbt(esC, "pT%d" % i, [128, 512], BF16) for i in range(3)]
            rz = sbt(esC, "rz", [128, 4], F32)
            wg = sbt(esC, "wg", [128, 4], F32)
            imp = sbt(esC, "imp", [128, 64], F32)
            imp2 = sbt(esC, "imp2", [128, 64], F32)
            m8a = sbt(esC, "m8a", [128, 8], F32)
            m8b = sbt(esC, "m8b", [128, 8], F32)
            negb = sbt(esC, "negb", [128, 128], F32)
            negT = sbt(esC, "negT", [128, 128], BF16)
            oacc = sbt(esC, "oacc", [128, 512], F32)
            obf = sbt(esC, "obf", [128, 512], BF16)
            boT = sbt(esC, "boT", [128, 512], BF16)
            hnew = [sbt(esC, "hnew%d" % i, [128, D], F32) for i in range(2)]
            for dup in range(2):
                S.dma("pool", E_sb[dup * 64:(dup + 1) * 64, :], C["c_E"][:, :], writes=["E_sb"])
            for i in range(2):
                S.dma("pool", mask2[:, i * 128:(i + 1) * 128], C["c_mask2"][i, :, :], writes=["mask2"])
            for j in range(8):
                S.dma("pool", w_out_sb[:, j * D:(j + 1) * D], I["ab_w_out"][j * 128:(j + 1) * 128, :], writes=["w_out_sb"])
            sc_rr = [0]
            pt_rr = [0]

            def zero_bank(b):
                PE(lambda e: e.matmul(pb[b][:, :], lhsT=zeros_b[0:1, 0:128], rhs=zeros_b[0:1, 0:512], start=True, stop=True),
                   r=["zeros_b"], w=[pbn[b]])

            def pv(acc, pt_i, nr, rhs_fn, width, stride):
                for h in range(4):
                    PE(lambda e, h=h: e.matmul(pb[acc][:, h * stride:h * stride + width], lhsT=pT[pt_i][0:nr, h * 128:(h + 1) * 128],
                                               rhs=rhs_fn(), start=False, stop=True, skip_group_check=True),
                       r=["pT%d" % pt_i, "vs_aug", "vw_aug", "vc_aug"], w=[pbn[acc]])

            def combine(acc, c, g, br, first):
                DVE(lambda e: e.tensor_scalar(rz[:], fap(pb[acc][:, 64:65], [[65, 4]]), 1e-30, None, ALU.max), w=["rz", pbn[acc]])
                DVE(lambda e: e.reciprocal(rz[:], rz[:]), w=["rz"])
                DVE(lambda e: e.tensor_tensor(wg[:], fap(gsig[:, c * 24 + g * 12 + br:c * 24 + g * 12 + br + 1], [[3, 4]]), rz[:], ALU.mult),
                    r=["gsig", "rz"], w=["wg"])
                for h in range(4):
                    oh = oacc[:, (g * 4 + h) * 64:(g * 4 + h + 1) * 64]
                    if first:
                        DVE(lambda e, h=h, oh=oh: e.tensor_scalar(oh, pb[acc][:, h * 65:h * 65 + 64], wg[:, h:h + 1], None, ALU.mult),
                            r=["wg"], w=["oacc", pbn[acc]])
                    else:
                        DVE(lambda e, h=h, oh=oh: e.scalar_tensor_tensor(oh, pb[acc][:, h * 65:h * 65 + 64], wg[:, h:h + 1], oh, ALU.mult, ALU.add),
                            r=["wg"], w=["oacc", pbn[acc]])

            for c in range(NT):
                c2 = c % 2
                S.dma("pool", cmpb[c2][:], C["c_cmpbias"][c, :, :, :], writes=["cmpb%d" % c2])
                S.dma("sp", F_sb[c2][:], C["c_F"][c, :, :], writes=["F_sb%d" % c2])
                S.dma("sp", xc[c2][:], I["x"][c * 128:(c + 1) * 128, :], writes=["xc%d" % c2])
                S.dma("sp", fap(aot[c2][:, 0:1], [[128, 4], [1, 128]]), AP(aoT_dram.tensor, c * 128, [[T, 128], [128 * T, 4], [1, 128]]),
                      reads=["aoT_dram"], writes=["aot%d" % c2])
                for g in range(2):
                    gp0, gp1 = g * 64, (g + 1) * 64
                    q_rhs = fap(qT_all[gp0:gp1, c * 128:c * 128 + 1], [[T, 4], [1, 128]])
                    zero_bank(2)
                    zero_bank(3)
                    nch = 1 if c < 16 else 2
                    for kch in range(nch):
                        nr = 128 if kch == 0 else 127
                        bs = sc_rr[0]; sc_rr[0] = 1 - bs
                        pi = pt_rr[0]; pt_rr[0] = (pi + 1) % 3
                        PE(lambda e, kch=kch, nr=nr, bs=bs: e.matmul(pb[bs][0:nr, :], lhsT=kcmpT[gp0:gp1, kch * 128:kch * 128 + nr], rhs=q_rhs,
                                                                      start=True, stop=True), r=["kcmpT", "qT_all"], w=[pbn[bs]])
                        ACT(lambda e, nr=nr, bs=bs, pi=pi: e.activation(pT[pi][0:nr, :], pb[bs][0:nr, :], AF.Exp), w=["pT%d" % pi, pbn[bs]])
                        POOL(lambda e, nr=nr, pi=pi, kch=kch: e.tensor_tensor(fap(pT[pi][0:nr, 0:1], [[128, 4], [1, 128]]),
                                                                              fap(pT[pi][0:nr, 0:1], [[128, 4], [1, 128]]),
                                                                              fap(cmpb[c2][0:nr, kch * 128:kch * 128 + 1], [[0, 4], [1, 128]]), ALU.mult),
                             r=["cmpb%d" % c2], w=["pT%d" % pi])
                        pv(2, pi, nr, lambda kch=kch: vc_aug[0:nr, (kch * 2 + g) * 129:(kch * 2 + g) * 129 + 65], 65, 65)
                        pv(3, pi, nr, lambda kch=kch: vc_aug[0:nr, (kch * 2 + g) * 129 + 65:(kch * 2 + g) * 129 + 129], 64, 64)
                    combine(2, c, g, 0, True)
                    for h in range(4):
                        DVE(lambda e, h=h: e.scalar_tensor_tensor(imp[:], pb[3][:, h * 64:(h + 1) * 64], rz[:, h:h + 1],
                                                                  (F_sb[c2][:] if h == 0 else imp[:]), ALU.mult, ALU.add),
                            r=["rz", "F_sb%d" % c2], w=["imp", pbn[3]])
                    DVE(lambda e: e.max(out=m8a[:], in_=imp[:]), r=["imp"], w=["m8a"])
                    DVE(lambda e: e.match_replace(out=imp2[:], in_to_replace=m8a[:], in_values=imp[:], imm_value=-3.0e38), r=["imp", "m8a"], w=["imp2"])
                    DVE(lambda e: e.max(out=m8b[:], in_=imp2[:]), r=["imp2"], w=["m8b"])
                    for dup in range(2):
                        DVE(lambda e, dup=dup: e.tensor_scalar(negb[:, dup * 64:(dup + 1) * 64], imp[:], m8b[:, 7:8], NEG, ALU.is_lt, ALU.mult),
                            r=["imp", "m8b"], w=["negb"])
                    zero_bank(5)
                    for kt in range(max(0, c - 4), c + 1):
                        bs = sc_rr[0]; sc_rr[0] = 1 - bs
                        pi = pt_rr[0]; pt_rr[0] = (pi + 1) % 3
                        PE(lambda e, kt=kt, bs=bs: e.matmul(pb[bs][:, :], lhsT=kwT[gp0:gp1, kt * 128:(kt + 1) * 128], rhs=q_rhs, start=True, stop=True),
                           r=["kwT", "qT_all"], w=[pbn[bs]])
                        ACT(lambda e, bs=bs, pi=pi: e.activation(pT[pi][:], pb[bs][:], AF.Exp), w=["pT%d" % pi, pbn[bs]])
                        if kt == c or kt == c - 4:
                            mo = 0 if kt == c else 128
                            POOL(lambda e, pi=pi, mo=mo: e.tensor_tensor(fap(pT[pi][:, 0:1], [[128, 4], [1, 128]]), fap(pT[pi][:, 0:1], [[128, 4], [1, 128]]),
                                                                         fap(mask2[:, mo:mo + 1], [[0, 4], [1, 128]]), ALU.mult), r=["mask2"], w=["pT%d" % pi])
                        pv(5, pi, 128, lambda kt=kt: vw_aug[:, kt * 130 + g * 65:kt * 130 + g * 65 + 65], 65, 65)
                    PE(lambda e: e.transpose(pb[6][:, 0:128], negb[:], ident_f[:]), r=["negb", "ident_f"], w=[pbn[6]])
                    DVE(lambda e: e.tensor_copy(negT[:], pb[6][:, 0:128]), w=["negT", pbn[6]])
                    zero_bank(4)
                    for kt in range(c + 1):
                        bs = sc_rr[0]; sc_rr[0] = 1 - bs
                        pi = pt_rr[0]; pt_rr[0] = (pi + 1) % 3
                        PE(lambda e, kt=kt, bs=bs: e.matmul(pb[bs][:, :], lhsT=ksT[gp0:gp1, kt * 128:(kt + 1) * 128], rhs=q_rhs, start=True, stop=False),
                           r=["ksT", "qT_all"], w=[pbn[bs]])
                        PE(lambda e, kt=kt, bs=bs: e.matmul(pb[bs][:, :], lhsT=E_sb[gp0:gp1, kt * 128:(kt + 1) * 128],
                                                            rhs=fap(negT[gp0:gp1, 0:1], [[0, 4], [1, 128]]), start=False, stop=True),
                           r=["E_sb", "negT"], w=[pbn[bs]])
                        ACT(lambda e, bs=bs, pi=pi: e.activation(pT[pi][:], pb[bs][:], AF.Exp), w=["pT%d" % pi, pbn[bs]])
                        if kt == c:
                            POOL(lambda e, pi=pi: e.tensor_tensor(fap(pT[pi][:, 0:1], [[128, 4], [1, 128]]), fap(pT[pi][:, 0:1], [[128, 4], [1, 128]]),
                                                                  fap(mask2[:, 0:1], [[0, 4], [1, 128]]), ALU.mult), r=["mask2"], w=["pT%d" % pi])
                        pv(4, pi, 128, lambda kt=kt: vs_aug[:, kt * 130 + g * 65:kt * 130 + g * 65 + 65], 65, 65)
                    combine(4, c, g, 1, False)
                    combine(5, c, g, 2, False)
                if debug_stop == "Cb":
                    S.dma("sp", dbg[c * 128:(c + 1) * 128, 0:512], oacc[:], reads=["oacc"])
                DVE(lambda e: e.tensor_copy(obf[:], oacc[:]), r=["oacc"], w=["obf"])
                transpose_to(obf, "obf", 6, fap(boT[:, 0:1], [[128, 4], [1, 128]]), "boT", nk=4, eng="dve")
                for dh in range(2):
                    bk = 6 + dh
                    for j in range(8):
                        lhs = aot[c2][:, j * 128:(j + 1) * 128] if j < 4 else boT[:, (j - 4) * 128:(j - 3) * 128]
                        PE(lambda e, j=j, dh=dh, bk=bk, lhs=lhs: e.matmul(pb[bk][:, :], lhsT=lhs, rhs=w_out_sb[:, j * D + dh * 512:j * D + (dh + 1) * 512],
                                                                          start=(j == 0), stop=(j == 7)),
                           r=["aot%d" % c2, "boT", "w_out_sb"], w=[pbn[bk]])
                    DVE(lambda e, dh=dh, bk=bk: e.tensor_tensor(hnew[c2][:, dh * 512:(dh + 1) * 512], xc[c2][:, dh * 512:(dh + 1) * 512], pb[bk][:, :], ALU.add),
                        r=["xc%d" % c2], w=["hnew%d" % c2, pbn[bk]])
                S.dma("sp", h_dram[c * 128:(c + 1) * 128, :], hnew[c2][:], reads=["hnew%d" % c2], writes=["h_dram"])
        esL0.close()
        S.barrier()
        if debug_stop in ("C", "E0", "P0"):
            S.dma("sp", dbg[0:T, :], h_dram[:, :], reads=["h_dram"])
        if debug_stop in ("C", "Cb"):
            S.finish("sp")
            return nc
        NG = NG_
        SCRC = ["scrC%d" % k for k in range(16)]
        SCRA = ["scrA_h0", "scrA_h1"]
        SCRB = ["scrB_h0", "scrB_h1"]
        TOPN = ["top%d" % k for k in range(16)]
        ITOPN = ["itop%d" % k for k in range(16)]
        CTOPN = ["ctop%d" % k for k in range(8)]
        CIDXN = ["cidx%d" % k for k in range(8)]
        NCG = 128 // NG

        def peer_phase(layer, with_pool, final):
            with ExitStack() as esE:
                sbt = lambda es, name, shape, dt: es.enter_context(nc.sbuf_tensor("%s_L%d" % (name, layer), shape, dt))
                gffn = sbt(esE, "gffn", [128, D], F32)
                gple = sbt(esE, "gple", [128, D], F32)
                gx = sbt(esE, "gx", [128, D], F32)
                gfin = sbt(esE, "gfin", [128, D], F32)
                skT = sbt(esE, "skT", [128, 8 * 128], BF16)
                sk_st = sbt(esE, "sk_st", [128, 128], F32)
                pproj = sbt(esE, "pproj", [128, 2 * D], BF16)
                iota128 = sbt(esE, "iota128", [128, 128], F32)
                iota16 = sbt(esE, "iota16", [128, 16], F32)
                wbuf = sbt(esE, "wbuf", [128, 8 * 512], BF16)
                G_sb = sbt(esE, "G_sb", [128, 256 * 128], BF16)
                NSL = 3
                ubuf = [sbt(esE, "ubuf%d" % i, [128, 8 * NG * 128], BF16) for i in range(NSL)]
                vbuf = [sbt(esE, "vbuf%d" % i, [128, NG * D], BF16) for i in range(NSL)]
                ht2 = [sbt(esE, "ht_%d" % i, [128, 2 * D], F32) for i in range(2)]
                hnb = sbt(esE, "hnb", [128, 2 * D], BF16)
                hnT2 = [sbt(esE, "hnTe%d" % i, [128, 8 * 256], BF16) for i in range(2)]
                scrA = sbt(esE, "scrA", [128, 2048], F32)
                scrB = sbt(esE, "scrB", [128, 2048], F32)
                scrC = sbt(esE, "scrC", [128, 2048], F32)
                A16, B16, C16 = scrA[:].bitcast(BF16), scrB[:].bitcast(BF16), scrC[:].bitcast(BF16)
                qTs = sbt(esE, "qTs", [128, 8 * 256], BF16)
                rT = sbt(esE, "rT", [128, 3 * 256], F32)
                iota128b = sbt(esE, "iota128b", [128, 128], BF16)
                top = sbt(esE, "top", [128, 256], F32)
                itop = sbt(esE, "itop", [128, 256], U32)
                itopf = sbt(esE, "itopf", [128, 256], F32)
                ctop = sbt(esE, "ctop", [128, 128], F32)
                cidx = sbt(esE, "cidx", [128, 128], U32)
                ai = sbt(esE, "ai", [128, 128], U32)
                af_ = sbt(esE, "af_", [128, 128], F32)
                bf_ = sbt(esE, "bf_", [128, 128], F32)
                gate = sbt(esE, "gate", [128, 128], F32)
                i1f = sbt(esE, "i1f", [128, 128], F32)
                i2f = sbt(esE, "i2f", [128, 128], F32)
                zt = sbt(esE, "zt", [128, 8], F32)
                a_sb = [sbt(esE, "a_sb%d" % i, [128, 256], BF16) for i in range(2)]
                W_sb = [sbt(esE, "W_sb%d" % i, [128, 256], BF16) for i in range(2)]
                ptb = sbt(esE, "ptb", [128, 512], BF16)
                pTs = sbt(esE, "pTs", [128, 512], BF16)
                if with_pool:
                    poolA = sbt(esE, "poolA", [128, 12 * 128], BF16)
                    pw_sb = sbt(esE, "pw_sb", [128, 2048], BF16)
                    hprev = sbt(esE, "hprev", [128, D], BF16)
                    dT_sb = sbt(esE, "dT_sb", [128, 8 * 128], BF16)

                S.dma("sp", gffn[:], I["ffn_norm"][layer, :].partition_broadcast(128), writes=["gffn"])
                S.dma("sp", gple[:], I["ple_norm"][layer, :].partition_broadcast(128), writes=["gple"])
                S.dma("sp", gfin[:], I["final_norm"].partition_broadcast(128), writes=["gfin"])
                S.dma("sp", iota128[:], C["c_iota128"][:, :], writes=["iota128"])
                S.dma("pool", iota128b[:], C["c_iota128"][:, :], writes=["iota128b"])
                S.dma("sp", iota16[:], C["c_iota16"][:, :], writes=["iota16"])
                S.dma("pool", fap(pproj[:, 0:1], [[D, 2], [1, D]]), AP(I["ple_proj"].tensor, layer * 256 * D, [[D, 128], [128 * D, 2], [1, D]]),
                      writes=["pproj"])
                for h in range(8):
                    S.dma("sp", fap(sk_st[:, 0:1], [[64, 2], [1, 64]]),
                          AP(I["peer_subkeys"].tensor, (layer * 8 + h) * 2 * 128 * 64, [[64, 128], [128 * 64, 2], [1, 64]]), writes=["sk_st"])
                    PE(lambda e: e.transpose(pb[7][:, 0:128], sk_st[:], ident_f[:]), r=["sk_st", "ident_f"], w=[pbn[7]])
                    DVE(lambda e, h=h: e.tensor_copy(skT[:, h * 128:(h + 1) * 128], pb[7][:, 0:128]), w=["skT", pbn[7]])
                if with_pool:
                    S.dma("sp", gx[:], I["mix_norm"][1, :].partition_broadcast(128), writes=["gx"])
                    for wi in range(4):
                        for kind in range(3):
                            S.dma("pool", poolA[:, (wi * 3 + kind) * 128:(wi * 3 + kind + 1) * 128], C["c_poolA"][wi, kind, :, :], writes=["poolA"])
                    S.dma("sp", scrB[:, 0:D], I["pool_scale"].partition_broadcast(128), writes=[SCRB])
                    for gi in range(4):
                        S.dma("sp", fap(scrA[:, gi * 512:gi * 512 + 1], [[256, 2], [1, 256]]),
                              AP(I["pool_w"].tensor, gi * 256 * 256, [[256, 128], [128 * 256, 2], [1, 256]]), writes=[SCRA])
                    DVE(lambda e: e.tensor_tensor(fap(pw_sb[:, 0:1], [[512, 4], [256, 2], [1, 256]]), fap(scrA[:, 0:1], [[512, 4], [256, 2], [1, 256]]),
                                                  fap(scrB[:, 0:1], [[256, 4], [0, 2], [1, 256]]), ALU.mult), r=[SCRA, SCRB], w=["pw_sb"])

                seq = [(blk, cg) for blk in range(16) for cg in range(NCG)]

                def issue_uv(idx):
                    if idx >= len(seq):
                        return
                    blk_, cg_ = seq[idx]
                    sl = idx % NSL
                    S.dma("sp", ubuf[sl][:, :],
                          AP(u16.tensor, layer * D * 16384 + cg_ * 128 * 8 * NG * 128, [[8 * NG * 128, 128], [1, 8 * NG * 128]]),
                          reads=["u16_%d" % layer], writes=["ubuf%d" % sl])
                    S.dma("sp", vbuf[sl][:, :],
                          AP(v16.tensor, layer * 16384 * D + cg_ * 128 * NG * D, [[NG * D, 128], [1, NG * D]]),
                          reads=["v16_%d" % layer], writes=["vbuf%d" % sl])

                for i_ in range(NSL):
                    issue_uv(i_)

                def HT(blk, tt):
                    par = blk % 2
                    return ht2[par][:, tt * D:(tt + 1) * D], "ht%d_%d" % (par, tt)

                def front(blk):
                    tok0 = blk * 256
                    par = blk % 2
                    hnT = hnT2[par]
                    hnTn = "hnTe%d" % par
                    for tt in range(2):
                        hv, hn_ = HT(blk, tt)
                        S.dma("sp", hv, h_dram[tok0 + tt * 128:tok0 + (tt + 1) * 128, :], reads=["h_dram"], writes=[hn_])
                    yield
                    if with_pool:
                        for tt in range(2):
                            hv, hn_ = HT(blk, tt)
                            rmsnorm_tile(hv, hn_, gx[:], "gx", hnb[:, tt * D:(tt + 1) * D], "hnb%d" % tt, tt)
                            yield
                        for tt in range(2):
                            hv, hn_ = HT(blk, tt)
                            gt = blk * 2 + tt
                            cur = hnb[:, tt * D:(tt + 1) * D]
                            prv = hprev[:] if tt == 0 else hnb[:, 0:D]
                            prvn = "hprev" if tt == 0 else "hnb0"
                            for k in range(8):
                                wi = k // 2
                                bk = 6 + k // 4
                                kind = 2 if gt == 0 else 0
                                PE(lambda e, k=k, wi=wi, bk=bk, kind=kind, cur=cur: e.matmul(
                                    pb[bk][:, (k % 4) * 128:(k % 4 + 1) * 128], lhsT=cur[:, k * 128:(k + 1) * 128],
                                    rhs=poolA[:, (wi * 3 + kind) * 128:(wi * 3 + kind + 1) * 128], start=True, stop=(gt == 0)),
                                   r=["hnb%d" % tt, "poolA"], w=[pbn[bk]])
                                if gt > 0:
                                    PE(lambda e, k=k, wi=wi, bk=bk, prv=prv: e.matmul(
                                        pb[bk][:, (k % 4) * 128:(k % 4 + 1) * 128], lhsT=prv[:, k * 128:(k + 1) * 128],
                                        rhs=poolA[:, (wi * 3 + 1) * 128:(wi * 3 + 2) * 128], start=False, stop=True),
                                       r=[prvn, "poolA"], w=[pbn[bk]])
                                if k % 4 == 3:
                                    ACT(lambda e, bk=bk: e.copy(dT_sb[:, (bk - 6) * 512:(bk - 5) * 512], pb[bk][:, :]), w=["dT_sb", pbn[bk]])
                                    yield
                            for gi in range(4):
                                bk = 6 + gi // 2
                                for k2 in range(2):
                                    PE(lambda e, gi=gi, k2=k2, bk=bk: e.matmul(pb[bk][:, (gi % 2) * 256:(gi % 2 + 1) * 256],
                                                                              lhsT=dT_sb[:, (2 * gi + k2) * 128:(2 * gi + k2 + 1) * 128],
                                                                              rhs=pw_sb[:, (gi * 2 + k2) * 256:(gi * 2 + k2 + 1) * 256],
                                                                              start=(k2 == 0), stop=(k2 == 1)),
                                       r=["dT_sb", "pw_sb"], w=[pbn[bk]])
                            yield
                            for dh in range(2):
                                hvv = hv[:, dh * 512:(dh + 1) * 512]
                                DVE(lambda e, dh=dh, hvv=hvv: e.tensor_tensor(hvv, hvv, pb[6 + dh][:, :], ALU.add), w=[hn_, pbn[6 + dh]])
                            yield
                        DVE(lambda e: e.tensor_copy(hprev[:], hnb[:, D:2 * D]), r=["hnb1"], w=["hprev"])
                        yield
                    for tt in range(2):
                        hv, hn_ = HT(blk, tt)
                        rmsnorm_tile(hv, hn_, gffn[:], "gffn", hnb[:, tt * D:(tt + 1) * D], "hnb%d" % tt, tt)
                        yield
                        transpose_to(hnb[:, tt * D:(tt + 1) * D], "hnb%d" % tt, 6 + tt, fap(hnT[:, tt * 128:tt * 128 + 1], [[256, 8], [1, 128]]), hnTn)
                        yield
                    for h in range(8):
                        bk = 6 + h % 2
                        if h % 4 == 0:
                            S.dma("sp", fap(wbuf[:, 0:1], [[512, 8], [1, 512]]),
                                  AP(wq16.tensor, layer * D * D + (h // 4) * 512, [[D, 128], [128 * D, 8], [1, 512]]),
                                  reads=["wq16_%d" % layer], writes=["wbuf"])
                            yield
                        for kc in range(8):
                            PE(lambda e, h=h, kc=kc, bk=bk: e.matmul(pb[bk][:, 0:256], lhsT=wbuf[:, kc * 512 + (h % 4) * 128:kc * 512 + (h % 4 + 1) * 128],
                                                                     rhs=hnT[:, kc * 256:(kc + 1) * 256], start=(kc == 0), stop=(kc == 7)),
                               r=["wbuf", hnTn], w=[pbn[bk]])
                        ACT(lambda e, h=h, bk=bk: e.copy(qTs[:, h * 256:(h + 1) * 256], pb[bk][:, 0:256]), w=["qTs", pbn[bk]])
                        yield
                    for tt in range(2):
                        for r_ in range(2):
                            for hl in range(4):
                                h = r_ * 4 + hl
                                for s_ in range(2):
                                    PE(lambda e, h=h, hl=hl, s_=s_: e.matmul(pb[6 + s_][:, hl * 128:(hl + 1) * 128],
                                                                             lhsT=qTs[s_ * 64:(s_ + 1) * 64, h * 256 + tt * 128:h * 256 + (tt + 1) * 128],
                                                                             rhs=skT[s_ * 64:(s_ + 1) * 64, h * 128:(h + 1) * 128], start=True, stop=True),
                                       r=["qTs", "skT"], w=[pbn[6 + s_]])
                            for s_ in range(2):
                                ACT(lambda e, s_=s_, r_=r_: e.copy(fap(scrA[:, r_ * 1024 + s_ * 128:r_ * 1024 + s_ * 128 + 1], [[256, 4], [1, 128]]),
                                                                   fap(pb[6 + s_][:, 0:1], [[128, 4], [1, 128]])), w=[SCRA, pbn[6 + s_]])
                            yield
                        def ch(hs):
                            return (scrA[:, hs * 128:(hs + 1) * 128], top[:, hs * 16:hs * 16 + 8], top[:, hs * 16 + 8:hs * 16 + 16],
                                    itop[:, hs * 16:hs * 16 + 8], itop[:, hs * 16 + 8:hs * 16 + 16], scrC[:, hs * 128:(hs + 1) * 128])
                        for hs in range(16):
                            sv, t8a, t8b, i8a, i8b, sc = ch(hs)
                            DVE(lambda e: e.max(out=t8a, in_=sv), r=[SCRA], w=["top%d" % hs])
                            if hs % 4 == 3:
                                yield
                        for hs in range(16):
                            sv, t8a, t8b, i8a, i8b, sc = ch(hs)
                            DVE(lambda e: e.max_index(out=i8a, in_max=t8a, in_values=sv), r=[SCRA, "top%d" % hs], w=["itop%d" % hs])
                            if hs % 4 == 3:
                                yield
                        for hs in range(16):
                            sv, t8a, t8b, i8a, i8b, sc = ch(hs)
                            DVE(lambda e: e.match_replace(out=sc, in_to_replace=t8a, in_values=sv, imm_value=-3.0e38),
                                r=[SCRA, "top%d" % hs], w=["scrC%d" % hs])
                            if hs % 4 == 3:
                                yield
                        for hs in range(16):
                            sv, t8a, t8b, i8a, i8b, sc = ch(hs)
                            DVE(lambda e: e.max(out=t8b, in_=sc), r=["scrC%d" % hs], w=["top%d" % hs])
                            if hs % 4 == 3:
                                yield
                        for hs in range(16):
                            sv, t8a, t8b, i8a, i8b, sc = ch(hs)
                            DVE(lambda e: e.max_index(out=i8b, in_max=t8b, in_values=sc), r=["scrC%d" % hs, "top%d" % hs], w=["itop%d" % hs])
                            if hs % 4 == 3:
                                yield
                        DVE(lambda e: e.tensor_copy(itopf[:], itop[:]), r=ITOPN, w=["itopf"])
                        DVE(lambda e: e.tensor_tensor(fap(scrB[:, 0:1], [[256, 8], [16, 16], [1, 16]]), fap(top[:, 0:1], [[32, 8], [1, 16], [0, 16]]),
                                                      fap(top[:, 16:17], [[32, 8], [0, 16], [1, 16]]), ALU.add), r=TOPN, w=[SCRB])
                        yield

                        def cch(h):
                            return (scrB[:, h * 256:(h + 1) * 256], ctop[:, h * 16:h * 16 + 8], ctop[:, h * 16 + 8:h * 16 + 16],
                                    cidx[:, h * 16:h * 16 + 8], cidx[:, h * 16 + 8:h * 16 + 16], scrC[:, h * 256:(h + 1) * 256],
                                    ["scrC%d" % (2 * h), "scrC%d" % (2 * h + 1)])
                        for h in range(8):
                            cv, c8a, c8b, j8a, j8b, sc, scn = cch(h)
                            DVE(lambda e: e.max(out=c8a, in_=cv), r=[SCRB], w=["ctop%d" % h])
                            if h % 4 == 3:
                                yield
                        for h in range(8):
                            cv, c8a, c8b, j8a, j8b, sc, scn = cch(h)
                            DVE(lambda e: e.max_index(out=j8a, in_max=c8a, in_values=cv), r=[SCRB, "ctop%d" % h], w=["cidx%d" % h])
                            if h % 4 == 3:
                                yield
                        for h in range(8):
                            cv, c8a, c8b, j8a, j8b, sc, scn = cch(h)
                            DVE(lambda e: e.match_replace(out=sc, in_to_replace=c8a, in_values=cv, imm_value=-3.0e38),
                                r=[SCRB, "ctop%d" % h], w=scn)
                            if h % 4 == 3:
                                yield
                        for h in range(8):
                            cv, c8a, c8b, j8a, j8b, sc, scn = cch(h)
                            DVE(lambda e: e.max(out=c8b, in_=sc), r=scn, w=["ctop%d" % h])
                            if h % 4 == 3:
                                yield
                        for h in range(8):
                            cv, c8a, c8b, j8a, j8b, sc, scn = cch(h)
                            DVE(lambda e: e.max_index(out=j8b, in_max=c8b, in_values=sc), r=scn + ["ctop%d" % h], w=["cidx%d" % h])
                            if h % 4 == 3:
                                yield
                        DVE(lambda e: e.tensor_tensor(fap(gate[:, 0:1], [[16, 8], [1, 16]]), fap(ctop[:, 0:1], [[16, 8], [1, 16]]),
                                                      fap(ctop[:, 0:1], [[16, 8], [0, 16]]), ALU.subtract), r=CTOPN, w=["gate"])
                        yield
                        ACT(lambda e: e.activation(gate[:], gate[:], AF.Exp), w=["gate"])
                        yield
                        DVE(lambda e: e.tensor_reduce(zt[:, 0:8], fap(gate[:, 0:1], [[16, 8], [1, 16]]), AX.X, ALU.add), r=["gate"], w=["zt"])
                        DVE(lambda e: e.reciprocal(zt[:, 0:8], zt[:, 0:8]), w=["zt"])
                        DVE(lambda e: e.tensor_tensor(fap(gate[:, 0:1], [[16, 8], [1, 16]]), fap(gate[:, 0:1], [[16, 8], [1, 16]]),
                                                      fap(zt[:, 0:1], [[1, 8], [0, 16]]), ALU.mult), r=["zt"], w=["gate"])
                        yield
                        DVE(lambda e: e.tensor_single_scalar(ai[:], cidx[:], 4, ALU.logical_shift_right), r=CIDXN, w=["ai"])
                        DVE(lambda e: e.tensor_copy(af_[:], ai[:]), r=["ai"], w=["af_"])
                        DVE(lambda e: e.tensor_single_scalar(ai[:], cidx[:], 15, ALU.bitwise_and), r=CIDXN, w=["ai"])
                        DVE(lambda e: e.tensor_copy(bf_[:], ai[:]), r=["ai"], w=["bf_"])
                        yield
                        for (xf, off, dst, dn) in ((af_, 0, i1f, "i1f"), (bf_, 16, i2f, "i2f")):
                            DVE(lambda e, xf=xf: e.tensor_tensor(fap(scrC[:, 0:1], [[16, 128], [1, 16]]), fap(xf[:, 0:1], [[1, 128], [0, 16]]),
                                                                 fap(iota16[:, 0:1], [[0, 128], [1, 16]]), ALU.is_equal),
                                r=["af_", "bf_", "iota16"], w=SCRC)
                            yield
                            DVE(lambda e, off=off: e.tensor_tensor(fap(scrC[:, 0:1], [[256, 8], [16, 16], [1, 16]]), fap(scrC[:, 0:1], [[256, 8], [16, 16], [1, 16]]),
                                                                   fap(itopf[:, off:off + 1], [[32, 8], [0, 16], [1, 16]]), ALU.mult),
                                r=["itopf"], w=SCRC)
                            yield
                            DVE(lambda e, dst=dst: e.tensor_reduce(dst[:], fap(scrC[:, 0:1], [[16, 128], [1, 16]]), AX.X, ALU.add), r=SCRC, w=[dn])
                            yield
                        for k3, (src_, sn) in enumerate(((i1f, "i1f"), (i2f, "i2f"), (gate, "gate"))):
                            PE(lambda e, src_=src_: e.transpose(pb[7][:, 0:128], src_[:], ident_f[:]), r=[sn, "ident_f"], w=[pbn[7]])
                            DVE(lambda e, k3=k3: e.tensor_copy(rT[:, k3 * 256 + tt * 128:k3 * 256 + (tt + 1) * 128], pb[7][:, 0:128]), w=["rT", pbn[7]])
                            yield

                def gbuild(blk):
                    for tt in range(2):
                        for sb in range(8):
                            hb = sb % 2
                            tl0 = tt * 128 + sb * 16
                            o16 = hb * 2048
                            vv = [[128, 16], [1, 128]]
                            for t_ in range(16):
                                DVE(lambda e, t_=t_: e.tensor_scalar(B16[:, o16 + t_ * 128:o16 + (t_ + 1) * 128], iota128b[:],
                                                                     rT[:, tl0 + t_:tl0 + t_ + 1], rT[:, 512 + tl0 + t_:512 + tl0 + t_ + 1],
                                                                     ALU.is_equal, ALU.mult), r=["iota128b", "rT"], w=[SCRB[hb]])
                            DVE(lambda e: e.tensor_tensor(fap(C16[:, o16:o16 + 1], vv), fap(iota128[:, 0:1], [[0, 16], [1, 128]]),
                                                          fap(rT[:, 256 + tl0:256 + tl0 + 1], [[1, 16], [0, 128]]), ALU.is_equal),
                                r=["iota128", "rT"], w=SCRC[hb * 8:(hb + 1) * 8])
                            for t4 in range(4):
                                bk = 4 + (sb * 4 + t4) % 2
                                for tq in range(4):
                                    tl_ = t4 * 4 + tq
                                    PE(lambda e, tl_=tl_, tq=tq, bk=bk: e.matmul(pb[bk][:, tq * 128:(tq + 1) * 128],
                                                                                 lhsT=B16[:, o16 + tl_ * 128:o16 + (tl_ + 1) * 128],
                                                                                 rhs=C16[:, o16 + tl_ * 128:o16 + (tl_ + 1) * 128], start=True, stop=True),
                                       r=[SCRB[hb]] + SCRC[hb * 8:(hb + 1) * 8], w=[pbn[bk]])
                                ACT(lambda e, t4=t4, bk=bk: e.copy(G_sb[:, (tl0 + t4 * 4) * 128:(tl0 + t4 * 4 + 4) * 128], pb[bk][:, :]),
                                    w=["G_sb", pbn[bk]])

                def dense(blk, gen):
                    par = blk % 2
                    hnT = hnT2[par]
                    hnTn = "hnTe%d" % par
                    tok0 = blk * 256
                    S.dma("pool", fap(ptb[:, 0:1], [[256, 2], [1, 256]]),
                          AP(I["p"].tensor, layer * T * 256 + tok0 * 256, [[256, 128], [128 * 256, 2], [1, 256]]), writes=["ptb"])

                    def act_part(i2):
                        cg, ci = divmod(i2, NG)
                        idx = blk * NCG + cg
                        sl = idx % NSL
                        ba = 4 + i2 % 2
                        for kc in range(8):
                            PE(lambda e, kc=kc: e.matmul(pb[ba][:, 0:256], lhsT=ubuf[sl][:, kc * NG * 128 + ci * 128:kc * NG * 128 + (ci + 1) * 128],
                                                         rhs=hnT[:, kc * 256:(kc + 1) * 256], start=(kc == 0), stop=(kc == 7)),
                               r=["ubuf%d" % sl, hnTn], w=[pbn[ba]])
                        ACT(lambda e: e.activation(a_sb[i2 % 2][:], pb[ba][:, 0:256], AF.Gelu_apprx_tanh), w=["a_sb%d" % (i2 % 2), pbn[ba]])
                        DVE(lambda e: e.tensor_tensor(W_sb[i2 % 2][:], a_sb[i2 % 2][:], fap(G_sb[:, i2:i2 + 1], [[128, 256]]), ALU.mult),
                            r=["a_sb%d" % (i2 % 2), "G_sb"], w=["W_sb%d" % (i2 % 2)])

                    def out_part(i2):
                        cg, ci = divmod(i2, NG)
                        sl = (blk * NCG + cg) % NSL
                        for tt in range(2):
                            for dh in range(2):
                                PE(lambda e, tt=tt, dh=dh: e.matmul(pb[tt * 2 + dh][:, :], lhsT=W_sb[i2 % 2][:, tt * 128:(tt + 1) * 128],
                                                                    rhs=vbuf[sl][:, ci * D + dh * 512:ci * D + (dh + 1) * 512],
                                                                    start=(i2 == 0), stop=(i2 == 127)),
                                   r=["W_sb%d" % (i2 % 2), "vbuf%d" % sl], w=[pbn[tt * 2 + dh]])
                        if ci == NG - 1:
                            issue_uv(blk * NCG + cg + NSL)

                    act_part(0)
                    for i2 in range(128):
                        if i2 + 1 < 128:
                            act_part(i2 + 1)
                        out_part(i2)
                        if gen is not None and i2 >= 2:
                            next(gen, None)
                    if gen is not None:
                        for _ in gen:
                            pass

                def tail(blk):
                    par = blk % 2
                    hnT = hnT2[par]
                    hnTn = "hnTe%d" % par
                    tok0 = blk * 256
                    for tt in range(2):
                        hv, hn_ = HT(blk, tt)
                        for dh in range(2):
                            hvv = hv[:, dh * 512:(dh + 1) * 512]
                            DVE(lambda e, tt=tt, dh=dh, hvv=hvv: e.tensor_tensor(hvv, hvv, pb[tt * 2 + dh][:, :], ALU.add), w=[hn_, pbn[tt * 2 + dh]])
                    if debug_stop in ("P%d" % layer, "E%d" % layer):
                        for tt in range(2):
                            hv, hn_ = HT(blk, tt)
                            S.dma("sp", dbg[T + tok0 + tt * 128:T + tok0 + (tt + 1) * 128, :], hv, reads=[hn_])
                    for tt in range(2):
                        hv, hn_ = HT(blk, tt)
                        transpose_to(ptb[:, tt * 256:(tt + 1) * 256], "ptb", 6, fap(pTs[:, tt * 128:tt * 128 + 1], [[256, 2], [1, 128]]), "pTs", nk=2, eng="dve")
                        rmsnorm_tile(hv, hn_, gple[:], "gple", hnb[:, tt * D:(tt + 1) * D], "hnb%d" % tt, tt)
                        transpose_to(hnb[:, tt * D:(tt + 1) * D], "hnb%d" % tt, 7, fap(hnT[:, tt * 128:tt * 128 + 1], [[256, 8], [1, 128]]), hnTn)
                    for dh in range(2):
                        S.dma("sp", fap(wbuf[:, 0:1], [[512, 8], [1, 512]]),
                              AP(gw16.tensor, layer * D * D + dh * 512, [[D, 128], [128 * D, 8], [1, 512]]),
                              reads=["gw16_%d" % layer], writes=["wbuf"])
                        for tt in range(2):
                            hv, hn_ = HT(blk, tt)
                            bg, bp = (tt * 2 + dh) % 4, 4 + (tt * 2 + dh) % 2
                            for kc in range(8):
                                PE(lambda e, kc=kc, tt=tt, dh=dh, bg=bg: e.matmul(pb[bg][:, :], lhsT=hnT[:, kc * 256 + tt * 128:kc * 256 + (tt + 1) * 128],
                                                                                  rhs=wbuf[:, kc * 512:(kc + 1) * 512],
                                                                                  start=(kc == 0), stop=(kc == 7)), r=[hnTn, "wbuf"], w=[pbn[bg]])
                            for kc in range(2):
                                PE(lambda e, kc=kc, tt=tt, dh=dh, bp=bp: e.matmul(pb[bp][:, :], lhsT=pTs[:, kc * 256 + tt * 128:kc * 256 + (tt + 1) * 128],
                                                                                  rhs=pproj[:, kc * D + dh * 512:kc * D + (dh + 1) * 512],
                                                                                  start=(kc == 0), stop=(kc == 1)), r=["pTs", "pproj"], w=[pbn[bp]])
                            ACT(lambda e, bg=bg: e.activation(scrA[:, 0:512], pb[bg][:, :], AF.Sigmoid), w=[SCRA, pbn[bg]])
                            DVE(lambda e, bp=bp: e.tensor_tensor(scrB[:, 0:512], pb[bp][:, :], scrA[:, 0:512], ALU.mult), r=[SCRA], w=[SCRB, pbn[bp]])
                            hvv = hv[:, dh * 512:(dh + 1) * 512]
                            DVE(lambda e, hvv=hvv: e.tensor_tensor(hvv, hvv, scrB[:, 0:512], ALU.add), r=[SCRB], w=[hn_])
                    for tt in range(2):
                        hv, hn_ = HT(blk, tt)
                        rows = slice(tok0 + tt * 128, tok0 + (tt + 1) * 128)
                        if final:
                            rmsnorm_tile(hv, hn_, gfin[:], "gfin", scrC[:, 0:D], SCRC, tt)
                            S.dma("sp", out[rows, :], scrC[:, 0:D], reads=SCRC)
                        else:
                            S.dma("sp", h_dram[rows, :], hv, reads=[hn_], writes=["h_dram"])
                        if debug_stop == "E%d" % layer:
                            S.dma("sp", dbg[2 * T + tok0 + tt * 128:2 * T + tok0 + (tt + 1) * 128, :],
                                  (scrC[:, 0:D] if final else hv), reads=SCRC + [hn_])

                for _ in front(0):
                    pass
                gbuild(0)
                for blk in range(16):
                    gen = front(blk + 1) if blk + 1 < 16 else None
                    dense(blk, gen)
                    tail(blk)
                    if blk + 1 < 16:
                        gbuild(blk + 1)
            S.barrier()

        peer_phase(0, False, False)
        if debug_stop in ("P0", "E0"):
            S.finish("sp")
            return nc
        peer_phase(1, True, True)
        S.finish("sp")
        return nc


def _host_inputs(inputs, b):
    f = lambda a: np.ascontiguousarray(np.asarray(a, dtype=np.float32))
    m = {}
    m["x"] = f(inputs["x"][b])
    m["p"] = f(inputs["p"][:, b])
    m["mix_norm"] = f(inputs["mix_norm"])
    m["ab_w_in"] = f(inputs["ab_w_in"][0])
    m["ab_conv_w"] = f(inputs["ab_conv_w"][0]).reshape(124, 128)
    m["ab_conv_b"] = f(inputs["ab_conv_b"][0]).reshape(4, 128)
    m["ab_conv_ln_g"] = f(inputs["ab_conv_ln_g"][0]).reshape(4, 128)
    m["ab_conv_ln_b"] = f(inputs["ab_conv_ln_b"][0]).reshape(4, 128)
    m["ab_cmp_pos"] = f(inputs["ab_cmp_pos"][0])
    m["ab_cmp_w1"] = f(inputs["ab_cmp_w1"][0])
    m["ab_cmp_w2"] = f(inputs["ab_cmp_w2"][0])
    m["ab_w_out"] = f(inputs["ab_w_out"][0])
    m["pool_w"] = f(inputs["pool_w"][0])
    m["pool_scale"] = f(inputs["pool_scale"][0])
    for k in ("ffn_norm", "peer_wq", "peer_subkeys", "ple_norm", "ple_gate_w", "ple_proj", "final_norm"):
        m[k] = f(inputs[k])
    return m


def _run(inputs, debug_stop=None, ncores=8):
    shared = _host_inputs(inputs, 0)
    u = np.asarray(inputs["peer_u"], dtype=np.float32).reshape(2, 128, 128, D)
    u6 = u.reshape(2, 128, 128 // NG_, NG_, 8, 128)
    shared["uP"] = np.ascontiguousarray(u6.transpose(0, 2, 5, 4, 3, 1)).reshape(2, D, 16384)
    v = np.asarray(inputs["peer_v"], dtype=np.float32).reshape(2, 128, 128 // NG_, NG_, D)
    shared["vP"] = np.ascontiguousarray(v.transpose(0, 2, 1, 3, 4)).reshape(2, 16384, D)
    shared.update(make_consts())
    in_maps = []
    for b in range(ncores):
        m = dict(shared)
        m["x"] = np.ascontiguousarray(np.asarray(inputs["x"][b], dtype=np.float32))
        m["p"] = np.ascontiguousarray(np.asarray(inputs["p"][:, b], dtype=np.float32))
        in_maps.append(m)
    nc = build_program(debug_stop)
    res = run_bass_kernel_spmd(nc, in_maps, core_ids=list(range(ncores)))
    return res


def kernel(**inputs):
    res = _run(inputs, None)
    return np.stack([np.asarray(r["out"], dtype=np.float32) for r in res.results], axis=0)
```

```python
import numpy as np
from contextlib import ExitStack
import concourse.bass as bass
import concourse.mybir as mybir
from concourse.bass_utils import run_bass_kernel_spmd
from concourse.ap import AP

F32 = mybir.dt.float32
BF16 = mybir.dt.bfloat16
U32 = mybir.dt.uint32
ALU = mybir.AluOpType
AF = mybir.ActivationFunctionType
AX = mybir.AxisListType

T = 4096
D = 1024
NT = 32
NEG = -30000.0
EPS = 1e-6
INC = 2328
NG_ = 2
DEBUG_STOP = None


class Sched:
    def __init__(self, nc, n_dma_sems=12):
        self.nc = nc
        self.eng = {"pe": nc.tensor, "dve": nc.vector, "act": nc.scalar, "pool": nc.gpsimd, "sp": nc.sync}
        self.sem = {k: nc.alloc_semaphore("sem_" + k) for k in ("pe", "dve", "act", "pool")}
        self.cnt = {k: 0 for k in self.sem}
        self.waited = {}
        self.dsem = [nc.alloc_semaphore("dsem%d" % i) for i in range(n_dma_sems)]
        self.ssem = []
        self.dcnt = [0] * n_dma_sems
        self.dnext = 0
        self.last_w = {}
        self.readers = {}
        self.ninst = 0

    def _need(self, engine, tok):
        if tok is None:
            return
        kind, idx, val = tok
        if kind == "c" and idx == "pe" and engine == "pe":
            return
        key = (engine, kind, idx)
        if self.waited.get(key, 0) >= val:
            return
        self.waited[key] = val
        sem = self.sem[idx] if kind == "c" else (self.dsem[idx] if kind == "d" else self.ssem[idx])
        self.eng[engine].wait_ge(sem, val)
        self.ninst += 1

    @staticmethod
    def _flat(xs):
        o = []
        for x in xs:
            if isinstance(x, (list, tuple)):
                o.extend(Sched._flat(x))
            else:
                o.append(x)
        return o

    def _deps(self, engine, reads, writes):
        for r in reads:
            self._need(engine, self.last_w.get(r))
        for w in writes:
            self._need(engine, self.last_w.get(w))
            for t in self.readers.get(w, ()):
                self._need(engine, t)

    def _record(self, tok, reads, writes):
        for r in reads:
            self.readers.setdefault(r, []).append(tok)
        for w in writes:
            self.last_w[w] = tok
            self.readers[w] = []

    def op(self, engine, fn, reads=(), writes=()):
        reads, writes = self._flat(reads), self._flat(writes)
        self._deps(engine, reads, writes)
        inst = fn(self.eng[engine])
        self.cnt[engine] += 1
        inst.then_inc(self.sem[engine], 1)
        tok = ("c", engine, self.cnt[engine])
        self._record(tok, reads, writes)
        self.ninst += 1
        return tok

    def dma(self, queue, out, in_, reads=(), writes=(), **kw):
        reads, writes = self._flat(reads), self._flat(writes)
        self._deps(queue, reads, writes)
        if queue == "pool":
            sem = self.nc.alloc_semaphore("ssem%d" % len(self.ssem))
            self.ssem.append(sem)
            self.eng[queue].dma_start(out=out, in_=in_, **kw).then_inc(sem, 16)
            tok = ("s", len(self.ssem) - 1, 16)
            self._record(tok, reads, writes)
            self.ninst += 1
            return tok
        i = self.dnext
        self.dnext = (self.dnext + 1) % len(self.dsem)
        if self.dcnt[i] > 0:
            self._need(queue, ("d", i, self.dcnt[i]))
        self.dcnt[i] += 16
        self.eng[queue].dma_start(out=out, in_=in_, **kw).then_inc(self.dsem[i], 16)
        tok = ("d", i, self.dcnt[i])
        self._record(tok, reads, writes)
        self.ninst += 1
        return tok

    def barrier(self):
        for e in ("pe", "dve", "act", "pool", "sp"):
            self.finish(e)

    def finish(self, engine="sp"):
        for i in range(len(self.ssem)):
            self._need(engine, ("s", i, 16))
        for i, c in enumerate(self.dcnt):
            if c:
                self._need(engine, ("d", i, c))
        for k, c in self.cnt.items():
            if c:
                self._need(engine, ("c", k, c))


def fap(ap, dims, off=0):
    return AP(ap.tensor, ap.offset + off, [list(ap.ap[0])] + [list(d) for d in dims])


def make_consts():
    c = {}
    n = np.arange(256)[:, None]
    s = np.arange(64)[None, :]
    ov = ((16 * n < 64 * s + 64) & (16 * n + 32 > 64 * s) & (n < 255)).astype(np.float32)
    c["c_overlap"] = ov
    cc = np.arange(32)[:, None, None, None]
    nl = np.arange(128)[None, :, None, None]
    kch = np.arange(2)[None, None, :, None]
    tl = np.arange(128)[None, None, None, :]
    nn = kch * 128 + nl
    vis = (16 * nn + 31 <= 128 * cc + tl) & (nn < 255)
    c["c_cmpbias"] = np.where(vis, 1.0, 0.0).astype(np.float32)
    cc = np.arange(32)[:, None, None]
    tl = np.arange(128)[None, :, None]
    blk = np.arange(64)[None, None, :]
    t = 128 * cc + tl
    cur = t // 64
    F = np.zeros((32, 128, 64), np.float32)
    F = np.where((blk == 0) | (blk == cur) | (blk == cur - 1), 1e9, F)
    F = np.where(blk > cur, -1e30, F)
    c["c_F"] = F.astype(np.float32)
    key = np.arange(4096)[None, :]
    c["c_E"] = (key // 64 == np.arange(64)[:, None]).astype(np.float32)
    kl = np.arange(128)[:, None]
    tl = np.arange(128)[None, :]
    c["c_mask2"] = np.stack([np.where(kl <= tl, 1.0, 0.0), np.where(kl > tl, 1.0, 0.0)]).astype(np.float32)
    A = np.zeros((4, 3, 128, 128), np.float32)
    tp = np.arange(128)[:, None]
    tt = np.arange(128)[None, :]
    for wi, w in enumerate((2, 4, 8, 16)):
        A[wi, 0] = np.where((tp <= tt) & (tp >= tt - w + 1), 1.0 / w, 0.0) - (tp == tt)
        A[wi, 1] = np.where(tp - 128 >= tt - w + 1, 1.0 / w, 0.0)
        cnt = np.minimum(w, tt + 1)
        A[wi, 2] = np.where((tp <= tt) & (tp >= tt - w + 1), 1.0 / cnt, 0.0) - (tp == tt)
    c["c_poolA"] = A
    c["c_iota128"] = np.tile(np.arange(128, dtype=np.float32)[None, :], (128, 1))
    c["c_iota16"] = np.tile(np.arange(16, dtype=np.float32)[None, :], (128, 1))
    c["c_ident"] = np.eye(128, dtype=np.float32)
    return c


CONST_SHAPES = {"c_overlap": [256, 64], "c_cmpbias": [32, 128, 2, 128], "c_F": [32, 128, 64], "c_E": [64, 4096],
                "c_mask2": [2, 128, 128], "c_poolA": [4, 3, 128, 128], "c_iota128": [128, 128],
                "c_iota16": [128, 16], "c_ident": [128, 128]}

IN_SHAPES = {
    "x": [T, D], "p": [2, T, 256], "mix_norm": [2, D], "ab_w_in": [D, INC], "ab_conv_w": [124, 128],
    "ab_conv_b": [4, 128], "ab_conv_ln_g": [4, 128], "ab_conv_ln_b": [4, 128], "ab_cmp_pos": [2, 32, 64],
    "ab_cmp_w1": [2, 2048, 256], "ab_cmp_w2": [2, 256, 64], "ab_w_out": [D, D], "pool_w": [4, 256, 256],
    "pool_scale": [D], "ffn_norm": [2, D], "peer_wq": [2, D, D], "peer_subkeys": [2, 8, 2, 128, 64],
    "uP": [2, D, 16384], "vP": [2, 16384, D], "ple_norm": [2, D], "ple_gate_w": [2, D, D],
    "ple_proj": [2, 256, D], "final_norm": [D],
}


def build_program(debug_stop=None):
    nc = bass.Bass("TRN2", target_bir_lowering=False)
    I = {k: nc.dram_tensor(k, v, F32, kind="ExternalInput").ap() for k, v in IN_SHAPES.items()}
    C = {k: nc.dram_tensor(k, v, F32, kind="ExternalInput").ap() for k, v in CONST_SHAPES.items()}
    out = nc.dram_tensor("out", [T, D], F32, kind="ExternalOutput").ap()
    dbg = None
    if debug_stop is not None:
        dbg = nc.dram_tensor("dbg", [3 * T, D], F32, kind="ExternalOutput").ap()
    h_dram = nc.dram_tensor("h_dram", [T, D], F32).ap()
    aoT_dram = nc.dram_tensor("aoT_dram", [512, T], BF16).ap()
    u16 = nc.dram_tensor("u16", [2, D, 16384], BF16).ap()
    v16 = nc.dram_tensor("v16", [2, 16384, D], BF16).ap()
    wq16 = nc.dram_tensor("wq16", [2, D, D], BF16).ap()
    gw16 = nc.dram_tensor("gw16", [2, D, D], BF16).ap()

    S = Sched(nc)
    PE = lambda fn, r=(), w=(): S.op("pe", fn, r, w)
    DVE = lambda fn, r=(), w=(): S.op("dve", fn, r, w)
    ACT = lambda fn, r=(), w=(): S.op("act", fn, r, w)
    POOL = lambda fn, r=(), w=(): S.op("pool", fn, r, w)

    with ExitStack() as es0:
        def sbt(es, name, shape, dt):
            return es.enter_context(nc.sbuf_tensor(name, shape, dt))
        pb = [es0.enter_context(nc.psum_tensor("pb%d" % i, [128, 512], F32)) for i in range(8)]
        pbn = ["pb%d" % i for i in range(8)]
        pbT = [pb[i][:].bitcast(BF16) for i in range(8)]

        ident_f = sbt(es0, "ident_f", [128, 128], F32)
        ident_b = sbt(es0, "ident_b", [128, 128], BF16)
        ones_f = sbt(es0, "ones_f", [128, 128], F32)
        zeros_b = sbt(es0, "zeros_b", [128, 512], BF16)
        junk = sbt(es0, "junk", [128, 1024], BF16)
        ss = sbt(es0, "ss", [128, 8], F32)
        rs = sbt(es0, "rs", [128, 8], F32)
        S.dma("sp", ident_f[:], C["c_ident"][:, :], writes=["ident_f"])
        S.dma("pool", ident_b[:], C["c_ident"][:, :], writes=["ident_b"])
        POOL(lambda e: e.memset(ones_f[:], 1.0), w=["ones_f"])
        POOL(lambda e: e.memset(zeros_b[:], 0.0), w=["zeros_b"])

        def rmsnorm_tile(x_ap, xres, g_ap, gres, out_ap, outres, col):
            ACT(lambda e: e.activation(junk[:], x_ap, AF.Square, accum_out=ss[:, col:col + 1]), r=[xres], w=["junk", "ss%d" % col])
            DVE(lambda e: e.tensor_scalar(rs[:, col:col + 1], ss[:, col:col + 1], 1.0 / D, EPS, ALU.mult, ALU.add),
                r=["ss%d" % col], w=["rs%d" % col])
            ACT(lambda e: e.activation(rs[:, col:col + 1], rs[:, col:col + 1], AF.Sqrt), w=["rs%d" % col])
            DVE(lambda e: e.reciprocal(rs[:, col:col + 1], rs[:, col:col + 1]), w=["rs%d" % col])
            DVE(lambda e: e.scalar_tensor_tensor(out_ap, x_ap, rs[:, col:col + 1], g_ap, ALU.mult, ALU.mult),
                r=[xres, "rs%d" % col, gres], w=(outres if isinstance(outres, list) else [outres]))

        def transpose_to(hn_ap, hnres, bank, dst_ap, dstres, nk=8, eng="act"):
            for kc in range(nk):
                PE(lambda e, kc=kc: e.transpose(pbT[bank][:, kc * 128:(kc + 1) * 128], hn_ap[:, kc * 128:(kc + 1) * 128], ident_b[:]),
                   r=[hnres, "ident_b"], w=[pbn[bank]])
            src = fap(pbT[bank][:, 0:1], [[128, nk], [1, 128]])
            if eng == "act":
                ACT(lambda e: e.copy(dst_ap, src), w=[dstres, pbn[bank]])
            else:
                DVE(lambda e: e.tensor_copy(dst_ap, src), w=[dstres, pbn[bank]])

        esL0 = es0.enter_context(ExitStack())
        qT_all = sbt(esL0, "qT_all", [128, 4 * T], BF16)
        ksT = sbt(esL0, "ksT", [128, T], BF16)
        kwT = sbt(esL0, "kwT", [128, T], BF16)
        vs_aug = sbt(esL0, "vs_aug", [128, NT * 130], BF16)
        vw_aug = sbt(esL0, "vw_aug", [128, NT * 130], BF16)
        gsig = sbt(esL0, "gsig", [128, NT * 24], F32)
        kcmpT = sbt(esL0, "kcmpT", [128, 256], BF16)
        vc_aug = sbt(esL0, "vc_aug", [128, 2 * 2 * 129], BF16)
        POOL(lambda e: e.memset(vs_aug[:], 1.0), w=["vs_aug"])
        POOL(lambda e: e.memset(vw_aug[:], 1.0), w=["vw_aug"])
        POOL(lambda e: e.memset(vc_aug[:], 1.0), w=["vc_aug"])

        esAB = es0.enter_context(ExitStack())
        kcT = sbt(esAB, "kcT", [128, T], BF16)
        vcT = sbt(esAB, "vcT", [128, T], BF16)

        with ExitStack() as esA:
            w_in_sb = sbt(esA, "w_in_sb", [128, 8 * INC], BF16)
            wqr = sbt(esA, "wqr", [128, 8 * 512], BF16)
            g0 = sbt(esA, "g0", [128, D], F32)
            cw = sbt(esA, "cw", [128, 124], F32)
            cp = sbt(esA, "cp", [128, 12], F32)
            stg = sbt(esA, "stg", [128, 128], F32)
            xt = [sbt(esA, "xt%d" % i, [128, D], F32) for i in range(2)]
            hn = [sbt(esA, "hn%d" % i, [128, D], BF16) for i in range(2)]
            hnT = sbt(esA, "hnT", [128, 8 * 512], BF16)
            a_pad = sbt(esA, "a_pad", [128, 4 * 542], F32)
            y = sbt(esA, "y", [128, 4 * 512], F32)
            ysq = [sbt(esA, "ysq%d" % i, [128, 512], F32) for i in range(2)]
            sig = [sbt(esA, "sig%d" % i, [128, 512], F32) for i in range(2)]
            mean_sb = sbt(esA, "mean_sb", [128, 512], F32)
            msq = sbt(esA, "msq", [128, 512], F32)
            rstd = sbt(esA, "rstd", [128, 512], F32)
            ao = sbt(esA, "ao", [128, 4 * 512], BF16)

            S.dma("pool", fap(w_in_sb[:, 0:1], [[INC, 8], [1, INC]]), AP(I["ab_w_in"].tensor, 0, [[INC, 128], [128 * INC, 8], [1, INC]]),
                  writes=["w_in_sb"])
            for g_ in range(2):
                for h_ in range(4):
                    src = AP(I["ab_w_in"].tensor, 1024 + (g_ * 4 + h_) * 64, [[INC, 128], [128 * INC, 8], [1, 64]])
                    dst = fap(wqr[:, h_ * 128 + g_ * 64:h_ * 128 + g_ * 64 + 1], [[512, 8], [1, 64]])
                    S.dma("pool", dst, src, writes=["wqr"])
            S.dma("sp", g0[:], I["mix_norm"][0, :].partition_broadcast(128), writes=["g0"])
            S.dma("sp", stg[0:124, :], I["ab_conv_w"][:, :], writes=["stg"])
            PE(lambda e: e.transpose(pb[7][:, 0:124], stg[0:124, :], ident_f[0:124, 0:124]), r=["stg", "ident_f"], w=[pbn[7]])
            DVE(lambda e: e.tensor_copy(cw[:], pb[7][:, 0:124]), w=["cw", pbn[7]])
            S.dma("sp", stg[0:4, :], I["ab_conv_b"][:, :], writes=["stg"], reads=[])
            S.dma("sp", stg[4:8, :], I["ab_conv_ln_g"][:, :], writes=["stg"])
            S.dma("sp", stg[8:12, :], I["ab_conv_ln_b"][:, :], writes=["stg"])
            PE(lambda e: e.transpose(pb[7][:, 0:12], stg[0:12, :], ident_f[0:12, 0:12]), r=["stg", "ident_f"], w=[pbn[7]])
            DVE(lambda e: e.tensor_copy(cp[:], pb[7][:, 0:12]), w=["cp", pbn[7]])
            for j in range(4):
                DVE(lambda e, j=j: e.memset(a_pad[:, j * 542:j * 542 + 30], 0.0), w=["a_pad%d" % j])

            for l in range(2):
                for r in range(2):
                    S.dma("pool", u16[l, r * 512:(r + 1) * 512, :], I["uP"][l, r * 512:(r + 1) * 512, :], writes=["u16_%d" % l])
                for r in range(2):
                    S.dma("pool", v16[l, r * 8192:(r + 1) * 8192, :], I["vP"][l, r * 8192:(r + 1) * 8192, :], writes=["v16_%d" % l])
                S.dma("pool", wq16[l, :, :], I["peer_wq"][l, :, :], writes=["wq16_%d" % l])
                S.dma("pool", gw16[l, :, :], I["ple_gate_w"][l, :, :], writes=["gw16_%d" % l])
            bank_rr = [0]

            def nb():
                b = bank_rr[0]
                bank_rr[0] = (b + 1) % 6
                return b

            for st in range(8):
                t0 = st * 512
                for j in range(4):
                    tile = st * 4 + j
                    b2 = j % 2
                    S.dma("sp", xt[b2][:], I["x"][tile * 128:(tile + 1) * 128, :], writes=["xt%d" % b2])
                    rmsnorm_tile(xt[b2][:], "xt%d" % b2, g0[:], "g0", hn[b2][:], "hn%d" % b2, b2)
                    transpose_to(hn[b2], "hn%d" % b2, 6 + b2, fap(hnT[:, j * 128:j * 128 + 1], [[512, 8], [1, 128]]), "hnT")

                def proj_T(lhs_tile, col_fn, bank):
                    for kc in range(8):
                        PE(lambda e, kc=kc: e.matmul(pb[bank][:], lhsT=col_fn(kc), rhs=hnT[:, kc * 512:(kc + 1) * 512],
                                                     start=(kc == 0), stop=(kc == 7)),
                           r=[lhs_tile, "hnT"], w=[pbn[bank]])

                for j in range(4):
                    bv, bg = nb(), nb()
                    proj_T("w_in_sb", lambda kc, j=j: w_in_sb[:, kc * INC + j * 128:kc * INC + (j + 1) * 128], bv)
                    proj_T("w_in_sb", lambda kc, j=j: w_in_sb[:, kc * INC + 512 + j * 128:kc * INC + 512 + (j + 1) * 128], bg)
                    ACT(lambda e, j=j, bg=bg: e.activation(sig[j % 2][:], pb[bg][:], AF.Sigmoid), w=["sig%d" % (j % 2), pbn[bg]])
                    DVE(lambda e, j=j, bv=bv: e.tensor_tensor(a_pad[:, j * 542 + 30:j * 542 + 542], pb[bv][:], sig[j % 2][:], ALU.mult),
                        r=["sig%d" % (j % 2)], w=["a_pad%d" % j, pbn[bv]])
                for h in range(4):
                    b = nb()
                    proj_T("wqr", lambda kc, h=h: wqr[:, kc * 512 + h * 128:kc * 512 + (h + 1) * 128], b)
                    ACT(lambda e, h=h, b=b: e.mul(qT_all[:, h * T + t0:h * T + t0 + 512], pb[b][:], 0.125), w=["qT_all", pbn[b]])
                for (tl_, tn, col0) in ((kcT, "kcT", 1536), (vcT, "vcT", 1664), (ksT, "ksT", 1792), (kwT, "kwT", 2048)):
                    b = nb()
                    proj_T("w_in_sb", lambda kc, col0=col0: w_in_sb[:, kc * INC + col0:kc * INC + col0 + 128], b)
                    ACT(lambda e, tl_=tl_, b=b: e.copy(tl_[:, t0:t0 + 512], pb[b][:]), w=[tn, pbn[b]])
                for j in range(4):
                    tile = st * 4 + j
                    b = nb()
                    for kc in range(8):
                        PE(lambda e, kc=kc, j=j, b=b: e.matmul(pb[b][:, 0:128], lhsT=hnT[:, kc * 512 + j * 128:kc * 512 + (j + 1) * 128],
                                                               rhs=w_in_sb[:, kc * INC + 1920:kc * INC + 2048], start=(kc == 0), stop=(kc == 7)),
                           r=["hnT", "w_in_sb"], w=[pbn[b]])
                    for kc in range(8):
                        PE(lambda e, kc=kc, j=j, b=b: e.matmul(pb[b][:, 128:280], lhsT=hnT[:, kc * 512 + j * 128:kc * 512 + (j + 1) * 128],
                                                               rhs=w_in_sb[:, kc * INC + 2176:kc * INC + 2328], start=(kc == 0), stop=(kc == 7)),
                           r=["hnT", "w_in_sb"], w=[pbn[b]])
                    ACT(lambda e, b=b, tile=tile: e.copy(fap(vs_aug[:, tile * 130:tile * 130 + 1], [[65, 2], [1, 64]]),
                                                         fap(pb[b][:, 0:1], [[64, 2], [1, 64]])), w=["vs_aug", pbn[b]])
                    ACT(lambda e, b=b, tile=tile: e.copy(fap(vw_aug[:, tile * 130:tile * 130 + 1], [[65, 2], [1, 64]]),
                                                         fap(pb[b][:, 128:129], [[64, 2], [1, 64]])), w=["vw_aug", pbn[b]])
                    ACT(lambda e, b=b, tile=tile: e.activation(gsig[:, tile * 24:(tile + 1) * 24], pb[b][:, 256:280], AF.Sigmoid),
                        w=["gsig", pbn[b]])
                for j in range(4):
                    engn = "dve"
                    yj = y[:, j * 512:(j + 1) * 512]
                    S.op(engn, lambda e, j=j, yj=yj: e.tensor_scalar(yj, a_pad[:, j * 542:j * 542 + 512], cw[:, j:j + 1], cp[:, j:j + 1],
                                                                     ALU.mult, ALU.add), ["a_pad%d" % j, "cw", "cp"], ["y%d" % j])
                    for k in range(1, 31):
                        S.op(engn, lambda e, j=j, k=k, yj=yj: e.scalar_tensor_tensor(yj, a_pad[:, j * 542 + k:j * 542 + k + 512],
                                                                                      cw[:, k * 4 + j:k * 4 + j + 1], yj, ALU.mult, ALU.add),
                             ["a_pad%d" % j, "cw"], ["y%d" % j])
                    S.op(engn, lambda e, j=j: e.tensor_copy(a_pad[:, j * 542:j * 542 + 30], a_pad[:, j * 542 + 512:j * 542 + 542]),
                         [], ["a_pad%d" % j])
                b1, b2_ = nb(), nb()
                for j in range(4):
                    PE(lambda e, j=j: e.matmul(pb[b1][:], lhsT=ones_f[:], rhs=y[:, j * 512:(j + 1) * 512], start=(j == 0), stop=(j == 3)),
                       r=["ones_f", "y%d" % j], w=[pbn[b1]])
                for j in range(4):
                    ACT(lambda e, j=j: e.activation(ysq[j % 2][:], y[:, j * 512:(j + 1) * 512], AF.Square), r=["y%d" % j], w=["ysq%d" % (j % 2)])
                    PE(lambda e, j=j: e.matmul(pb[b2_][:], lhsT=ones_f[:], rhs=ysq[j % 2][:], start=(j == 0), stop=(j == 3)),
                       r=["ones_f", "ysq%d" % (j % 2)], w=[pbn[b2_]])
                DVE(lambda e: e.tensor_scalar(mean_sb[:], pb[b1][:], 1.0 / 512, None, ALU.mult), w=["mean_sb", pbn[b1]])
                DVE(lambda e: e.tensor_tensor(msq[:], mean_sb[:], mean_sb[:], ALU.mult), r=["mean_sb"], w=["msq"])
                DVE(lambda e: e.scalar_tensor_tensor(rstd[:], pb[b2_][:], 1.0 / 512, msq[:], ALU.mult, ALU.subtract), r=["msq"], w=["rstd", pbn[b2_]])
                DVE(lambda e: e.tensor_scalar(rstd[:], rstd[:], EPS, None, ALU.add), r=[], w=["rstd"])
                ACT(lambda e: e.activation(rstd[:], rstd[:], AF.Sqrt), w=["rstd"])
                DVE(lambda e: e.reciprocal(rstd[:], rstd[:]), w=["rstd"])
                for j in range(4):
                    yj = y[:, j * 512:(j + 1) * 512]
                    DVE(lambda e, yj=yj: e.tensor_tensor(yj, yj, mean_sb[:], ALU.subtract), r=["mean_sb"], w=["y%d" % j])
                    DVE(lambda e, yj=yj: e.tensor_tensor(yj, yj, rstd[:], ALU.mult), r=["rstd"], w=["y%d" % j])
                    ACT(lambda e, j=j, yj=yj: e.activation(ao[:, j * 512:(j + 1) * 512], yj, AF.Silu, bias=cp[:, 8 + j:9 + j], scale=cp[:, 4 + j:5 + j]),
                        r=["y%d" % j, "cp"], w=["ao%d" % j])
                    S.dma("sp", aoT_dram[j * 128:(j + 1) * 128, t0:t0 + 512], ao[:, j * 512:(j + 1) * 512], reads=["ao%d" % j], writes=["aoT_dram"])
        S.barrier()
        if debug_stop == "A":
            S.dma("pool", AP(dbg.tensor, 0, [[4096, 512], [1, 4096]]), aoT_dram[:, :], reads=["aoT_dram"])
            S.finish("sp")
            return nc
        with ExitStack() as esB:
            w1_sb = sbt(esB, "w1_sb", [128, 32 * 256], BF16)
            w2_sb = sbt(esB, "w2_sb", [128, 128], BF16)
            w2pad = sbt(esB, "w2pad", [128, 4 * 128], BF16)
            posr = sbt(esB, "posr", [32, 128], F32)
            posT = sbt(esB, "posT", [128, 32], BF16)
            lo = sbt(esB, "lo", [128, T], BF16)
            hi = sbt(esB, "hi", [128, T], BF16)
            hid = [sbt(esB, "hid%d" % g, [128, 512], BF16) for g in range(2)]
            for kch in range(2):
                for g in range(2):
                    S.dma("pool", vc_aug[:, (kch * 2 + g) * 129 + 65:(kch * 2 + g) * 129 + 129],
                          C["c_overlap"][kch * 128:(kch + 1) * 128, :], writes=["vc_aug"])
            for src_i, (srcT, srcname) in enumerate(((kcT, "kcT"), (vcT, "vcT"))):
                for dup in range(2):
                    src_ap = AP(I["ab_cmp_w1"].tensor, src_i * 2048 * 256, [[256, 64], [64 * 256, 32], [1, 256]])
                    S.dma("pool", fap(w1_sb[dup * 64:(dup + 1) * 64, 0:1], [[256, 32], [1, 256]]), src_ap, writes=["w1_sb"])
                S.dma("pool", fap(w2_sb[:, 0:1], [[64, 2], [1, 64]]),
                      AP(I["ab_cmp_w2"].tensor, src_i * 256 * 64, [[64, 128], [128 * 64, 2], [1, 64]]), writes=["w2_sb"])
                for dup in range(2):
                    S.dma("sp", posr[:, dup * 64:(dup + 1) * 64], I["ab_cmp_pos"][src_i, :, :], writes=["posr"])
                PE(lambda e: e.transpose(pb[7][:, 0:32], posr[0:32, :], ident_f[0:32, 0:32]), r=["posr", "ident_f"], w=[pbn[7]])
                DVE(lambda e: e.tensor_copy(posT[:], pb[7][:, 0:32]), w=["posT", pbn[7]])
                DVE(lambda e, srcT=srcT: e.tensor_tensor(fap(lo[:, 0:1], [[16, 256], [1, 16]]), fap(srcT[:, 0:1], [[16, 256], [1, 16]]),
                                                         fap(posT[:, 0:1], [[0, 256], [1, 16]]), ALU.add), r=[srcname, "posT"], w=["lo"])
                DVE(lambda e, srcT=srcT: e.tensor_tensor(fap(hi[:, 0:1], [[16, 256], [1, 16]]), fap(srcT[:, 0:1], [[16, 256], [1, 16]]),
                                                         fap(posT[:, 16:17], [[0, 256], [1, 16]]), ALU.add), r=[srcname, "posT"], w=["hi"])
                for g in range(2):
                    for jc in range(2):
                        bk = g * 2 + jc
                        for l in range(32):
                            src_t = lo if l < 16 else hi
                            PE(lambda e, g=g, jc=jc, l=l, bk=bk, src_t=src_t: e.matmul(
                                pb[bk][:, 0:255], lhsT=w1_sb[g * 64:(g + 1) * 64, l * 256 + jc * 128:l * 256 + (jc + 1) * 128],
                                rhs=fap(src_t[g * 64:(g + 1) * 64, l:l + 1], [[16, 255]]), start=(l == 0), stop=(l == 31)),
                               r=["w1_sb", "lo", "hi"], w=[pbn[bk]])
                        ACT(lambda e, g=g, jc=jc, bk=bk: e.activation(hid[g][:, jc * 256:jc * 256 + 255], pb[bk][:, 0:255], AF.Gelu_apprx_tanh),
                            w=["hid%d" % g, pbn[bk]])
                if src_i == 0:
                    DVE(lambda e: e.memset(w2pad[:], 0.0), w=["w2pad"])
                    for g in range(2):
                        for jc in range(2):
                            DVE(lambda e, g=g, jc=jc: e.tensor_copy(w2pad[:, (g * 2 + jc) * 128 + g * 64:(g * 2 + jc) * 128 + g * 64 + 64],
                                                                    w2_sb[:, jc * 64:(jc + 1) * 64]), r=["w2_sb"], w=["w2pad"])
                    n_ = 0
                    for g in range(2):
                        for jc in range(2):
                            PE(lambda e, g=g, jc=jc, n_=n_: e.matmul(pb[4][:, 0:255], lhsT=w2pad[:, (g * 2 + jc) * 128:(g * 2 + jc + 1) * 128],
                                                                      rhs=hid[g][:, jc * 256:jc * 256 + 255], start=(n_ == 0), stop=(n_ == 3)),
                               r=["w2pad", "hid%d" % g], w=[pbn[4]])
                            n_ += 1
                    DVE(lambda e: e.tensor_copy(kcmpT[:, 0:255], pb[4][:, 0:255]), w=["kcmpT", pbn[4]])
                    DVE(lambda e: e.memset(kcmpT[:, 255:256], 0.0), w=["kcmpT"])
                else:
                    for g in range(2):
                        for kch in range(2):
                            nr = 128 if kch == 0 else 127
                            bk = 4 + (g * 2 + kch) % 2
                            for jc in range(2):
                                PE(lambda e, g=g, kch=kch, jc=jc, nr=nr, bk=bk: e.matmul(
                                    pb[bk][0:nr, 0:64], lhsT=hid[g][:, jc * 256 + kch * 128:jc * 256 + kch * 128 + nr],
                                    rhs=w2_sb[:, jc * 64:(jc + 1) * 64], start=(jc == 0), stop=(jc == 1)),
                                   r=["hid%d" % g, "w2_sb"], w=[pbn[bk]])
                            DVE(lambda e, g=g, kch=kch, nr=nr, bk=bk: e.tensor_copy(vc_aug[0:nr, (kch * 2 + g) * 129:(kch * 2 + g) * 129 + 64],
                                                                                   pb[bk][0:nr, 0:64]), w=["vc_aug", pbn[bk]])
        esAB.close()
        S.barrier()
        if debug_stop == "B":
            dtmp = es0.enter_context(nc.sbuf_tensor("dtmp", [128, 256 + 516], F32))
            DVE(lambda e: e.tensor_copy(dtmp[:, 0:256], kcmpT[:]), r=["kcmpT"], w=["dtmp"])
            DVE(lambda e: e.tensor_copy(dtmp[:, 256:772], vc_aug[:]), r=["vc_aug"], w=["dtmp"])
            S.dma("sp", AP(dbg.tensor, 0, [[256, 128], [1, 256]]), dtmp[:, 0:256], reads=["dtmp"])
            S.dma("sp", AP(dbg.tensor, 128 * 256, [[516, 128], [1, 516]]), dtmp[:, 256:772], reads=["dtmp"])
            S.finish("sp")
            return nc

        with ExitStack() as esC:
            E_sb = sbt(esC, "E_sb", [128, T], BF16)
            mask2 = sbt(esC, "mask2", [128, 256], BF16)
            w_out_sb = sbt(esC, "w_out_sb", [128, 8 * D], BF16)
            cmpb_all = sbt(esC, "cmpb_all", [128, 32 * 256], BF16)
            F_sb = [sbt(esC, "F_sb%d" % i, [128, 64], F32) for i in range(2)]
            xc = [sbt(esC, "xc%d" % i, [128, D], F32) for i in range(2)]
            aot = [sbt(esC, "aot%d" % i, [128, 512], BF16) for i in range(2)]
            pT = [sbt(esC, "pT%d" % i, [128, 512], BF16) for i in range(3)]
            rz = sbt(esC, "rz", [128, 4], F32)
            wg = sbt(esC, "wg", [128, 4], F32)
            imp = sbt(esC, "imp", [128, 64], F32)
            imp2 = sbt(esC, "imp2", [128, 64], F32)
            m8a = sbt(esC, "m8a", [128, 8], F32)
            m8b = sbt(esC, "m8b", [128, 8], F32)
            negb = sbt(esC, "negb", [128, 128], F32)
            negT = sbt(esC, "negT", [128, 128], BF16)
            oacc = sbt(esC, "oacc", [128, 512], F32)
            obf = sbt(esC, "obf", [128, 512], BF16)
            boT = sbt(esC, "boT", [128, 512], BF16)
            hnew = [sbt(esC, "hnew%d" % i, [128, D], F32) for i in range(2)]
            for dup in range(2):
                S.dma("pool", E_sb[dup * 64:(dup + 1) * 64, :], C["c_E"][:, :], writes=["E_sb"])
            S.dma("pool", fap(mask2[:, 0:1], [[128, 2], [1, 128]]), AP(C["c_mask2"].tensor, 0, [[128, 128], [128 * 128, 2], [1, 128]]), writes=["mask2"])
            S.dma("pool", fap(w_out_sb[:, 0:1], [[D, 8], [1, D]]), AP(I["ab_w_out"].tensor, 0, [[D, 128], [128 * D, 8], [1, D]]), writes=["w_out_sb"])
            S.dma("pool", fap(cmpb_all[:, 0:1], [[256, 32], [1, 256]]), AP(C["c_cmpbias"].tensor, 0, [[256, 128], [128 * 256, 32], [1, 256]]),
                  writes=["cmpb_all"])
            sc_rr = [0]
            pt_rr = [0]

            def zero_bank(b):
                PE(lambda e: e.matmul(pb[b][:, :], lhsT=zeros_b[0:1, 0:128], rhs=zeros_b[0:1, 0:512], start=True, stop=True),
                   r=["zeros_b"], w=[pbn[b]])

            def pv(acc, pt_i, nr, rhs_fn, width, stride):
                for h in range(4):
                    PE(lambda e, h=h: e.matmul(pb[acc][:, h * stride:h * stride + width], lhsT=pT[pt_i][0:nr, h * 128:(h + 1) * 128],
                                               rhs=rhs_fn(), start=False, stop=True, skip_group_check=True),
                       r=["pT%d" % pt_i, "vs_aug", "vw_aug", "vc_aug"], w=[pbn[acc]])

            def combine(acc, c, g, br, first):
                DVE(lambda e: e.tensor_scalar(rz[:], fap(pb[acc][:, 64:65], [[65, 4]]), 1e-30, None, ALU.max), w=["rz", pbn[acc]])
                DVE(lambda e: e.reciprocal(rz[:], rz[:]), w=["rz"])
                DVE(lambda e: e.tensor_tensor(wg[:], fap(gsig[:, c * 24 + g * 12 + br:c * 24 + g * 12 + br + 1], [[3, 4]]), rz[:], ALU.mult),
                    r=["gsig", "rz"], w=["wg"])
                for h in range(4):
                    oh = oacc[:, (g * 4 + h) * 64:(g * 4 + h + 1) * 64]
                    if first:
                        DVE(lambda e, h=h, oh=oh: e.tensor_scalar(oh, pb[acc][:, h * 65:h * 65 + 64], wg[:, h:h + 1], None, ALU.mult),
                            r=["wg"], w=["oacc", pbn[acc]])
                    else:
                        DVE(lambda e, h=h, oh=oh: e.scalar_tensor_tensor(oh, pb[acc][:, h * 65:h * 65 + 64], wg[:, h:h + 1], oh, ALU.mult, ALU.add),
                            r=["wg"], w=["oacc", pbn[acc]])

            for c in range(NT):
                c2 = c % 2
                S.dma("sp", F_sb[c2][:], C["c_F"][c, :, :], writes=["F_sb%d" % c2])
                S.dma("sp", xc[c2][:], I["x"][c * 128:(c + 1) * 128, :], writes=["xc%d" % c2])
                S.dma("sp", fap(aot[c2][:, 0:1], [[128, 4], [1, 128]]), AP(aoT_dram.tensor, c * 128, [[T, 128], [128 * T, 4], [1, 128]]),
                      reads=["aoT_dram"], writes=["aot%d" % c2])
                for g in range(2):
                    gp0, gp1 = g * 64, (g + 1) * 64
                    q_rhs = fap(qT_all[gp0:gp1, c * 128:c * 128 + 1], [[T, 4], [1, 128]])
                    zero_bank(2)
                    zero_bank(3)
                    nch = 1 if c < 16 else 2
                    for kch in range(nch):
                        nr = 128 if kch == 0 else 127
                        bs = sc_rr[0]; sc_rr[0] = 1 - bs
                        pi = pt_rr[0]; pt_rr[0] = (pi + 1) % 3
                        PE(lambda e, kch=kch, nr=nr, bs=bs: e.matmul(pb[bs][0:nr, :], lhsT=kcmpT[gp0:gp1, kch * 128:kch * 128 + nr], rhs=q_rhs,
                                                                      start=True, stop=True), r=["kcmpT", "qT_all"], w=[pbn[bs]])
                        ACT(lambda e, nr=nr, bs=bs, pi=pi: e.activation(pT[pi][0:nr, :], pb[bs][0:nr, :], AF.Exp), w=["pT%d" % pi, pbn[bs]])
                        POOL(lambda e, nr=nr, pi=pi, kch=kch: e.tensor_tensor(fap(pT[pi][0:nr, 0:1], [[128, 4], [1, 128]]),
                                                                              fap(pT[pi][0:nr, 0:1], [[128, 4], [1, 128]]),
                                                                              fap(cmpb_all[0:nr, c * 256 + kch * 128:c * 256 + kch * 128 + 1], [[0, 4], [1, 128]]), ALU.mult),
                             r=["cmpb_all"], w=["pT%d" % pi])
                        pv(2, pi, nr, lambda kch=kch: vc_aug[0:nr, (kch * 2 + g) * 129:(kch * 2 + g) * 129 + 65], 65, 65)
                        pv(3, pi, nr, lambda kch=kch: vc_aug[0:nr, (kch * 2 + g) * 129 + 65:(kch * 2 + g) * 129 + 129], 64, 64)
                    combine(2, c, g, 0, True)
                    for h in range(4):
                        DVE(lambda e, h=h: e.scalar_tensor_tensor(imp[:], pb[3][:, h * 64:(h + 1) * 64], rz[:, h:h + 1],
                                                                  (F_sb[c2][:] if h == 0 else imp[:]), ALU.mult, ALU.add),
                            r=["rz", "F_sb%d" % c2], w=["imp", pbn[3]])
                    DVE(lambda e: e.max(out=m8a[:], in_=imp[:]), r=["imp"], w=["m8a"])
                    DVE(lambda e: e.match_replace(out=imp2[:], in_to_replace=m8a[:], in_values=imp[:], imm_value=-3.0e38), r=["imp", "m8a"], w=["imp2"])
                    DVE(lambda e: e.max(out=m8b[:], in_=imp2[:]), r=["imp2"], w=["m8b"])
                    for dup in range(2):
                        DVE(lambda e, dup=dup: e.tensor_scalar(negb[:, dup * 64:(dup + 1) * 64], imp[:], m8b[:, 7:8], NEG, ALU.is_lt, ALU.mult),
                            r=["imp", "m8b"], w=["negb"])
                    zero_bank(5)
                    for kt in range(max(0, c - 4), c + 1):
                        bs = sc_rr[0]; sc_rr[0] = 1 - bs
                        pi = pt_rr[0]; pt_rr[0] = (pi + 1) % 3
                        PE(lambda e, kt=kt, bs=bs: e.matmul(pb[bs][:, :], lhsT=kwT[gp0:gp1, kt * 128:(kt + 1) * 128], rhs=q_rhs, start=True, stop=True),
                           r=["kwT", "qT_all"], w=[pbn[bs]])
                        ACT(lambda e, bs=bs, pi=pi: e.activation(pT[pi][:], pb[bs][:], AF.Exp), w=["pT%d" % pi, pbn[bs]])
                        if kt == c or kt == c - 4:
                            mo = 0 if kt == c else 128
                            POOL(lambda e, pi=pi, mo=mo: e.tensor_tensor(fap(pT[pi][:, 0:1], [[128, 4], [1, 128]]), fap(pT[pi][:, 0:1], [[128, 4], [1, 128]]),
                                                                         fap(mask2[:, mo:mo + 1], [[0, 4], [1, 128]]), ALU.mult), r=["mask2"], w=["pT%d" % pi])
                        pv(5, pi, 128, lambda kt=kt: vw_aug[:, kt * 130 + g * 65:kt * 130 + g * 65 + 65], 65, 65)
                    PE(lambda e: e.transpose(pb[6][:, 0:128], negb[:], ident_f[:]), r=["negb", "ident_f"], w=[pbn[6]])
                    DVE(lambda e: e.tensor_copy(negT[:], pb[6][:, 0:128]), w=["negT", pbn[6]])
                    zero_bank(4)
                    for kt in range(c + 1):
                        bs = sc_rr[0]; sc_rr[0] = 1 - bs
                        pi = pt_rr[0]; pt_rr[0] = (pi + 1) % 3
                        PE(lambda e, kt=kt, bs=bs: e.matmul(pb[bs][:, :], lhsT=ksT[gp0:gp1, kt * 128:(kt + 1) * 128], rhs=q_rhs, start=True, stop=False),
                           r=["ksT", "qT_all"], w=[pbn[bs]])
                        PE(lambda e, kt=kt, bs=bs: e.matmul(pb[bs][:, :], lhsT=E_sb[gp0:gp1, kt * 128:(kt + 1) * 128],
                                                            rhs=fap(negT[gp0:gp1, 0:1], [[0, 4], [1, 128]]), start=False, stop=True),
                           r=["E_sb", "negT"], w=[pbn[bs]])
                        ACT(lambda e, bs=bs, pi=pi: e.activation(pT[pi][:], pb[bs][:], AF.Exp), w=["pT%d" % pi, pbn[bs]])
                        if kt == c:
                            POOL(lambda e, pi=pi: e.tensor_tensor(fap(pT[pi][:, 0:1], [[128, 4], [1, 128]]), fap(pT[pi][:, 0:1], [[128, 4], [1, 128]]),
                                                                  fap(mask2[:, 0:1], [[0, 4], [1, 128]]), ALU.mult), r=["mask2"], w=["pT%d" % pi])
                        pv(4, pi, 128, lambda kt=kt: vs_aug[:, kt * 130 + g * 65:kt * 130 + g * 65 + 65], 65, 65)
                    combine(4, c, g, 1, False)
                    combine(5, c, g, 2, False)
                if debug_stop == "Cb":
                    S.dma("sp", dbg[c * 128:(c + 1) * 128, 0:512], oacc[:], reads=["oacc"])
                DVE(lambda e: e.tensor_copy(obf[:], oacc[:]), r=["oacc"], w=["obf"])
                transpose_to(obf, "obf", 6, fap(boT[:, 0:1], [[128, 4], [1, 128]]), "boT", nk=4, eng="dve")
                for dh in range(2):
                    bk = 6 + dh
                    for j in range(8):
                        lhs = aot[c2][:, j * 128:(j + 1) * 128] if j < 4 else boT[:, (j - 4) * 128:(j - 3) * 128]
                        PE(lambda e, j=j, dh=dh, bk=bk, lhs=lhs: e.matmul(pb[bk][:, :], lhsT=lhs, rhs=w_out_sb[:, j * D + dh * 512:j * D + (dh + 1) * 512],
                                                                          start=(j == 0), stop=(j == 7)),
                           r=["aot%d" % c2, "boT", "w_out_sb"], w=[pbn[bk]])
                    DVE(lambda e, dh=dh, bk=bk: e.tensor_tensor(hnew[c2][:, dh * 512:(dh + 1) * 512], xc[c2][:, dh * 512:(dh + 1) * 512], pb[bk][:, :], ALU.add),
                        r=["xc%d" % c2], w=["hnew%d" % c2, pbn[bk]])
                S.dma("sp", h_dram[c * 128:(c + 1) * 128, :], hnew[c2][:], reads=["hnew%d" % c2], writes=["h_dram"])
        esL0.close()
        S.barrier()
        if debug_stop in ("C", "E0", "P0"):
            S.dma("sp", dbg[0:T, :], h_dram[:, :], reads=["h_dram"])
        if debug_stop in ("C", "Cb"):
            S.finish("sp")
            return nc
        NG = NG_
        SCRC = ["scrC%d" % k for k in range(16)]
        SCRA = ["scrA_h0", "scrA_h1"]
        SCRB = ["scrB_h0", "scrB_h1"]
        TOPN = ["top%d" % k for k in range(16)]
        ITOPN = ["itop%d" % k for k in range(16)]
        CTOPN = ["ctop%d" % k for k in range(8)]
        CIDXN = ["cidx%d" % k for k in range(8)]
        NCG = 128 // NG

        def peer_phase(layer, with_pool, final):
            with ExitStack() as esE:
                sbt = lambda es, name, shape, dt: es.enter_context(nc.sbuf_tensor("%s_L%d" % (name, layer), shape, dt))
                gffn = sbt(esE, "gffn", [128, D], F32)
                gple = sbt(esE, "gple", [128, D], F32)
                gx = sbt(esE, "gx", [128, D], F32)
                gfin = sbt(esE, "gfin", [128, D], F32)
                skT = sbt(esE, "skT", [128, 8 * 128], BF16)
                sk_st = sbt(esE, "sk_st", [128, 128], F32)
                pproj = sbt(esE, "pproj", [128, 2 * D], BF16)
                iota128 = sbt(esE, "iota128", [128, 128], F32)
                iota16 = sbt(esE, "iota16", [128, 16], F32)
                wbuf = sbt(esE, "wbuf", [128, 8 * 512], BF16)
                G_sb = sbt(esE, "G_sb", [128, 256 * 128], BF16)
                NSL = 3
                ubuf = [sbt(esE, "ubuf%d" % i, [128, 8 * NG * 128], BF16) for i in range(NSL)]
                vbuf = [sbt(esE, "vbuf%d" % i, [128, NG * D], BF16) for i in range(NSL)]
                ht2 = [sbt(esE, "ht_%d" % i, [128, 2 * D], F32) for i in range(2)]
                hnb = sbt(esE, "hnb", [128, 2 * D], BF16)
                hnT2 = [sbt(esE, "hnTe%d" % i, [128, 8 * 256], BF16) for i in range(2)]
                scrA = sbt(esE, "scrA", [128, 2048], F32)
                scrB = sbt(esE, "scrB", [128, 2048], F32)
                scrC = sbt(esE, "scrC", [128, 2048], F32)
                A16, B16, C16 = scrA[:].bitcast(BF16), scrB[:].bitcast(BF16), scrC[:].bitcast(BF16)
                qTs = sbt(esE, "qTs", [128, 8 * 256], BF16)
                rT = sbt(esE, "rT", [128, 3 * 256], F32)
                iota128b = sbt(esE, "iota128b", [128, 128], BF16)
                top = sbt(esE, "top", [128, 256], F32)
                itop = sbt(esE, "itop", [128, 256], U32)
                itopf = sbt(esE, "itopf", [128, 256], F32)
                ctop = sbt(esE, "ctop", [128, 128], F32)
                cidx = sbt(esE, "cidx", [128, 128], U32)
                ai = sbt(esE, "ai", [128, 128], U32)
                af_ = sbt(esE, "af_", [128, 128], F32)
                bf_ = sbt(esE, "bf_", [128, 128], F32)
                gate = sbt(esE, "gate", [128, 128], F32)
                i1f = sbt(esE, "i1f", [128, 128], F32)
                i2f = sbt(esE, "i2f", [128, 128], F32)
                zt = sbt(esE, "zt", [128, 8], F32)
                a_sb = [sbt(esE, "a_sb%d" % i, [128, 256], BF16) for i in range(2)]
                W_sb = [sbt(esE, "W_sb%d" % i, [128, 256], BF16) for i in range(2)]
                ptb = sbt(esE, "ptb", [128, 512], BF16)
                pTs = sbt(esE, "pTs", [128, 512], BF16)
                if with_pool:
                    poolA = sbt(esE, "poolA", [128, 12 * 128], BF16)
                    pw_sb = sbt(esE, "pw_sb", [128, 2048], BF16)
                    hprev = sbt(esE, "hprev", [128, D], BF16)
                    dT_sb = sbt(esE, "dT_sb", [128, 8 * 128], BF16)

                S.dma("sp", gffn[:], I["ffn_norm"][layer, :].partition_broadcast(128), writes=["gffn"])
                S.dma("sp", gple[:], I["ple_norm"][layer, :].partition_broadcast(128), writes=["gple"])
                S.dma("sp", gfin[:], I["final_norm"].partition_broadcast(128), writes=["gfin"])
                S.dma("sp", iota128[:], C["c_iota128"][:, :], writes=["iota128"])
                S.dma("pool", iota128b[:], C["c_iota128"][:, :], writes=["iota128b"])
                S.dma("sp", iota16[:], C["c_iota16"][:, :], writes=["iota16"])
                S.dma("pool", fap(pproj[:, 0:1], [[D, 2], [1, D]]), AP(I["ple_proj"].tensor, layer * 256 * D, [[D, 128], [128 * D, 2], [1, D]]),
                      writes=["pproj"])
                for h in range(8):
                    S.dma("sp", fap(sk_st[:, 0:1], [[64, 2], [1, 64]]),
                          AP(I["peer_subkeys"].tensor, (layer * 8 + h) * 2 * 128 * 64, [[64, 128], [128 * 64, 2], [1, 64]]), writes=["sk_st"])
                    PE(lambda e: e.transpose(pb[7][:, 0:128], sk_st[:], ident_f[:]), r=["sk_st", "ident_f"], w=[pbn[7]])
                    DVE(lambda e, h=h: e.tensor_copy(skT[:, h * 128:(h + 1) * 128], pb[7][:, 0:128]), w=["skT", pbn[7]])
                if with_pool:
                    S.dma("sp", gx[:], I["mix_norm"][1, :].partition_broadcast(128), writes=["gx"])
                    S.dma("pool", fap(poolA[:, 0:1], [[128, 12], [1, 128]]), AP(C["c_poolA"].tensor, 0, [[128, 128], [128 * 128, 12], [1, 128]]),
                          writes=["poolA"])
                    S.dma("sp", scrB[:, 0:D], I["pool_scale"].partition_broadcast(128), writes=[SCRB])
                    for gi in range(4):
                        S.dma("sp", fap(scrA[:, gi * 512:gi * 512 + 1], [[256, 2], [1, 256]]),
                              AP(I["pool_w"].tensor, gi * 256 * 256, [[256, 128], [128 * 256, 2], [1, 256]]), writes=[SCRA])
                    DVE(lambda e: e.tensor_tensor(fap(pw_sb[:, 0:1], [[512, 4], [256, 2], [1, 256]]), fap(scrA[:, 0:1], [[512, 4], [256, 2], [1, 256]]),
                                                  fap(scrB[:, 0:1], [[256, 4], [0, 2], [1, 256]]), ALU.mult), r=[SCRA, SCRB], w=["pw_sb"])

                seq = [(blk, cg) for blk in range(16) for cg in range(NCG)]

                def issue_uv(idx):
                    if idx >= len(seq):
                        return
                    blk_, cg_ = seq[idx]
                    sl = idx % NSL
                    S.dma("sp", ubuf[sl][:, :],
                          AP(u16.tensor, layer * D * 16384 + cg_ * 128 * 8 * NG * 128, [[8 * NG * 128, 128], [1, 8 * NG * 128]]),
                          reads=["u16_%d" % layer], writes=["ubuf%d" % sl])
                    S.dma("sp", vbuf[sl][:, :],
                          AP(v16.tensor, layer * 16384 * D + cg_ * 128 * NG * D, [[NG * D, 128], [1, NG * D]]),
                          reads=["v16_%d" % layer], writes=["vbuf%d" % sl])

                for i_ in range(NSL):
                    issue_uv(i_)

                def HT(blk, tt):
                    par = blk % 2
                    return ht2[par][:, tt * D:(tt + 1) * D], "ht%d_%d" % (par, tt)

                def front(blk):
                    tok0 = blk * 256
                    par = blk % 2
                    hnT = hnT2[par]
                    hnTn = "hnTe%d" % par
                    for tt in range(2):
                        hv, hn_ = HT(blk, tt)
                        S.dma("sp", hv, h_dram[tok0 + tt * 128:tok0 + (tt + 1) * 128, :], reads=["h_dram"], writes=[hn_])
                    yield
                    if with_pool:
                        for tt in range(2):
                            hv, hn_ = HT(blk, tt)
                            rmsnorm_tile(hv, hn_, gx[:], "gx", hnb[:, tt * D:(tt + 1) * D], "hnb%d" % tt, tt)
                            yield
                        for tt in range(2):
                            hv, hn_ = HT(blk, tt)
                            gt = blk * 2 + tt
                            cur = hnb[:, tt * D:(tt + 1) * D]
                            prv = hprev[:] if tt == 0 else hnb[:, 0:D]
                            prvn = "hprev" if tt == 0 else "hnb0"
                            for k in range(8):
                                wi = k // 2
                                bk = 6 + k // 4
                                kind = 2 if gt == 0 else 0
                                PE(lambda e, k=k, wi=wi, bk=bk, kind=kind, cur=cur: e.matmul(
                                    pb[bk][:, (k % 4) * 128:(k % 4 + 1) * 128], lhsT=cur[:, k * 128:(k + 1) * 128],
                                    rhs=poolA[:, (wi * 3 + kind) * 128:(wi * 3 + kind + 1) * 128], start=True, stop=(gt == 0)),
                                   r=["hnb%d" % tt, "poolA"], w=[pbn[bk]])
                                if gt > 0:
                                    PE(lambda e, k=k, wi=wi, bk=bk, prv=prv: e.matmul(
                                        pb[bk][:, (k % 4) * 128:(k % 4 + 1) * 128], lhsT=prv[:, k * 128:(k + 1) * 128],
                                        rhs=poolA[:, (wi * 3 + 1) * 128:(wi * 3 + 2) * 128], start=False, stop=True),
                                       r=[prvn, "poolA"], w=[pbn[bk]])
                                if k % 4 == 3:
                                    ACT(lambda e, bk=bk: e.copy(dT_sb[:, (bk - 6) * 512:(bk - 5) * 512], pb[bk][:, :]), w=["dT_sb", pbn[bk]])
                                    yield
                            for gi in range(4):
                                bk = 6 + gi // 2
                                for k2 in range(2):
                                    PE(lambda e, gi=gi, k2=k2, bk=bk: e.matmul(pb[bk][:, (gi % 2) * 256:(gi % 2 + 1) * 256],
                                                                              lhsT=dT_sb[:, (2 * gi + k2) * 128:(2 * gi + k2 + 1) * 128],
                                                                              rhs=pw_sb[:, (gi * 2 + k2) * 256:(gi * 2 + k2 + 1) * 256],
                                                                              start=(k2 == 0), stop=(k2 == 1)),
                                       r=["dT_sb", "pw_sb"], w=[pbn[bk]])
                            yield
                            for dh in range(2):
                                hvv = hv[:, dh * 512:(dh + 1) * 512]
                                DVE(lambda e, dh=dh, hvv=hvv: e.tensor_tensor(hvv, hvv, pb[6 + dh][:, :], ALU.add), w=[hn_, pbn[6 + dh]])
                            yield
                        DVE(lambda e: e.tensor_copy(hprev[:], hnb[:, D:2 * D]), r=["hnb1"], w=["hprev"])
                        yield
                    for tt in range(2):
                        hv, hn_ = HT(blk, tt)
                        rmsnorm_tile(hv, hn_, gffn[:], "gffn", hnb[:, tt * D:(tt + 1) * D], "hnb%d" % tt, tt)
                        yield
                        transpose_to(hnb[:, tt * D:(tt + 1) * D], "hnb%d" % tt, 6 + tt, fap(hnT[:, tt * 128:tt * 128 + 1], [[256, 8], [1, 128]]), hnTn)
                        yield
                    for h in range(8):
                        bk = 6 + h % 2
                        if h % 4 == 0:
                            S.dma("sp", fap(wbuf[:, 0:1], [[512, 8], [1, 512]]),
                                  AP(wq16.tensor, layer * D * D + (h // 4) * 512, [[D, 128], [128 * D, 8], [1, 512]]),
                                  reads=["wq16_%d" % layer], writes=["wbuf"])
                            yield
                        for kc in range(8):
                            PE(lambda e, h=h, kc=kc, bk=bk: e.matmul(pb[bk][:, 0:256], lhsT=wbuf[:, kc * 512 + (h % 4) * 128:kc * 512 + (h % 4 + 1) * 128],
                                                                     rhs=hnT[:, kc * 256:(kc + 1) * 256], start=(kc == 0), stop=(kc == 7)),
                               r=["wbuf", hnTn], w=[pbn[bk]])
                        ACT(lambda e, h=h, bk=bk: e.copy(qTs[:, h * 256:(h + 1) * 256], pb[bk][:, 0:256]), w=["qTs", pbn[bk]])
                        yield
                    for tt in range(2):
                        for r_ in range(2):
                            for hl in range(4):
                                h = r_ * 4 + hl
                                for s_ in range(2):
                                    PE(lambda e, h=h, hl=hl, s_=s_: e.matmul(pb[6 + s_][:, hl * 128:(hl + 1) * 128],
                                                                             lhsT=qTs[s_ * 64:(s_ + 1) * 64, h * 256 + tt * 128:h * 256 + (tt + 1) * 128],
                                                                             rhs=skT[s_ * 64:(s_ + 1) * 64, h * 128:(h + 1) * 128], start=True, stop=True),
                                       r=["qTs", "skT"], w=[pbn[6 + s_]])
                            for s_ in range(2):
                                ACT(lambda e, s_=s_, r_=r_: e.copy(fap(scrA[:, r_ * 1024 + s_ * 128:r_ * 1024 + s_ * 128 + 1], [[256, 4], [1, 128]]),
                                                                   fap(pb[6 + s_][:, 0:1], [[128, 4], [1, 128]])), w=[SCRA, pbn[6 + s_]])
                            yield
                        def ch(hs):
                            return (scrA[:, hs * 128:(hs + 1) * 128], top[:, hs * 16:hs * 16 + 8], top[:, hs * 16 + 8:hs * 16 + 16],
                                    itop[:, hs * 16:hs * 16 + 8], itop[:, hs * 16 + 8:hs * 16 + 16], scrC[:, hs * 128:(hs + 1) * 128])
                        for hs in range(16):
                            sv, t8a, t8b, i8a, i8b, sc = ch(hs)
                            DVE(lambda e: e.max(out=t8a, in_=sv), r=[SCRA], w=["top%d" % hs])
                            if hs % 4 == 3:
                                yield
                        for hs in range(16):
                            sv, t8a, t8b, i8a, i8b, sc = ch(hs)
                            DVE(lambda e: e.max_index(out=i8a, in_max=t8a, in_values=sv), r=[SCRA, "top%d" % hs], w=["itop%d" % hs])
                            if hs % 4 == 3:
                                yield
                        for hs in range(16):
                            sv, t8a, t8b, i8a, i8b, sc = ch(hs)
                            DVE(lambda e: e.match_replace(out=sc, in_to_replace=t8a, in_values=sv, imm_value=-3.0e38),
                                r=[SCRA, "top%d" % hs], w=["scrC%d" % hs])
                            if hs % 4 == 3:
                                yield
                        for hs in range(16):
                            sv, t8a, t8b, i8a, i8b, sc = ch(hs)
                            DVE(lambda e: e.max(out=t8b, in_=sc), r=["scrC%d" % hs], w=["top%d" % hs])
                            if hs % 4 == 3:
                                yield
                        for hs in range(16):
                            sv, t8a, t8b, i8a, i8b, sc = ch(hs)
                            DVE(lambda e: e.max_index(out=i8b, in_max=t8b, in_values=sc), r=["scrC%d" % hs, "top%d" % hs], w=["itop%d" % hs])
                            if hs % 4 == 3:
                                yield
                        DVE(lambda e: e.tensor_copy(itopf[:], itop[:]), r=ITOPN, w=["itopf"])
                        DVE(lambda e: e.tensor_tensor(fap(scrB[:, 0:1], [[256, 8], [16, 16], [1, 16]]), fap(top[:, 0:1], [[32, 8], [1, 16], [0, 16]]),
                                                      fap(top[:, 16:17], [[32, 8], [0, 16], [1, 16]]), ALU.add), r=TOPN, w=[SCRB])
                        yield

                        def cch(h):
                            return (scrB[:, h * 256:(h + 1) * 256], ctop[:, h * 16:h * 16 + 8], ctop[:, h * 16 + 8:h * 16 + 16],
                                    cidx[:, h * 16:h * 16 + 8], cidx[:, h * 16 + 8:h * 16 + 16], scrC[:, h * 256:(h + 1) * 256],
                                    ["scrC%d" % (2 * h), "scrC%d" % (2 * h + 1)])
                        for h in range(8):
                            cv, c8a, c8b, j8a, j8b, sc, scn = cch(h)
                            DVE(lambda e: e.max(out=c8a, in_=cv), r=[SCRB], w=["ctop%d" % h])
                            if h % 4 == 3:
                                yield
                        for h in range(8):
                            cv, c8a, c8b, j8a, j8b, sc, scn = cch(h)
                            DVE(lambda e: e.max_index(out=j8a, in_max=c8a, in_values=cv), r=[SCRB, "ctop%d" % h], w=["cidx%d" % h])
                            if h % 4 == 3:
                                yield
                        for h in range(8):
                            cv, c8a, c8b, j8a, j8b, sc, scn = cch(h)
                            DVE(lambda e: e.match_replace(out=sc, in_to_replace=c8a, in_values=cv, imm_value=-3.0e38),
                                r=[SCRB, "ctop%d" % h], w=scn)
                            if h % 4 == 3:
                                yield
                        for h in range(8):
                            cv, c8a, c8b, j8a, j8b, sc, scn = cch(h)
                            DVE(lambda e: e.max(out=c8b, in_=sc), r=scn, w=["ctop%d" % h])
                            if h % 4 == 3:
                                yield
                        for h in range(8):
                            cv, c8a, c8b, j8a, j8b, sc, scn = cch(h)
                            DVE(lambda e: e.max_index(out=j8b, in_max=c8b, in_values=sc), r=scn + ["ctop%d" % h], w=["cidx%d" % h])
                            if h % 4 == 3:
                                yield
                        DVE(lambda e: e.tensor_tensor(fap(gate[:, 0:1], [[16, 8], [1, 16]]), fap(ctop[:, 0:1], [[16, 8], [1, 16]]),
                                                      fap(ctop[:, 0:1], [[16, 8], [0, 16]]), ALU.subtract), r=CTOPN, w=["gate"])
                        yield
                        ACT(lambda e: e.activation(gate[:], gate[:], AF.Exp), w=["gate"])
                        yield
                        DVE(lambda e: e.tensor_reduce(zt[:, 0:8], fap(gate[:, 0:1], [[16, 8], [1, 16]]), AX.X, ALU.add), r=["gate"], w=["zt"])
                        DVE(lambda e: e.reciprocal(zt[:, 0:8], zt[:, 0:8]), w=["zt"])
                        DVE(lambda e: e.tensor_tensor(fap(gate[:, 0:1], [[16, 8], [1, 16]]), fap(gate[:, 0:1], [[16, 8], [1, 16]]),
                                                      fap(zt[:, 0:1], [[1, 8], [0, 16]]), ALU.mult), r=["zt"], w=["gate"])
                        yield
                        DVE(lambda e: e.tensor_single_scalar(ai[:], cidx[:], 4, ALU.logical_shift_right), r=CIDXN, w=["ai"])
                        DVE(lambda e: e.tensor_copy(af_[:], ai[:]), r=["ai"], w=["af_"])
                        DVE(lambda e: e.tensor_single_scalar(ai[:], cidx[:], 15, ALU.bitwise_and), r=CIDXN, w=["ai"])
                        DVE(lambda e: e.tensor_copy(bf_[:], ai[:]), r=["ai"], w=["bf_"])
                        yield
                        for (xf, off, dst, dn) in ((af_, 0, i1f, "i1f"), (bf_, 16, i2f, "i2f")):
                            DVE(lambda e, xf=xf: e.tensor_tensor(fap(scrC[:, 0:1], [[16, 128], [1, 16]]), fap(xf[:, 0:1], [[1, 128], [0, 16]]),
                                                                 fap(iota16[:, 0:1], [[0, 128], [1, 16]]), ALU.is_equal),
                                r=["af_", "bf_", "iota16"], w=SCRC)
                            yield
                            DVE(lambda e, off=off: e.tensor_tensor(fap(scrC[:, 0:1], [[256, 8], [16, 16], [1, 16]]), fap(scrC[:, 0:1], [[256, 8], [16, 16], [1, 16]]),
                                                                   fap(itopf[:, off:off + 1], [[32, 8], [0, 16], [1, 16]]), ALU.mult),
                                r=["itopf"], w=SCRC)
                            yield
                            DVE(lambda e, dst=dst: e.tensor_reduce(dst[:], fap(scrC[:, 0:1], [[16, 128], [1, 16]]), AX.X, ALU.add), r=SCRC, w=[dn])
                            yield
                        for k3, (src_, sn) in enumerate(((i1f, "i1f"), (i2f, "i2f"), (gate, "gate"))):
                            PE(lambda e, src_=src_: e.transpose(pb[7][:, 0:128], src_[:], ident_f[:]), r=[sn, "ident_f"], w=[pbn[7]])
                            DVE(lambda e, k3=k3: e.tensor_copy(rT[:, k3 * 256 + tt * 128:k3 * 256 + (tt + 1) * 128], pb[7][:, 0:128]), w=["rT", pbn[7]])
                            yield

                def gbuild(blk):
                    for tt in range(2):
                        for sb in range(8):
                            hb = sb % 2
                            tl0 = tt * 128 + sb * 16
                            o16 = hb * 2048
                            vv = [[128, 16], [1, 128]]
                            for t_ in range(16):
                                DVE(lambda e, t_=t_: e.tensor_scalar(B16[:, o16 + t_ * 128:o16 + (t_ + 1) * 128], iota128b[:],
                                                                     rT[:, tl0 + t_:tl0 + t_ + 1], rT[:, 512 + tl0 + t_:512 + tl0 + t_ + 1],
                                                                     ALU.is_equal, ALU.mult), r=["iota128b", "rT"], w=[SCRB[hb]])
                            DVE(lambda e: e.tensor_tensor(fap(C16[:, o16:o16 + 1], vv), fap(iota128[:, 0:1], [[0, 16], [1, 128]]),
                                                          fap(rT[:, 256 + tl0:256 + tl0 + 1], [[1, 16], [0, 128]]), ALU.is_equal),
                                r=["iota128", "rT"], w=SCRC[hb * 8:(hb + 1) * 8])
                            for t4 in range(4):
                                bk = 4 + (sb * 4 + t4) % 2
                                for tq in range(4):
                                    tl_ = t4 * 4 + tq
                                    PE(lambda e, tl_=tl_, tq=tq, bk=bk: e.matmul(pb[bk][:, tq * 128:(tq + 1) * 128],
                                                                                 lhsT=B16[:, o16 + tl_ * 128:o16 + (tl_ + 1) * 128],
                                                                                 rhs=C16[:, o16 + tl_ * 128:o16 + (tl_ + 1) * 128], start=True, stop=True),
                                       r=[SCRB[hb]] + SCRC[hb * 8:(hb + 1) * 8], w=[pbn[bk]])
                                ACT(lambda e, t4=t4, bk=bk: e.copy(G_sb[:, (tl0 + t4 * 4) * 128:(tl0 + t4 * 4 + 4) * 128], pb[bk][:, :]),
                                    w=["G_sb", pbn[bk]])

                def dense(blk, gen):
                    par = blk % 2
                    hnT = hnT2[par]
                    hnTn = "hnTe%d" % par
                    tok0 = blk * 256
                    junk32 = junk[:].bitcast(F32)
                    S.dma("sp", fap(junk32[:, 0:1], [[256, 2], [1, 256]]),
                          AP(I["p"].tensor, layer * T * 256 + tok0 * 256, [[256, 128], [128 * 256, 2], [1, 256]]), writes=["junk"])
                    DVE(lambda e: e.tensor_copy(ptb[:], junk32), r=["junk"], w=["ptb"])

                    def act_part(i2):
                        cg, ci = divmod(i2, NG)
                        idx = blk * NCG + cg
                        sl = idx % NSL
                        ba = 4 + i2 % 2
                        for kc in range(8):
                            PE(lambda e, kc=kc: e.matmul(pb[ba][:, 0:256], lhsT=ubuf[sl][:, kc * NG * 128 + ci * 128:kc * NG * 128 + (ci + 1) * 128],
                                                         rhs=hnT[:, kc * 256:(kc + 1) * 256], start=(kc == 0), stop=(kc == 7)),
                               r=["ubuf%d" % sl, hnTn], w=[pbn[ba]])
                        ACT(lambda e: e.activation(a_sb[i2 % 2][:], pb[ba][:, 0:256], AF.Gelu_apprx_tanh), w=["a_sb%d" % (i2 % 2), pbn[ba]])
                        DVE(lambda e: e.tensor_tensor(W_sb[i2 % 2][:], a_sb[i2 % 2][:], fap(G_sb[:, i2:i2 + 1], [[128, 256]]), ALU.mult),
                            r=["a_sb%d" % (i2 % 2), "G_sb"], w=["W_sb%d" % (i2 % 2)])

                    def out_part(i2):
                        cg, ci = divmod(i2, NG)
                        sl = (blk * NCG + cg) % NSL
                        for tt in range(2):
                            for dh in range(2):
                                PE(lambda e, tt=tt, dh=dh: e.matmul(pb[tt * 2 + dh][:, :], lhsT=W_sb[i2 % 2][:, tt * 128:(tt + 1) * 128],
                                                                    rhs=vbuf[sl][:, ci * D + dh * 512:ci * D + (dh + 1) * 512],
                                                                    start=(i2 == 0), stop=(i2 == 127)),
                                   r=["W_sb%d" % (i2 % 2), "vbuf%d" % sl], w=[pbn[tt * 2 + dh]])
                        if ci == NG - 1:
                            issue_uv(blk * NCG + cg + NSL)

                    act_part(0)
                    for i2 in range(128):
                        if i2 + 1 < 128:
                            act_part(i2 + 1)
                        out_part(i2)
                        if gen is not None and i2 >= 2:
                            next(gen, None)
                    if gen is not None:
                        for _ in gen:
                            pass

                def tail(blk):
                    par = blk % 2
                    hnT = hnT2[par]
                    hnTn = "hnTe%d" % par
                    tok0 = blk * 256
                    for tt in range(2):
                        hv, hn_ = HT(blk, tt)
                        for dh in range(2):
                            hvv = hv[:, dh * 512:(dh + 1) * 512]
                            DVE(lambda e, tt=tt, dh=dh, hvv=hvv: e.tensor_tensor(hvv, hvv, pb[tt * 2 + dh][:, :], ALU.add), w=[hn_, pbn[tt * 2 + dh]])
                    if debug_stop in ("P%d" % layer, "E%d" % layer):
                        for tt in range(2):
                            hv, hn_ = HT(blk, tt)
                            S.dma("sp", dbg[T + tok0 + tt * 128:T + tok0 + (tt + 1) * 128, :], hv, reads=[hn_])
                    for tt in range(2):
                        hv, hn_ = HT(blk, tt)
                        transpose_to(ptb[:, tt * 256:(tt + 1) * 256], "ptb", 6, fap(pTs[:, tt * 128:tt * 128 + 1], [[256, 2], [1, 128]]), "pTs", nk=2, eng="dve")
                        rmsnorm_tile(hv, hn_, gple[:], "gple", hnb[:, tt * D:(tt + 1) * D], "hnb%d" % tt, tt)
                        transpose_to(hnb[:, tt * D:(tt + 1) * D], "hnb%d" % tt, 7, fap(hnT[:, tt * 128:tt * 128 + 1], [[256, 8], [1, 128]]), hnTn)
                    for dh in range(2):
                        S.dma("sp", fap(wbuf[:, 0:1], [[512, 8], [1, 512]]),
                              AP(gw16.tensor, layer * D * D + dh * 512, [[D, 128], [128 * D, 8], [1, 512]]),
                              reads=["gw16_%d" % layer], writes=["wbuf"])
                        for tt in range(2):
                            hv, hn_ = HT(blk, tt)
                            bg, bp = (tt * 2 + dh) % 4, 4 + (tt * 2 + dh) % 2
                            for kc in range(8):
                                PE(lambda e, kc=kc, tt=tt, dh=dh, bg=bg: e.matmul(pb[bg][:, :], lhsT=hnT[:, kc * 256 + tt * 128:kc * 256 + (tt + 1) * 128],
                                                                                  rhs=wbuf[:, kc * 512:(kc + 1) * 512],
                                                                                  start=(kc == 0), stop=(kc == 7)), r=[hnTn, "wbuf"], w=[pbn[bg]])
                            for kc in range(2):
                                PE(lambda e, kc=kc, tt=tt, dh=dh, bp=bp: e.matmul(pb[bp][:, :], lhsT=pTs[:, kc * 256 + tt * 128:kc * 256 + (tt + 1) * 128],
                                                                                  rhs=pproj[:, kc * D + dh * 512:kc * D + (dh + 1) * 512],
                                                                                  start=(kc == 0), stop=(kc == 1)), r=["pTs", "pproj"], w=[pbn[bp]])
                            ACT(lambda e, bg=bg: e.activation(scrA[:, 0:512], pb[bg][:, :], AF.Sigmoid), w=[SCRA, pbn[bg]])
                            DVE(lambda e, bp=bp: e.tensor_tensor(scrB[:, 0:512], pb[bp][:, :], scrA[:, 0:512], ALU.mult), r=[SCRA], w=[SCRB, pbn[bp]])
                            hvv = hv[:, dh * 512:(dh + 1) * 512]
                            DVE(lambda e, hvv=hvv: e.tensor_tensor(hvv, hvv, scrB[:, 0:512], ALU.add), r=[SCRB], w=[hn_])
                    for tt in range(2):
                        hv, hn_ = HT(blk, tt)
                        rows = slice(tok0 + tt * 128, tok0 + (tt + 1) * 128)
                        if final:
                            rmsnorm_tile(hv, hn_, gfin[:], "gfin", scrC[:, 0:D], SCRC, tt)
                            S.dma("sp", out[rows, :], scrC[:, 0:D], reads=SCRC)
                        else:
                            S.dma("sp", h_dram[rows, :], hv, reads=[hn_], writes=["h_dram"])
                        if debug_stop == "E%d" % layer:
                            S.dma("sp", dbg[2 * T + tok0 + tt * 128:2 * T + tok0 + (tt + 1) * 128, :],
                                  (scrC[:, 0:D] if final else hv), reads=SCRC + [hn_])

                for _ in front(0):
                    pass
                gbuild(0)
                for blk in range(16):
                    gen = front(blk + 1) if blk + 1 < 16 else None
                    dense(blk, gen)
                    tail(blk)
                    if blk + 1 < 16:
                        gbuild(blk + 1)
            S.barrier()

        peer_phase(0, False, False)
        if debug_stop in ("P0", "E0"):
            S.finish("sp")
            return nc
        peer_phase(1, True, True)
        S.finish("sp")
        return nc


def _host_inputs(inputs, b):
    f = lambda a: np.ascontiguousarray(np.asarray(a, dtype=np.float32))
    m = {}
    m["x"] = f(inputs["x"][b])
    m["p"] = f(inputs["p"][:, b])
    m["mix_norm"] = f(inputs["mix_norm"])
    m["ab_w_in"] = f(inputs["ab_w_in"][0])
    m["ab_conv_w"] = f(inputs["ab_conv_w"][0]).reshape(124, 128)
    m["ab_conv_b"] = f(inputs["ab_conv_b"][0]).reshape(4, 128)
    m["ab_conv_ln_g"] = f(inputs["ab_conv_ln_g"][0]).reshape(4, 128)
    m["ab_conv_ln_b"] = f(inputs["ab_conv_ln_b"][0]).reshape(4, 128)
    m["ab_cmp_pos"] = f(inputs["ab_cmp_pos"][0])
    m["ab_cmp_w1"] = f(inputs["ab_cmp_w1"][0])
    m["ab_cmp_w2"] = f(inputs["ab_cmp_w2"][0])
    m["ab_w_out"] = f(inputs["ab_w_out"][0])
    m["pool_w"] = f(inputs["pool_w"][0])
    m["pool_scale"] = f(inputs["pool_scale"][0])
    for k in ("ffn_norm", "peer_wq", "peer_subkeys", "ple_norm", "ple_gate_w", "ple_proj", "final_norm"):
        m[k] = f(inputs[k])
    return m


def _run(inputs, debug_stop=None, ncores=8):
    shared = _host_inputs(inputs, 0)
    u = np.asarray(inputs["peer_u"], dtype=np.float32).reshape(2, 128, 128, D)
    u6 = u.reshape(2, 128, 128 // NG_, NG_, 8, 128)
    shared["uP"] = np.ascontiguousarray(u6.transpose(0, 2, 5, 4, 3, 1)).reshape(2, D, 16384)
    v = np.asarray(inputs["peer_v"], dtype=np.float32).reshape(2, 128, 128 // NG_, NG_, D)
    shared["vP"] = np.ascontiguousarray(v.transpose(0, 2, 1, 3, 4)).reshape(2, 16384, D)
    shared.update(make_consts())
    in_maps = []
    for b in range(ncores):
        m = dict(shared)
        m["x"] = np.ascontiguousarray(np.asarray(inputs["x"][b], dtype=np.float32))
        m["p"] = np.ascontiguousarray(np.asarray(inputs["p"][:, b], dtype=np.float32))
        in_maps.append(m)
    nc = build_program(debug_stop)
    res = run_bass_kernel_spmd(nc, in_maps, core_ids=list(range(ncores)))
    return res


def kernel(**inputs):
    res = _run(inputs, None)
    return np.stack([np.asarray(r["out"], dtype=np.float32) for r in res.results], axis=0)
```

```python
import numpy as np
from contextlib import ExitStack
import concourse.bass as bass
import concourse.mybir as mybir
from concourse.bass_utils import run_bass_kernel_spmd
from concourse.ap import AP

F32 = mybir.dt.float32
BF16 = mybir.dt.bfloat16
U32 = mybir.dt.uint32
ALU = mybir.AluOpType
AF = mybir.ActivationFunctionType
AX = mybir.AxisListType

T = 4096
D = 1024
NT = 32
NEG = -30000.0
EPS = 1e-6
INC = 2328
NG_ = 2
DEBUG_STOP = None


class Sched:
    def __init__(self, nc, n_dma_sems=12):
        self.nc = nc
        self.eng = {"pe": nc.tensor, "dve": nc.vector, "act": nc.scalar, "pool": nc.gpsimd, "sp": nc.sync}
        self.sem = {k: nc.alloc_semaphore("sem_" + k) for k in ("pe", "dve", "act", "pool")}
        self.cnt = {k: 0 for k in self.sem}
        self.waited = {}
        self.dsem = [nc.alloc_semaphore("dsem%d" % i) for i in range(n_dma_sems)]
        self.ssem = []
        self.dcnt = [0] * n_dma_sems
        self.dnext = 0
        self.last_w = {}
        self.readers = {}
        self.ninst = 0

    def _need(self, engine, tok):
        if tok is None:
            return
        kind, idx, val = tok
        if kind == "c" and idx == "pe" and engine == "pe":
            return
        key = (engine, kind, idx)
        if self.waited.get(key, 0) >= val:
            return
        self.waited[key] = val
        sem = self.sem[idx] if kind == "c" else (self.dsem[idx] if kind == "d" else self.ssem[idx])
        self.eng[engine].wait_ge(sem, val)
        self.ninst += 1

    @staticmethod
    def _flat(xs):
        o = []
        for x in xs:
            if isinstance(x, (list, tuple)):
                o.extend(Sched._flat(x))
            else:
                o.append(x)
        return o

    def _deps(self, engine, reads, writes):
        for r in reads:
            self._need(engine, self.last_w.get(r))
        for w in writes:
            self._need(engine, self.last_w.get(w))
            for t in self.readers.get(w, ()):
                self._need(engine, t)

    def _record(self, tok, reads, writes):
        for r in reads:
            self.readers.setdefault(r, []).append(tok)
        for w in writes:
            self.last_w[w] = tok
            self.readers[w] = []

    def op(self, engine, fn, reads=(), writes=()):
        reads, writes = self._flat(reads), self._flat(writes)
        self._deps(engine, reads, writes)
        inst = fn(self.eng[engine])
        self.cnt[engine] += 1
        inst.then_inc(self.sem[engine], 1)
        tok = ("c", engine, self.cnt[engine])
        self._record(tok, reads, writes)
        self.ninst += 1
        return tok

    def dma(self, queue, out, in_, reads=(), writes=(), **kw):
        reads, writes = self._flat(reads), self._flat(writes)
        self._deps(queue, reads, writes)
        if queue == "pool":
            sem = self.nc.alloc_semaphore("ssem%d" % len(self.ssem))
            self.ssem.append(sem)
            self.eng[queue].dma_start(out=out, in_=in_, **kw).then_inc(sem, 16)
            tok = ("s", len(self.ssem) - 1, 16)
            self._record(tok, reads, writes)
            self.ninst += 1
            return tok
        i = self.dnext
        self.dnext = (self.dnext + 1) % len(self.dsem)
        if self.dcnt[i] > 0:
            self._need(queue, ("d", i, self.dcnt[i]))
        self.dcnt[i] += 16
        self.eng[queue].dma_start(out=out, in_=in_, **kw).then_inc(self.dsem[i], 16)
        tok = ("d", i, self.dcnt[i])
        self._record(tok, reads, writes)
        self.ninst += 1
        return tok

    def barrier(self):
        for e in ("pe", "dve", "act", "pool", "sp"):
            self.finish(e)

    def finish(self, engine="sp"):
        for i in range(len(self.ssem)):
            self._need(engine, ("s", i, 16))
        for i, c in enumerate(self.dcnt):
            if c:
                self._need(engine, ("d", i, c))
        for k, c in self.cnt.items():
            if c:
                self._need(engine, ("c", k, c))


def fap(ap, dims, off=0):
    return AP(ap.tensor, ap.offset + off, [list(ap.ap[0])] + [list(d) for d in dims])


def make_consts():
    c = {}
    n = np.arange(256)[:, None]
    s = np.arange(64)[None, :]
    ov = ((16 * n < 64 * s + 64) & (16 * n + 32 > 64 * s) & (n < 255)).astype(np.float32)
    c["c_overlap"] = ov
    cc = np.arange(32)[:, None, None, None]
    nl = np.arange(128)[None, :, None, None]
    kch = np.arange(2)[None, None, :, None]
    tl = np.arange(128)[None, None, None, :]
    nn = kch * 128 + nl
    vis = (16 * nn + 31 <= 128 * cc + tl) & (nn < 255)
    c["c_cmpbias"] = np.where(vis, 1.0, 0.0).astype(np.float32)
    cc = np.arange(32)[:, None, None]
    tl = np.arange(128)[None, :, None]
    blk = np.arange(64)[None, None, :]
    t = 128 * cc + tl
    cur = t // 64
    F = np.zeros((32, 128, 64), np.float32)
    F = np.where((blk == 0) | (blk == cur) | (blk == cur - 1), 1e9, F)
    F = np.where(blk > cur, -1e30, F)
    c["c_F"] = F.astype(np.float32)
    key = np.arange(4096)[None, :]
    c["c_E"] = (key // 64 == np.arange(64)[:, None]).astype(np.float32)
    kl = np.arange(128)[:, None]
    tl = np.arange(128)[None, :]
    c["c_mask2"] = np.stack([np.where(kl <= tl, 1.0, 0.0), np.where(kl > tl, 1.0, 0.0)]).astype(np.float32)
    A = np.zeros((4, 3, 128, 128), np.float32)
    tp = np.arange(128)[:, None]
    tt = np.arange(128)[None, :]
    for wi, w in enumerate((2, 4, 8, 16)):
        A[wi, 0] = np.where((tp <= tt) & (tp >= tt - w + 1), 1.0 / w, 0.0) - (tp == tt)
        A[wi, 1] = np.where(tp - 128 >= tt - w + 1, 1.0 / w, 0.0)
        cnt = np.minimum(w, tt + 1)
        A[wi, 2] = np.where((tp <= tt) & (tp >= tt - w + 1), 1.0 / cnt, 0.0) - (tp == tt)
    c["c_poolA"] = A
    c["c_iota128"] = np.tile(np.arange(128, dtype=np.float32)[None, :], (128, 1))
    c["c_iota16"] = np.tile(np.arange(16, dtype=np.float32)[None, :], (128, 1))
    c["c_ident"] = np.eye(128, dtype=np.float32)
    return c


CONST_SHAPES = {"c_overlap": [256, 64], "c_cmpbias": [32, 128, 2, 128], "c_F": [32, 128, 64], "c_E": [64, 4096],
                "c_mask2": [2, 128, 128], "c_poolA": [4, 3, 128, 128], "c_iota128": [128, 128],
                "c_iota16": [128, 16], "c_ident": [128, 128]}

IN_SHAPES = {
    "x": [T, D], "p": [2, T, 256], "mix_norm": [2, D], "ab_w_in": [D, INC], "ab_conv_w": [124, 128],
    "ab_conv_b": [4, 128], "ab_conv_ln_g": [4, 128], "ab_conv_ln_b": [4, 128], "ab_cmp_pos": [2, 32, 64],
    "ab_cmp_w1": [2, 2048, 256], "ab_cmp_w2": [2, 256, 64], "ab_w_out": [D, D], "pool_w": [4, 256, 256],
    "pool_scale": [D], "ffn_norm": [2, D], "peer_wq": [2, D, D], "peer_subkeys": [2, 8, 2, 128, 64],
    "uP": [2, D, 16384], "vP": [2, 16384, D], "ple_norm": [2, D], "ple_gate_w": [2, D, D],
    "ple_proj": [2, 256, D], "final_norm": [D],
}


def build_program(debug_stop=None):
    nc = bass.Bass("TRN2", target_bir_lowering=False)
    I = {k: nc.dram_tensor(k, v, F32, kind="ExternalInput").ap() for k, v in IN_SHAPES.items()}
    C = {k: nc.dram_tensor(k, v, F32, kind="ExternalInput").ap() for k, v in CONST_SHAPES.items()}
    out = nc.dram_tensor("out", [T, D], F32, kind="ExternalOutput").ap()
    dbg = None
    if debug_stop is not None:
        dbg = nc.dram_tensor("dbg", [3 * T, D], F32, kind="ExternalOutput").ap()
    h_dram = nc.dram_tensor("h_dram", [T, D], F32).ap()
    aoT_dram = nc.dram_tensor("aoT_dram", [512, T], BF16).ap()
    u16 = nc.dram_tensor("u16", [2, D, 16384], BF16).ap()
    v16 = nc.dram_tensor("v16", [2, 16384, D], BF16).ap()
    wq16 = nc.dram_tensor("wq16", [2, D, D], BF16).ap()
    gw16 = nc.dram_tensor("gw16", [2, D, D], BF16).ap()

    S = Sched(nc)
    PE = lambda fn, r=(), w=(): S.op("pe", fn, r, w)
    DVE = lambda fn, r=(), w=(): S.op("dve", fn, r, w)
    ACT = lambda fn, r=(), w=(): S.op("act", fn, r, w)
    POOL = lambda fn, r=(), w=(): S.op("pool", fn, r, w)

    with ExitStack() as es0:
        def sbt(es, name, shape, dt):
            return es.enter_context(nc.sbuf_tensor(name, shape, dt))
        pb = [es0.enter_context(nc.psum_tensor("pb%d" % i, [128, 512], F32)) for i in range(8)]
        pbn = ["pb%d" % i for i in range(8)]
        pbT = [pb[i][:].bitcast(BF16) for i in range(8)]

        ident_f = sbt(es0, "ident_f", [128, 128], F32)
        ident_b = sbt(es0, "ident_b", [128, 128], BF16)
        ones_f = sbt(es0, "ones_f", [128, 128], F32)
        zeros_b = sbt(es0, "zeros_b", [128, 512], BF16)
        junk = sbt(es0, "junk", [128, 1024], BF16)
        ss = sbt(es0, "ss", [128, 8], F32)
        rs = sbt(es0, "rs", [128, 8], F32)
        S.dma("sp", ident_f[:], C["c_ident"][:, :], writes=["ident_f"])
        S.dma("pool", ident_b[:], C["c_ident"][:, :], writes=["ident_b"])
        POOL(lambda e: e.memset(ones_f[:], 1.0), w=["ones_f"])
        POOL(lambda e: e.memset(zeros_b[:], 0.0), w=["zeros_b"])

        def rmsnorm_tile(x_ap, xres, g_ap, gres, out_ap, outres, col):
            ACT(lambda e: e.activation(junk[:], x_ap, AF.Square, accum_out=ss[:, col:col + 1]), r=[xres], w=["junk", "ss%d" % col])
            DVE(lambda e: e.tensor_scalar(rs[:, col:col + 1], ss[:, col:col + 1], 1.0 / D, EPS, ALU.mult, ALU.add),
                r=["ss%d" % col], w=["rs%d" % col])
            ACT(lambda e: e.activation(rs[:, col:col + 1], rs[:, col:col + 1], AF.Sqrt), w=["rs%d" % col])
            DVE(lambda e: e.reciprocal(rs[:, col:col + 1], rs[:, col:col + 1]), w=["rs%d" % col])
            DVE(lambda e: e.scalar_tensor_tensor(out_ap, x_ap, rs[:, col:col + 1], g_ap, ALU.mult, ALU.mult),
                r=[xres, "rs%d" % col, gres], w=(outres if isinstance(outres, list) else [outres]))

        def transpose_to(hn_ap, hnres, bank, dst_ap, dstres, nk=8, eng="act"):
            for kc in range(nk):
                PE(lambda e, kc=kc: e.transpose(pbT[bank][:, kc * 128:(kc + 1) * 128], hn_ap[:, kc * 128:(kc + 1) * 128], ident_b[:]),
                   r=[hnres, "ident_b"], w=[pbn[bank]])
            src = fap(pbT[bank][:, 0:1], [[128, nk], [1, 128]])
            if eng == "act":
                ACT(lambda e: e.copy(dst_ap, src), w=[dstres, pbn[bank]])
            else:
                DVE(lambda e: e.tensor_copy(dst_ap, src), w=[dstres, pbn[bank]])

        esL0 = es0.enter_context(ExitStack())
        qT_all = sbt(esL0, "qT_all", [128, 4 * T], BF16)
        ksT = sbt(esL0, "ksT", [128, T], BF16)
        kwT = sbt(esL0, "kwT", [128, T], BF16)
        vs_aug = sbt(esL0, "vs_aug", [128, NT * 130], BF16)
        vw_aug = sbt(esL0, "vw_aug", [128, NT * 130], BF16)
        gsig = sbt(esL0, "gsig", [128, NT * 24], F32)
        kcmpT = sbt(esL0, "kcmpT", [128, 256], BF16)
        vc_aug = sbt(esL0, "vc_aug", [128, 2 * 2 * 129], BF16)
        POOL(lambda e: e.memset(vs_aug[:], 1.0), w=["vs_aug"])
        POOL(lambda e: e.memset(vw_aug[:], 1.0), w=["vw_aug"])
        POOL(lambda e: e.memset(vc_aug[:], 1.0), w=["vc_aug"])

        esAB = es0.enter_context(ExitStack())
        kcT = sbt(esAB, "kcT", [128, T], BF16)
        vcT = sbt(esAB, "vcT", [128, T], BF16)

        with ExitStack() as esA:
            w_in_sb = sbt(esA, "w_in_sb", [128, 8 * INC], BF16)
            wqr = sbt(esA, "wqr", [128, 8 * 512], BF16)
            g0 = sbt(esA, "g0", [128, D], F32)
            cw = sbt(esA, "cw", [128, 124], F32)
            cp = sbt(esA, "cp", [128, 12], F32)
            stg = sbt(esA, "stg", [128, 128], F32)
            xt = [sbt(esA, "xt%d" % i, [128, D], F32) for i in range(2)]
            hn = [sbt(esA, "hn%d" % i, [128, D], BF16) for i in range(2)]
            hnT = sbt(esA, "hnT", [128, 8 * 512], BF16)
            a_pad = sbt(esA, "a_pad", [128, 4 * 542], F32)
            y = sbt(esA, "y", [128, 4 * 512], F32)
            ysq = [sbt(esA, "ysq%d" % i, [128, 512], F32) for i in range(2)]
            sig = [sbt(esA, "sig%d" % i, [128, 512], F32) for i in range(2)]
            mean_sb = sbt(esA, "mean_sb", [128, 512], F32)
            msq = sbt(esA, "msq", [128, 512], F32)
            rstd = sbt(esA, "rstd", [128, 512], F32)
            ao = sbt(esA, "ao", [128, 4 * 512], BF16)

            S.dma("pool", fap(w_in_sb[:, 0:1], [[INC, 8], [1, INC]]), AP(I["ab_w_in"].tensor, 0, [[INC, 128], [128 * INC, 8], [1, INC]]),
                  writes=["w_in_sb"])
            for g_ in range(2):
                for h_ in range(4):
                    src = AP(I["ab_w_in"].tensor, 1024 + (g_ * 4 + h_) * 64, [[INC, 128], [128 * INC, 8], [1, 64]])
                    dst = fap(wqr[:, h_ * 128 + g_ * 64:h_ * 128 + g_ * 64 + 1], [[512, 8], [1, 64]])
                    S.dma("pool", dst, src, writes=["wqr"])
            S.dma("sp", g0[:], I["mix_norm"][0, :].partition_broadcast(128), writes=["g0"])
            S.dma("sp", stg[0:124, :], I["ab_conv_w"][:, :], writes=["stg"])
            PE(lambda e: e.transpose(pb[7][:, 0:124], stg[0:124, :], ident_f[0:124, 0:124]), r=["stg", "ident_f"], w=[pbn[7]])
            DVE(lambda e: e.tensor_copy(cw[:], pb[7][:, 0:124]), w=["cw", pbn[7]])
            S.dma("sp", stg[0:4, :], I["ab_conv_b"][:, :], writes=["stg"], reads=[])
            S.dma("sp", stg[4:8, :], I["ab_conv_ln_g"][:, :], writes=["stg"])
            S.dma("sp", stg[8:12, :], I["ab_conv_ln_b"][:, :], writes=["stg"])
            PE(lambda e: e.transpose(pb[7][:, 0:12], stg[0:12, :], ident_f[0:12, 0:12]), r=["stg", "ident_f"], w=[pbn[7]])
            DVE(lambda e: e.tensor_copy(cp[:], pb[7][:, 0:12]), w=["cp", pbn[7]])
            for j in range(4):
                DVE(lambda e, j=j: e.memset(a_pad[:, j * 542:j * 542 + 30], 0.0), w=["a_pad%d" % j])

            for l in range(2):
                for r in range(2):
                    S.dma("pool", u16[l, r * 512:(r + 1) * 512, :], I["uP"][l, r * 512:(r + 1) * 512, :], writes=["u16_%d" % l])
                for r in range(2):
                    S.dma("pool", v16[l, r * 8192:(r + 1) * 8192, :], I["vP"][l, r * 8192:(r + 1) * 8192, :], writes=["v16_%d" % l])
                S.dma("pool", wq16[l, :, :], I["peer_wq"][l, :, :], writes=["wq16_%d" % l])
                S.dma("pool", gw16[l, :, :], I["ple_gate_w"][l, :, :], writes=["gw16_%d" % l])
            bank_rr = [0]

            def nb():
                b = bank_rr[0]
                bank_rr[0] = (b + 1) % 6
                return b

            for st in range(8):
                t0 = st * 512
                for j in range(4):
                    tile = st * 4 + j
                    b2 = j % 2
                    S.dma("sp", xt[b2][:], I["x"][tile * 128:(tile + 1) * 128, :], writes=["xt%d" % b2])
                    rmsnorm_tile(xt[b2][:], "xt%d" % b2, g0[:], "g0", hn[b2][:], "hn%d" % b2, b2)
                    transpose_to(hn[b2], "hn%d" % b2, 6 + b2, fap(hnT[:, j * 128:j * 128 + 1], [[512, 8], [1, 128]]), "hnT")

                def proj_T(lhs_tile, col_fn, bank):
                    for kc in range(8):
                        PE(lambda e, kc=kc: e.matmul(pb[bank][:], lhsT=col_fn(kc), rhs=hnT[:, kc * 512:(kc + 1) * 512],
                                                     start=(kc == 0), stop=(kc == 7)),
                           r=[lhs_tile, "hnT"], w=[pbn[bank]])

                for j in range(4):
                    bv, bg = nb(), nb()
                    proj_T("w_in_sb", lambda kc, j=j: w_in_sb[:, kc * INC + j * 128:kc * INC + (j + 1) * 128], bv)
                    proj_T("w_in_sb", lambda kc, j=j: w_in_sb[:, kc * INC + 512 + j * 128:kc * INC + 512 + (j + 1) * 128], bg)
                    ACT(lambda e, j=j, bg=bg: e.activation(sig[j % 2][:], pb[bg][:], AF.Sigmoid), w=["sig%d" % (j % 2), pbn[bg]])
                    DVE(lambda e, j=j, bv=bv: e.tensor_tensor(a_pad[:, j * 542 + 30:j * 542 + 542], pb[bv][:], sig[j % 2][:], ALU.mult),
                        r=["sig%d" % (j % 2)], w=["a_pad%d" % j, pbn[bv]])
                for h in range(4):
                    b = nb()
                    proj_T("wqr", lambda kc, h=h: wqr[:, kc * 512 + h * 128:kc * 512 + (h + 1) * 128], b)
                    ACT(lambda e, h=h, b=b: e.mul(qT_all[:, h * T + t0:h * T + t0 + 512], pb[b][:], 0.125), w=["qT_all", pbn[b]])
                for (tl_, tn, col0) in ((kcT, "kcT", 1536), (vcT, "vcT", 1664), (ksT, "ksT", 1792), (kwT, "kwT", 2048)):
                    b = nb()
                    proj_T("w_in_sb", lambda kc, col0=col0: w_in_sb[:, kc * INC + col0:kc * INC + col0 + 128], b)
                    ACT(lambda e, tl_=tl_, b=b: e.copy(tl_[:, t0:t0 + 512], pb[b][:]), w=[tn, pbn[b]])
                for j in range(4):
                    tile = st * 4 + j
                    b = nb()
                    for kc in range(8):
                        PE(lambda e, kc=kc, j=j, b=b: e.matmul(pb[b][:, 0:128], lhsT=hnT[:, kc * 512 + j * 128:kc * 512 + (j + 1) * 128],
                                                               rhs=w_in_sb[:, kc * INC + 1920:kc * INC + 2048], start=(kc == 0), stop=(kc == 7)),
                           r=["hnT", "w_in_sb"], w=[pbn[b]])
                    for kc in range(8):
                        PE(lambda e, kc=kc, j=j, b=b: e.matmul(pb[b][:, 128:280], lhsT=hnT[:, kc * 512 + j * 128:kc * 512 + (j + 1) * 128],
                                                               rhs=w_in_sb[:, kc * INC + 2176:kc * INC + 2328], start=(kc == 0), stop=(kc == 7)),
                           r=["hnT", "w_in_sb"], w=[pbn[b]])
                    ACT(lambda e, b=b, tile=tile: e.copy(fap(vs_aug[:, tile * 130:tile * 130 + 1], [[65, 2], [1, 64]]),
                                                         fap(pb[b][:, 0:1], [[64, 2], [1, 64]])), w=["vs_aug", pbn[b]])
                    ACT(lambda e, b=b, tile=tile: e.copy(fap(vw_aug[:, tile * 130:tile * 130 + 1], [[65, 2], [1, 64]]),
                                                         fap(pb[b][:, 128:129], [[64, 2], [1, 64]])), w=["vw_aug", pbn[b]])
                    ACT(lambda e, b=b, tile=tile: e.activation(gsig[:, tile * 24:(tile + 1) * 24], pb[b][:, 256:280], AF.Sigmoid),
                        w=["gsig", pbn[b]])
                for j in range(4):
                    engn = "dve"
                    yj = y[:, j * 512:(j + 1) * 512]
                    S.op(engn, lambda e, j=j, yj=yj: e.tensor_scalar(yj, a_pad[:, j * 542:j * 542 + 512], cw[:, j:j + 1], cp[:, j:j + 1],
                                                                     ALU.mult, ALU.add), ["a_pad%d" % j, "cw", "cp"], ["y%d" % j])
                    for k in range(1, 31):
                        S.op(engn, lambda e, j=j, k=k, yj=yj: e.scalar_tensor_tensor(yj, a_pad[:, j * 542 + k:j * 542 + k + 512],
                                                                                      cw[:, k * 4 + j:k * 4 + j + 1], yj, ALU.mult, ALU.add),
                             ["a_pad%d" % j, "cw"], ["y%d" % j])
                    S.op(engn, lambda e, j=j: e.tensor_copy(a_pad[:, j * 542:j * 542 + 30], a_pad[:, j * 542 + 512:j * 542 + 542]),
                         [], ["a_pad%d" % j])
                b1, b2_ = nb(), nb()
                for j in range(4):
                    PE(lambda e, j=j: e.matmul(pb[b1][:], lhsT=ones_f[:], rhs=y[:, j * 512:(j + 1) * 512], start=(j == 0), stop=(j == 3)),
                       r=["ones_f", "y%d" % j], w=[pbn[b1]])
                for j in range(4):
                    ACT(lambda e, j=j: e.activation(ysq[j % 2][:], y[:, j * 512:(j + 1) * 512], AF.Square), r=["y%d" % j], w=["ysq%d" % (j % 2)])
                    PE(lambda e, j=j: e.matmul(pb[b2_][:], lhsT=ones_f[:], rhs=ysq[j % 2][:], start=(j == 0), stop=(j == 3)),
                       r=["ones_f", "ysq%d" % (j % 2)], w=[pbn[b2_]])
                DVE(lambda e: e.tensor_scalar(mean_sb[:], pb[b1][:], 1.0 / 512, None, ALU.mult), w=["mean_sb", pbn[b1]])
                DVE(lambda e: e.tensor_tensor(msq[:], mean_sb[:], mean_sb[:], ALU.mult), r=["mean_sb"], w=["msq"])
                DVE(lambda e: e.scalar_tensor_tensor(rstd[:], pb[b2_][:], 1.0 / 512, msq[:], ALU.mult, ALU.subtract), r=["msq"], w=["rstd", pbn[b2_]])
                DVE(lambda e: e.tensor_scalar(rstd[:], rstd[:], EPS, None, ALU.add), r=[], w=["rstd"])
                ACT(lambda e: e.activation(rstd[:], rstd[:], AF.Sqrt), w=["rstd"])
                DVE(lambda e: e.reciprocal(rstd[:], rstd[:]), w=["rstd"])
                for j in range(4):
                    yj = y[:, j * 512:(j + 1) * 512]
                    DVE(lambda e, yj=yj: e.tensor_tensor(yj, yj, mean_sb[:], ALU.subtract), r=["mean_sb"], w=["y%d" % j])
                    DVE(lambda e, yj=yj: e.tensor_tensor(yj, yj, rstd[:], ALU.mult), r=["rstd"], w=["y%d" % j])
                    ACT(lambda e, j=j, yj=yj: e.activation(ao[:, j * 512:(j + 1) * 512], yj, AF.Silu, bias=cp[:, 8 + j:9 + j], scale=cp[:, 4 + j:5 + j]),
                        r=["y%d" % j, "cp"], w=["ao%d" % j])
                    S.dma("sp", aoT_dram[j * 128:(j + 1) * 128, t0:t0 + 512], ao[:, j * 512:(j + 1) * 512], reads=["ao%d" % j], writes=["aoT_dram"])
        S.barrier()
        if debug_stop == "A":
            S.dma("pool", AP(dbg.tensor, 0, [[4096, 512], [1, 4096]]), aoT_dram[:, :], reads=["aoT_dram"])
            S.finish("sp")
            return nc
        with ExitStack() as esB:
            w1_sb = sbt(esB, "w1_sb", [128, 32 * 256], BF16)
            w2_sb = sbt(esB, "w2_sb", [128, 128], BF16)
            w2pad = sbt(esB, "w2pad", [128, 4 * 128], BF16)
            posr = sbt(esB, "posr", [32, 128], F32)
            posT = sbt(esB, "posT", [128, 32], BF16)
            lo = sbt(esB, "lo", [128, T], BF16)
            hi = sbt(esB, "hi", [128, T], BF16)
            hid = [sbt(esB, "hid%d" % g, [128, 512], BF16) for g in range(2)]
            for kch in range(2):
                for g in range(2):
                    S.dma("pool", vc_aug[:, (kch * 2 + g) * 129 + 65:(kch * 2 + g) * 129 + 129],
                          C["c_overlap"][kch * 128:(kch + 1) * 128, :], writes=["vc_aug"])
            for src_i, (srcT, srcname) in enumerate(((kcT, "kcT"), (vcT, "vcT"))):
                for dup in range(2):
                    src_ap = AP(I["ab_cmp_w1"].tensor, src_i * 2048 * 256, [[256, 64], [64 * 256, 32], [1, 256]])
                    S.dma("pool", fap(w1_sb[dup * 64:(dup + 1) * 64, 0:1], [[256, 32], [1, 256]]), src_ap, writes=["w1_sb"])
                S.dma("pool", fap(w2_sb[:, 0:1], [[64, 2], [1, 64]]),
                      AP(I["ab_cmp_w2"].tensor, src_i * 256 * 64, [[64, 128], [128 * 64, 2], [1, 64]]), writes=["w2_sb"])
                for dup in range(2):
                    S.dma("sp", posr[:, dup * 64:(dup + 1) * 64], I["ab_cmp_pos"][src_i, :, :], writes=["posr"])
                PE(lambda e: e.transpose(pb[7][:, 0:32], posr[0:32, :], ident_f[0:32, 0:32]), r=["posr", "ident_f"], w=[pbn[7]])
                DVE(lambda e: e.tensor_copy(posT[:], pb[7][:, 0:32]), w=["posT", pbn[7]])
                DVE(lambda e, srcT=srcT: e.tensor_tensor(fap(lo[:, 0:1], [[16, 256], [1, 16]]), fap(srcT[:, 0:1], [[16, 256], [1, 16]]),
                                                         fap(posT[:, 0:1], [[0, 256], [1, 16]]), ALU.add), r=[srcname, "posT"], w=["lo"])
                DVE(lambda e, srcT=srcT: e.tensor_tensor(fap(hi[:, 0:1], [[16, 256], [1, 16]]), fap(srcT[:, 0:1], [[16, 256], [1, 16]]),
                                                         fap(posT[:, 16:17], [[0, 256], [1, 16]]), ALU.add), r=[srcname, "posT"], w=["hi"])
                for g in range(2):
                    for jc in range(2):
                        bk = g * 2 + jc
                        for l in range(32):
                            src_t = lo if l < 16 else hi
                            PE(lambda e, g=g, jc=jc, l=l, bk=bk, src_t=src_t: e.matmul(
                                pb[bk][:, 0:255], lhsT=w1_sb[g * 64:(g + 1) * 64, l * 256 + jc * 128:l * 256 + (jc + 1) * 128],
                                rhs=fap(src_t[g * 64:(g + 1) * 64, l:l + 1], [[16, 255]]), start=(l == 0), stop=(l == 31)),
                               r=["w1_sb", "lo", "hi"], w=[pbn[bk]])
                        ACT(lambda e, g=g, jc=jc, bk=bk: e.activation(hid[g][:, jc * 256:jc * 256 + 255], pb[bk][:, 0:255], AF.Gelu_apprx_tanh),
                            w=["hid%d" % g, pbn[bk]])
                if src_i == 0:
                    DVE(lambda e: e.memset(w2pad[:], 0.0), w=["w2pad"])
                    for g in range(2):
                        for jc in range(2):
                            DVE(lambda e, g=g, jc=jc: e.tensor_copy(w2pad[:, (g * 2 + jc) * 128 + g * 64:(g * 2 + jc) * 128 + g * 64 + 64],
                                                                    w2_sb[:, jc * 64:(jc + 1) * 64]), r=["w2_sb"], w=["w2pad"])
                    n_ = 0
                    for g in range(2):
                        for jc in range(2):
                            PE(lambda e, g=g, jc=jc, n_=n_: e.matmul(pb[4][:, 0:255], lhsT=w2pad[:, (g * 2 + jc) * 128:(g * 2 + jc + 1) * 128],
                                                                      rhs=hid[g][:, jc * 256:jc * 256 + 255], start=(n_ == 0), stop=(n_ == 3)),
                               r=["w2pad", "hid%d" % g], w=[pbn[4]])
                            n_ += 1
                    DVE(lambda e: e.tensor_copy(kcmpT[:, 0:255], pb[4][:, 0:255]), w=["kcmpT", pbn[4]])
                    DVE(lambda e: e.memset(kcmpT[:, 255:256], 0.0), w=["kcmpT"])
                else:
                    for g in range(2):
                        for kch in range(2):
                            nr = 128 if kch == 0 else 127
                            bk = 4 + (g * 2 + kch) % 2
                            for jc in range(2):
                                PE(lambda e, g=g, kch=kch, jc=jc, nr=nr, bk=bk: e.matmul(
                                    pb[bk][0:nr, 0:64], lhsT=hid[g][:, jc * 256 + kch * 128:jc * 256 + kch * 128 + nr],
                                    rhs=w2_sb[:, jc * 64:(jc + 1) * 64], start=(jc == 0), stop=(jc == 1)),
                                   r=["hid%d" % g, "w2_sb"], w=[pbn[bk]])
                            DVE(lambda e, g=g, kch=kch, nr=nr, bk=bk: e.tensor_copy(vc_aug[0:nr, (kch * 2 + g) * 129:(kch * 2 + g) * 129 + 64],
                                                                                   pb[bk][0:nr, 0:64]), w=["vc_aug", pbn[bk]])
        esAB.close()
        S.barrier()
        if debug_stop == "B":
            dtmp = es0.enter_context(nc.sbuf_tensor("dtmp", [128, 256 + 516], F32))
            DVE(lambda e: e.tensor_copy(dtmp[:, 0:256], kcmpT[:]), r=["kcmpT"], w=["dtmp"])
            DVE(lambda e: e.tensor_copy(dtmp[:, 256:772], vc_aug[:]), r=["vc_aug"], w=["dtmp"])
            S.dma("sp", AP(dbg.tensor, 0, [[256, 128], [1, 256]]), dtmp[:, 0:256], reads=["dtmp"])
            S.dma("sp", AP(dbg.tensor, 128 * 256, [[516, 128], [1, 516]]), dtmp[:, 256:772], reads=["dtmp"])
            S.finish("sp")
            return nc

        with ExitStack() as esC:
            E_sb = sbt(esC, "E_sb", [128, T], BF16)
            mask2 = sbt(esC, "mask2", [128, 256], BF16)
            w_out_sb = sbt(esC, "w_out_sb", [128, 8 * D], BF16)
            cmpb_all = sbt(esC, "cmpb_all", [128, 32 * 256], BF16)
            F_sb = [sbt(esC, "F_sb%d" % i, [128, 64], F32) for i in range(2)]
            xc = [sbt(esC, "xc%d" % i, [128, D], F32) for i in range(2)]
            aot = [sbt(esC, "aot%d" % i, [128, 512], BF16) for i in range(2)]
            pT = [sbt(esC, "pT%d" % i, [128, 512], BF16) for i in range(3)]
            rz = sbt(esC, "rz", [128, 4], F32)
            wg = sbt(esC, "wg", [128, 4], F32)
            imp = sbt(esC, "imp", [128, 64], F32)
            imp2 = sbt(esC, "imp2", [128, 64], F32)
            m8a = sbt(esC, "m8a", [128, 8], F32)
            m8b = sbt(esC, "m8b", [128, 8], F32)
            negb = sbt(esC, "negb", [128, 128], F32)
            negT = sbt(esC, "negT", [128, 128], BF16)
            oacc = sbt(esC, "oacc", [128, 512], F32)
            obf = sbt(esC, "obf", [128, 512], BF16)
            boT = sbt(esC, "boT", [128, 512], BF16)
            hnew = [sbt(esC, "hnew%d" % i, [128, D], F32) for i in range(2)]
            for dup in range(2):
                S.dma("pool", E_sb[dup * 64:(dup + 1) * 64, :], C["c_E"][:, :], writes=["E_sb"])
            S.dma("pool", fap(mask2[:, 0:1], [[128, 2], [1, 128]]), AP(C["c_mask2"].tensor, 0, [[128, 128], [128 * 128, 2], [1, 128]]), writes=["mask2"])
            S.dma("pool", fap(w_out_sb[:, 0:1], [[D, 8], [1, D]]), AP(I["ab_w_out"].tensor, 0, [[D, 128], [128 * D, 8], [1, D]]), writes=["w_out_sb"])
            S.dma("pool", fap(cmpb_all[:, 0:1], [[256, 32], [1, 256]]), AP(C["c_cmpbias"].tensor, 0, [[256, 128], [128 * 256, 32], [1, 256]]),
                  writes=["cmpb_all"])
            sc_rr = [0]
            pt_rr = [0]

            def zero_bank(b):
                PE(lambda e: e.matmul(pb[b][:, :], lhsT=zeros_b[0:1, 0:128], rhs=zeros_b[0:1, 0:512], start=True, stop=True),
                   r=["zeros_b"], w=[pbn[b]])

            def pv(acc, pt_i, nr, rhs_fn, width, stride):
                for h in range(4):
                    PE(lambda e, h=h: e.matmul(pb[acc][:, h * stride:h * stride + width], lhsT=pT[pt_i][0:nr, h * 128:(h + 1) * 128],
                                               rhs=rhs_fn(), start=False, stop=True, skip_group_check=True),
                       r=["pT%d" % pt_i, "vs_aug", "vw_aug", "vc_aug"], w=[pbn[acc]])

            def combine(acc, c, g, br, first):
                DVE(lambda e: e.tensor_scalar(rz[:], fap(pb[acc][:, 64:65], [[65, 4]]), 1e-30, None, ALU.max), w=["rz", pbn[acc]])
                DVE(lambda e: e.reciprocal(rz[:], rz[:]), w=["rz"])
                DVE(lambda e: e.tensor_tensor(wg[:], fap(gsig[:, c * 24 + g * 12 + br:c * 24 + g * 12 + br + 1], [[3, 4]]), rz[:], ALU.mult),
                    r=["gsig", "rz"], w=["wg"])
                for h in range(4):
                    oh = oacc[:, (g * 4 + h) * 64:(g * 4 + h + 1) * 64]
                    if first:
                        DVE(lambda e, h=h, oh=oh: e.tensor_scalar(oh, pb[acc][:, h * 65:h * 65 + 64], wg[:, h:h + 1], None, ALU.mult),
                            r=["wg"], w=["oacc", pbn[acc]])
                    else:
                        DVE(lambda e, h=h, oh=oh: e.scalar_tensor_tensor(oh, pb[acc][:, h * 65:h * 65 + 64], wg[:, h:h + 1], oh, ALU.mult, ALU.add),
                            r=["wg"], w=["oacc", pbn[acc]])

            for c in range(NT):
                c2 = c % 2
                S.dma("sp", F_sb[c2][:], C["c_F"][c, :, :], writes=["F_sb%d" % c2])
                S.dma("sp", xc[c2][:], I["x"][c * 128:(c + 1) * 128, :], writes=["xc%d" % c2])
                S.dma("sp", fap(aot[c2][:, 0:1], [[128, 4], [1, 128]]), AP(aoT_dram.tensor, c * 128, [[T, 128], [128 * T, 4], [1, 128]]),
                      reads=["aoT_dram"], writes=["aot%d" % c2])
                for g in range(2):
                    gp0, gp1 = g * 64, (g + 1) * 64
                    q_rhs = fap(qT_all[gp0:gp1, c * 128:c * 128 + 1], [[T, 4], [1, 128]])
                    zero_bank(2)
                    zero_bank(3)
                    nch = 1 if c < 16 else 2
                    for kch in range(nch):
                        nr = 128 if kch == 0 else 127
                        bs = sc_rr[0]; sc_rr[0] = 1 - bs
                        pi = pt_rr[0]; pt_rr[0] = (pi + 1) % 3
                        PE(lambda e, kch=kch, nr=nr, bs=bs: e.matmul(pb[bs][0:nr, :], lhsT=kcmpT[gp0:gp1, kch * 128:kch * 128 + nr], rhs=q_rhs,
                                                                      start=True, stop=True), r=["kcmpT", "qT_all"], w=[pbn[bs]])
                        ACT(lambda e, nr=nr, bs=bs, pi=pi: e.activation(pT[pi][0:nr, :], pb[bs][0:nr, :], AF.Exp), w=["pT%d" % pi, pbn[bs]])
                        POOL(lambda e, nr=nr, pi=pi, kch=kch: e.tensor_tensor(fap(pT[pi][0:nr, 0:1], [[128, 4], [1, 128]]),
                                                                              fap(pT[pi][0:nr, 0:1], [[128, 4], [1, 128]]),
                                                                              fap(cmpb_all[0:nr, c * 256 + kch * 128:c * 256 + kch * 128 + 1], [[0, 4], [1, 128]]), ALU.mult),
                             r=["cmpb_all"], w=["pT%d" % pi])
                        pv(2, pi, nr, lambda kch=kch: vc_aug[0:nr, (kch * 2 + g) * 129:(kch * 2 + g) * 129 + 65], 65, 65)
                        pv(3, pi, nr, lambda kch=kch: vc_aug[0:nr, (kch * 2 + g) * 129 + 65:(kch * 2 + g) * 129 + 129], 64, 64)
                    combine(2, c, g, 0, True)
                    for h in range(4):
                        DVE(lambda e, h=h: e.scalar_tensor_tensor(imp[:], pb[3][:, h * 64:(h + 1) * 64], rz[:, h:h + 1],
                                                                  (F_sb[c2][:] if h == 0 else imp[:]), ALU.mult, ALU.add),
                            r=["rz", "F_sb%d" % c2], w=["imp", pbn[3]])
                    DVE(lambda e: e.max(out=m8a[:], in_=imp[:]), r=["imp"], w=["m8a"])
                    DVE(lambda e: e.match_replace(out=imp2[:], in_to_replace=m8a[:], in_values=imp[:], imm_value=-3.0e38), r=["imp", "m8a"], w=["imp2"])
                    DVE(lambda e: e.max(out=m8b[:], in_=imp2[:]), r=["imp2"], w=["m8b"])
                    for dup in range(2):
                        DVE(lambda e, dup=dup: e.tensor_scalar(negb[:, dup * 64:(dup + 1) * 64], imp[:], m8b[:, 7:8], NEG, ALU.is_lt, ALU.mult),
                            r=["imp", "m8b"], w=["negb"])
                    zero_bank(5)
                    wslot = {}

                    def win_score(kt):
                        bs = sc_rr[0]; sc_rr[0] = 1 - bs
                        pi = pt_rr[0]; pt_rr[0] = (pi + 1) % 3
                        wslot[kt] = pi
                        PE(lambda e: e.matmul(pb[bs][:, :], lhsT=kwT[gp0:gp1, kt * 128:(kt + 1) * 128], rhs=q_rhs, start=True, stop=True),
                           r=["kwT", "qT_all"], w=[pbn[bs]])
                        ACT(lambda e: e.activation(pT[pi][:], pb[bs][:], AF.Exp), w=["pT%d" % pi, pbn[bs]])
                        if kt == c or kt == c - 4:
                            mo = 0 if kt == c else 128
                            POOL(lambda e: e.tensor_tensor(fap(pT[pi][:, 0:1], [[128, 4], [1, 128]]), fap(pT[pi][:, 0:1], [[128, 4], [1, 128]]),
                                                           fap(mask2[:, mo:mo + 1], [[0, 4], [1, 128]]), ALU.mult), r=["mask2"], w=["pT%d" % pi])

                    wts = list(range(max(0, c - 4), c + 1))
                    win_score(wts[0])
                    for wi_, kt in enumerate(wts):
                        if wi_ + 1 < len(wts):
                            win_score(wts[wi_ + 1])
                        pv(5, wslot[kt], 128, lambda kt=kt: vw_aug[:, kt * 130 + g * 65:kt * 130 + g * 65 + 65], 65, 65)
                    PE(lambda e: e.transpose(pb[6][:, 0:128], negb[:], ident_f[:]), r=["negb", "ident_f"], w=[pbn[6]])
                    DVE(lambda e: e.tensor_copy(negT[:], pb[6][:, 0:128]), w=["negT", pbn[6]])
                    zero_bank(4)
                    slot = {}

                    def slc_score(kt):
                        bs = sc_rr[0]; sc_rr[0] = 1 - bs
                        pi = pt_rr[0]; pt_rr[0] = (pi + 1) % 3
                        slot[kt] = pi
                        PE(lambda e: e.matmul(pb[bs][:, :], lhsT=ksT[gp0:gp1, kt * 128:(kt + 1) * 128], rhs=q_rhs, start=True, stop=False),
                           r=["ksT", "qT_all"], w=[pbn[bs]])
                        PE(lambda e: e.matmul(pb[bs][:, :], lhsT=E_sb[gp0:gp1, kt * 128:(kt + 1) * 128],
                                              rhs=fap(negT[gp0:gp1, 0:1], [[0, 4], [1, 128]]), start=False, stop=True),
                           r=["E_sb", "negT"], w=[pbn[bs]])
                        ACT(lambda e: e.activation(pT[pi][:], pb[bs][:], AF.Exp), w=["pT%d" % pi, pbn[bs]])
                        if kt == c:
                            POOL(lambda e: e.tensor_tensor(fap(pT[pi][:, 0:1], [[128, 4], [1, 128]]), fap(pT[pi][:, 0:1], [[128, 4], [1, 128]]),
                                                           fap(mask2[:, 0:1], [[0, 4], [1, 128]]), ALU.mult), r=["mask2"], w=["pT%d" % pi])

                    slc_score(0)
                    for kt in range(c + 1):
                        if kt + 1 <= c:
                            slc_score(kt + 1)
                        pv(4, slot[kt], 128, lambda kt=kt: vs_aug[:, kt * 130 + g * 65:kt * 130 + g * 65 + 65], 65, 65)
                    combine(4, c, g, 1, False)
                    combine(5, c, g, 2, False)
                if debug_stop == "Cb":
                    S.dma("sp", dbg[c * 128:(c + 1) * 128, 0:512], oacc[:], reads=["oacc"])
                DVE(lambda e: e.tensor_copy(obf[:], oacc[:]), r=["oacc"], w=["obf"])
                transpose_to(obf, "obf", 6, fap(boT[:, 0:1], [[128, 4], [1, 128]]), "boT", nk=4, eng="dve")
                for dh in range(2):
                    bk = 6 + dh
                    for j in range(8):
                        lhs = aot[c2][:, j * 128:(j + 1) * 128] if j < 4 else boT[:, (j - 4) * 128:(j - 3) * 128]
                        PE(lambda e, j=j, dh=dh, bk=bk, lhs=lhs: e.matmul(pb[bk][:, :], lhsT=lhs, rhs=w_out_sb[:, j * D + dh * 512:j * D + (dh + 1) * 512],
                                                                          start=(j == 0), stop=(j == 7)),
                           r=["aot%d" % c2, "boT", "w_out_sb"], w=[pbn[bk]])
                    DVE(lambda e, dh=dh, bk=bk: e.tensor_tensor(hnew[c2][:, dh * 512:(dh + 1) * 512], xc[c2][:, dh * 512:(dh + 1) * 512], pb[bk][:, :], ALU.add),
                        r=["xc%d" % c2], w=["hnew%d" % c2, pbn[bk]])
                S.dma("sp", h_dram[c * 128:(c + 1) * 128, :], hnew[c2][:], reads=["hnew%d" % c2], writes=["h_dram"])
        esL0.close()
        S.barrier()
        if debug_stop in ("C", "E0", "P0"):
            S.dma("sp", dbg[0:T, :], h_dram[:, :], reads=["h_dram"])
        if debug_stop in ("C", "Cb"):
            S.finish("sp")
            return nc
        NG = NG_
        SCRC = ["scrC%d" % k for k in range(16)]
        SCRA = ["scrA_h0", "scrA_h1"]
        SCRB = ["scrB_h0", "scrB_h1"] + ["scrB_h%d_t%d" % (hb_, t_) for hb_ in range(2) for t_ in range(16)]
        TOPN = ["top%d" % k for k in range(16)]
        ITOPN = ["itop%d" % k for k in range(16)]
        CTOPN = ["ctop%d" % k for k in range(8)]
        CIDXN = ["cidx%d" % k for k in range(8)]
        NCG = 128 // NG

        def peer_phase(layer, with_pool, final):
            with ExitStack() as esE:
                sbt = lambda es, name, shape, dt: es.enter_context(nc.sbuf_tensor("%s_L%d" % (name, layer), shape, dt))
                gffn = sbt(esE, "gffn", [128, D], F32)
                gple = sbt(esE, "gple", [128, D], F32)
                gx = sbt(esE, "gx", [128, D], F32)
                gfin = sbt(esE, "gfin", [128, D], F32)
                skT = sbt(esE, "skT", [128, 8 * 128], BF16)
                sk_st = sbt(esE, "sk_st", [128, 128], F32)
                pproj = sbt(esE, "pproj", [128, 2 * D], BF16)
                iota128 = sbt(esE, "iota128", [128, 128], F32)
                iota16 = sbt(esE, "iota16", [128, 16], F32)
                wbuf = sbt(esE, "wbuf", [128, 8 * 512], BF16)
                G_sb = sbt(esE, "G_sb", [128, 256 * 128], BF16)
                NSL = 3
                ubuf = [sbt(esE, "ubuf%d" % i, [128, 8 * NG * 128], BF16) for i in range(NSL)]
                vbuf = [sbt(esE, "vbuf%d" % i, [128, NG * D], BF16) for i in range(NSL)]
                ht2 = [sbt(esE, "ht_%d" % i, [128, 2 * D], F32) for i in range(2)]
                hnb = sbt(esE, "hnb", [128, 2 * D], BF16)
                hnT2 = [sbt(esE, "hnTe%d" % i, [128, 8 * 256], BF16) for i in range(2)]
                scrA = sbt(esE, "scrA", [128, 2048], F32)
                scrB = sbt(esE, "scrB", [128, 2048], F32)
                scrC = sbt(esE, "scrC", [128, 2048], F32)
                A16, B16, C16 = scrA[:].bitcast(BF16), scrB[:].bitcast(BF16), scrC[:].bitcast(BF16)
                qTs = sbt(esE, "qTs", [128, 8 * 256], BF16)
                rT = sbt(esE, "rT", [128, 3 * 256], F32)
                iota128b = sbt(esE, "iota128b", [128, 128], BF16)
                top = sbt(esE, "top", [128, 256], F32)
                itop = sbt(esE, "itop", [128, 256], U32)
                itopf = sbt(esE, "itopf", [128, 256], F32)
                ctop = sbt(esE, "ctop", [128, 128], F32)
                cidx = sbt(esE, "cidx", [128, 128], U32)
                ai = sbt(esE, "ai", [128, 128], U32)
                af_ = sbt(esE, "af_", [128, 128], F32)
                bf_ = sbt(esE, "bf_", [128, 128], F32)
                gate = sbt(esE, "gate", [128, 128], F32)
                i1f = sbt(esE, "i1f", [128, 128], F32)
                i2f = sbt(esE, "i2f", [128, 128], F32)
                zt = sbt(esE, "zt", [128, 8], F32)
                a_sb = [sbt(esE, "a_sb%d" % i, [128, 256], BF16) for i in range(2)]
                W_sb = [sbt(esE, "W_sb%d" % i, [128, 256], BF16) for i in range(2)]
                ptb = sbt(esE, "ptb", [128, 512], BF16)
                pTs = sbt(esE, "pTs", [128, 512], BF16)
                if with_pool:
                    poolA = sbt(esE, "poolA", [128, 12 * 128], BF16)
                    pw_sb = sbt(esE, "pw_sb", [128, 2048], BF16)
                    hprev = sbt(esE, "hprev", [128, D], BF16)
                    dT_sb = sbt(esE, "dT_sb", [128, 8 * 128], BF16)

                S.dma("sp", gffn[:], I["ffn_norm"][layer, :].partition_broadcast(128), writes=["gffn"])
                S.dma("sp", gple[:], I["ple_norm"][layer, :].partition_broadcast(128), writes=["gple"])
                S.dma("sp", gfin[:], I["final_norm"].partition_broadcast(128), writes=["gfin"])
                S.dma("sp", iota128[:], C["c_iota128"][:, :], writes=["iota128"])
                S.dma("pool", iota128b[:], C["c_iota128"][:, :], writes=["iota128b"])
                S.dma("sp", iota16[:], C["c_iota16"][:, :], writes=["iota16"])
                S.dma("pool", fap(pproj[:, 0:1], [[D, 2], [1, D]]), AP(I["ple_proj"].tensor, layer * 256 * D, [[D, 128], [128 * D, 2], [1, D]]),
                      writes=["pproj"])
                for h in range(8):
                    S.dma("sp", fap(sk_st[:, 0:1], [[64, 2], [1, 64]]),
                          AP(I["peer_subkeys"].tensor, (layer * 8 + h) * 2 * 128 * 64, [[64, 128], [128 * 64, 2], [1, 64]]), writes=["sk_st"])
                    PE(lambda e: e.transpose(pb[7][:, 0:128], sk_st[:], ident_f[:]), r=["sk_st", "ident_f"], w=[pbn[7]])
                    DVE(lambda e, h=h: e.tensor_copy(skT[:, h * 128:(h + 1) * 128], pb[7][:, 0:128]), w=["skT", pbn[7]])
                if with_pool:
                    S.dma("sp", gx[:], I["mix_norm"][1, :].partition_broadcast(128), writes=["gx"])
                    S.dma("pool", fap(poolA[:, 0:1], [[128, 12], [1, 128]]), AP(C["c_poolA"].tensor, 0, [[128, 128], [128 * 128, 12], [1, 128]]),
                          writes=["poolA"])
                    S.dma("sp", scrB[:, 0:D], I["pool_scale"].partition_broadcast(128), writes=[SCRB])
                    for gi in range(4):
                        S.dma("sp", fap(scrA[:, gi * 512:gi * 512 + 1], [[256, 2], [1, 256]]),
                              AP(I["pool_w"].tensor, gi * 256 * 256, [[256, 128], [128 * 256, 2], [1, 256]]), writes=[SCRA])
                    DVE(lambda e: e.tensor_tensor(fap(pw_sb[:, 0:1], [[512, 4], [256, 2], [1, 256]]), fap(scrA[:, 0:1], [[512, 4], [256, 2], [1, 256]]),
                                                  fap(scrB[:, 0:1], [[256, 4], [0, 2], [1, 256]]), ALU.mult), r=[SCRA, SCRB], w=["pw_sb"])

                seq = [(blk, cg) for blk in range(16) for cg in range(NCG)]

                def issue_uv(idx):
                    if idx >= len(seq):
                        return
                    blk_, cg_ = seq[idx]
                    sl = idx % NSL
                    S.dma("sp", ubuf[sl][:, :],
                          AP(u16.tensor, layer * D * 16384 + cg_ * 128 * 8 * NG * 128, [[8 * NG * 128, 128], [1, 8 * NG * 128]]),
                          reads=["u16_%d" % layer], writes=["ubuf%d" % sl])
                    S.dma("sp", vbuf[sl][:, :],
                          AP(v16.tensor, layer * 16384 * D + cg_ * 128 * NG * D, [[NG * D, 128], [1, NG * D]]),
                          reads=["v16_%d" % layer], writes=["vbuf%d" % sl])

                for i_ in range(NSL):
                    issue_uv(i_)

                def HT(blk, tt):
                    par = blk % 2
                    return ht2[par][:, tt * D:(tt + 1) * D], "ht%d_%d" % (par, tt)

                def front(blk):
                    tok0 = blk * 256
                    par = blk % 2
                    hnT = hnT2[par]
                    hnTn = "hnTe%d" % par
                    for tt in range(2):
                        hv, hn_ = HT(blk, tt)
                        S.dma("sp", hv, h_dram[tok0 + tt * 128:tok0 + (tt + 1) * 128, :], reads=["h_dram"], writes=[hn_])
                    yield
                    if with_pool:
                        for tt in range(2):
                            hv, hn_ = HT(blk, tt)
                            rmsnorm_tile(hv, hn_, gx[:], "gx", hnb[:, tt * D:(tt + 1) * D], "hnb%d" % tt, tt)
                            yield
                        for tt in range(2):
                            hv, hn_ = HT(blk, tt)
                            gt = blk * 2 + tt
                            cur = hnb[:, tt * D:(tt + 1) * D]
                            prv = hprev[:] if tt == 0 else hnb[:, 0:D]
                            prvn = "hprev" if tt == 0 else "hnb0"
                            for k in range(8):
                                wi = k // 2
                                bk = 6 + k // 4
                                kind = 2 if gt == 0 else 0
                                PE(lambda e, k=k, wi=wi, bk=bk, kind=kind, cur=cur: e.matmul(
                                    pb[bk][:, (k % 4) * 128:(k % 4 + 1) * 128], lhsT=cur[:, k * 128:(k + 1) * 128],
                                    rhs=poolA[:, (wi * 3 + kind) * 128:(wi * 3 + kind + 1) * 128], start=True, stop=(gt == 0)),
                                   r=["hnb%d" % tt, "poolA"], w=[pbn[bk]])
                                if gt > 0:
                                    PE(lambda e, k=k, wi=wi, bk=bk, prv=prv: e.matmul(
                                        pb[bk][:, (k % 4) * 128:(k % 4 + 1) * 128], lhsT=prv[:, k * 128:(k + 1) * 128],
                                        rhs=poolA[:, (wi * 3 + 1) * 128:(wi * 3 + 2) * 128], start=False, stop=True),
                                       r=[prvn, "poolA"], w=[pbn[bk]])
                                if k % 4 == 3:
                                    ACT(lambda e, bk=bk: e.copy(dT_sb[:, (bk - 6) * 512:(bk - 5) * 512], pb[bk][:, :]), w=["dT_sb", pbn[bk]])
                                    yield
                            for gi in range(4):
                                bk = 6 + gi // 2
                                for k2 in range(2):
                                    PE(lambda e, gi=gi, k2=k2, bk=bk: e.matmul(pb[bk][:, (gi % 2) * 256:(gi % 2 + 1) * 256],
                                                                              lhsT=dT_sb[:, (2 * gi + k2) * 128:(2 * gi + k2 + 1) * 128],
                                                                              rhs=pw_sb[:, (gi * 2 + k2) * 256:(gi * 2 + k2 + 1) * 256],
                                                                              start=(k2 == 0), stop=(k2 == 1)),
                                       r=["dT_sb", "pw_sb"], w=[pbn[bk]])
                            yield
                            for dh in range(2):
                                hvv = hv[:, dh * 512:(dh + 1) * 512]
                                DVE(lambda e, dh=dh, hvv=hvv: e.tensor_tensor(hvv, hvv, pb[6 + dh][:, :], ALU.add), w=[hn_, pbn[6 + dh]])
                            yield
                        DVE(lambda e: e.tensor_copy(hprev[:], hnb[:, D:2 * D]), r=["hnb1"], w=["hprev"])
                        yield
                    for tt in range(2):
                        hv, hn_ = HT(blk, tt)
                        rmsnorm_tile(hv, hn_, gffn[:], "gffn", hnb[:, tt * D:(tt + 1) * D], "hnb%d" % tt, tt)
                        yield
                        transpose_to(hnb[:, tt * D:(tt + 1) * D], "hnb%d" % tt, 6 + tt, fap(hnT[:, tt * 128:tt * 128 + 1], [[256, 8], [1, 128]]), hnTn)
                        yield
                    for h in range(8):
                        bk = 6 + h % 2
                        if h % 4 == 0:
                            S.dma("sp", fap(wbuf[:, 0:1], [[512, 8], [1, 512]]),
                                  AP(wq16.tensor, layer * D * D + (h // 4) * 512, [[D, 128], [128 * D, 8], [1, 512]]),
                                  reads=["wq16_%d" % layer], writes=["wbuf"])
                            yield
                        for kc in range(8):
                            PE(lambda e, h=h, kc=kc, bk=bk: e.matmul(pb[bk][:, 0:256], lhsT=wbuf[:, kc * 512 + (h % 4) * 128:kc * 512 + (h % 4 + 1) * 128],
                                                                     rhs=hnT[:, kc * 256:(kc + 1) * 256], start=(kc == 0), stop=(kc == 7)),
                               r=["wbuf", hnTn], w=[pbn[bk]])
                        ACT(lambda e, h=h, bk=bk: e.copy(qTs[:, h * 256:(h + 1) * 256], pb[bk][:, 0:256]), w=["qTs", pbn[bk]])
                        yield
                    for tt in range(2):
                        for r_ in range(2):
                            for hl in range(4):
                                h = r_ * 4 + hl
                                for s_ in range(2):
                                    PE(lambda e, h=h, hl=hl, s_=s_: e.matmul(pb[6 + s_][:, hl * 128:(hl + 1) * 128],
                                                                             lhsT=qTs[s_ * 64:(s_ + 1) * 64, h * 256 + tt * 128:h * 256 + (tt + 1) * 128],
                                                                             rhs=skT[s_ * 64:(s_ + 1) * 64, h * 128:(h + 1) * 128], start=True, stop=True),
                                       r=["qTs", "skT"], w=[pbn[6 + s_]])
                            for s_ in range(2):
                                ACT(lambda e, s_=s_, r_=r_: e.copy(fap(scrA[:, r_ * 1024 + s_ * 128:r_ * 1024 + s_ * 128 + 1], [[256, 4], [1, 128]]),
                                                                   fap(pb[6 + s_][:, 0:1], [[128, 4], [1, 128]])), w=[SCRA, pbn[6 + s_]])
                            yield
                        def ch(hs):
                            return (scrA[:, hs * 128:(hs + 1) * 128], top[:, hs * 16:hs * 16 + 8], top[:, hs * 16 + 8:hs * 16 + 16],
                                    itop[:, hs * 16:hs * 16 + 8], itop[:, hs * 16 + 8:hs * 16 + 16], scrC[:, hs * 128:(hs + 1) * 128])
                        for hs in range(16):
                            sv, t8a, t8b, i8a, i8b, sc = ch(hs)
                            DVE(lambda e: e.max(out=t8a, in_=sv), r=[SCRA], w=["top%d" % hs])
                            if hs % 4 == 3:
                                yield
                        for hs in range(16):
                            sv, t8a, t8b, i8a, i8b, sc = ch(hs)
                            DVE(lambda e: e.max_index(out=i8a, in_max=t8a, in_values=sv), r=[SCRA, "top%d" % hs], w=["itop%d" % hs])
                            if hs % 4 == 3:
                                yield
                        for hs in range(16):
                            sv, t8a, t8b, i8a, i8b, sc = ch(hs)
                            DVE(lambda e: e.match_replace(out=sc, in_to_replace=t8a, in_values=sv, imm_value=-3.0e38),
                                r=[SCRA, "top%d" % hs], w=["scrC%d" % hs])
                            if hs % 4 == 3:
                                yield
                        for hs in range(16):
                            sv, t8a, t8b, i8a, i8b, sc = ch(hs)
                            DVE(lambda e: e.max(out=t8b, in_=sc), r=["scrC%d" % hs], w=["top%d" % hs])
                            if hs % 4 == 3:
                                yield
                        for hs in range(16):
                            sv, t8a, t8b, i8a, i8b, sc = ch(hs)
                            DVE(lambda e: e.max_index(out=i8b, in_max=t8b, in_values=sc), r=["scrC%d" % hs, "top%d" % hs], w=["itop%d" % hs])
                            if hs % 4 == 3:
                                yield
                        DVE(lambda e: e.tensor_copy(itopf[:], itop[:]), r=ITOPN, w=["itopf"])
                        DVE(lambda e: e.tensor_tensor(fap(scrB[:, 0:1], [[256, 8], [16, 16], [1, 16]]), fap(top[:, 0:1], [[32, 8], [1, 16], [0, 16]]),
                                                      fap(top[:, 16:17], [[32, 8], [0, 16], [1, 16]]), ALU.add), r=TOPN, w=[SCRB])
                        yield

                        def cch(h):
                            return (scrB[:, h * 256:(h + 1) * 256], ctop[:, h * 16:h * 16 + 8], ctop[:, h * 16 + 8:h * 16 + 16],
                                    cidx[:, h * 16:h * 16 + 8], cidx[:, h * 16 + 8:h * 16 + 16], scrC[:, h * 256:(h + 1) * 256],
                                    ["scrC%d" % (2 * h), "scrC%d" % (2 * h + 1)])
                        for h in range(8):
                            cv, c8a, c8b, j8a, j8b, sc, scn = cch(h)
                            DVE(lambda e: e.max(out=c8a, in_=cv), r=[SCRB], w=["ctop%d" % h])
                            if h % 4 == 3:
                                yield
                        for h in range(8):
                            cv, c8a, c8b, j8a, j8b, sc, scn = cch(h)
                            DVE(lambda e: e.max_index(out=j8a, in_max=c8a, in_values=cv), r=[SCRB, "ctop%d" % h], w=["cidx%d" % h])
                            if h % 4 == 3:
                                yield
                        for h in range(8):
                            cv, c8a, c8b, j8a, j8b, sc, scn = cch(h)
                            DVE(lambda e: e.match_replace(out=sc, in_to_replace=c8a, in_values=cv, imm_value=-3.0e38),
                                r=[SCRB, "ctop%d" % h], w=scn)
                            if h % 4 == 3:
                                yield
                        for h in range(8):
                            cv, c8a, c8b, j8a, j8b, sc, scn = cch(h)
                            DVE(lambda e: e.max(out=c8b, in_=sc), r=scn, w=["ctop%d" % h])
                            if h % 4 == 3:
                                yield
                        for h in range(8):
                            cv, c8a, c8b, j8a, j8b, sc, scn = cch(h)
                            DVE(lambda e: e.max_index(out=j8b, in_max=c8b, in_values=sc), r=scn + ["ctop%d" % h], w=["cidx%d" % h])
                            if h % 4 == 3:
                                yield
                        DVE(lambda e: e.tensor_tensor(fap(gate[:, 0:1], [[16, 8], [1, 16]]), fap(ctop[:, 0:1], [[16, 8], [1, 16]]),
                                                      fap(ctop[:, 0:1], [[16, 8], [0, 16]]), ALU.subtract), r=CTOPN, w=["gate"])
                        yield
                        ACT(lambda e: e.activation(gate[:], gate[:], AF.Exp), w=["gate"])
                        yield
                        DVE(lambda e: e.tensor_reduce(zt[:, 0:8], fap(gate[:, 0:1], [[16, 8], [1, 16]]), AX.X, ALU.add), r=["gate"], w=["zt"])
                        DVE(lambda e: e.reciprocal(zt[:, 0:8], zt[:, 0:8]), w=["zt"])
                        DVE(lambda e: e.tensor_tensor(fap(gate[:, 0:1], [[16, 8], [1, 16]]), fap(gate[:, 0:1], [[16, 8], [1, 16]]),
                                                      fap(zt[:, 0:1], [[1, 8], [0, 16]]), ALU.mult), r=["zt"], w=["gate"])
                        yield
                        DVE(lambda e: e.tensor_single_scalar(ai[:], cidx[:], 4, ALU.logical_shift_right), r=CIDXN, w=["ai"])
                        DVE(lambda e: e.tensor_copy(af_[:], ai[:]), r=["ai"], w=["af_"])
                        DVE(lambda e: e.tensor_single_scalar(ai[:], cidx[:], 15, ALU.bitwise_and), r=CIDXN, w=["ai"])
                        DVE(lambda e: e.tensor_copy(bf_[:], ai[:]), r=["ai"], w=["bf_"])
                        yield
                        for (xf, off, dst, dn) in ((af_, 0, i1f, "i1f"), (bf_, 16, i2f, "i2f")):
                            DVE(lambda e, xf=xf: e.tensor_tensor(fap(scrC[:, 0:1], [[16, 128], [1, 16]]), fap(xf[:, 0:1], [[1, 128], [0, 16]]),
                                                                 fap(iota16[:, 0:1], [[0, 128], [1, 16]]), ALU.is_equal),
                                r=["af_", "bf_", "iota16"], w=SCRC)
                            yield
                            DVE(lambda e, off=off: e.tensor_tensor(fap(scrC[:, 0:1], [[256, 8], [16, 16], [1, 16]]), fap(scrC[:, 0:1], [[256, 8], [16, 16], [1, 16]]),
                                                                   fap(itopf[:, off:off + 1], [[32, 8], [0, 16], [1, 16]]), ALU.mult),
                                r=["itopf"], w=SCRC)
                            yield
                            DVE(lambda e, dst=dst: e.tensor_reduce(dst[:], fap(scrC[:, 0:1], [[16, 128], [1, 16]]), AX.X, ALU.add), r=SCRC, w=[dn])
                            yield
                        for k3, (src_, sn) in enumerate(((i1f, "i1f"), (i2f, "i2f"), (gate, "gate"))):
                            PE(lambda e, src_=src_: e.transpose(pb[7][:, 0:128], src_[:], ident_f[:]), r=[sn, "ident_f"], w=[pbn[7]])
                            DVE(lambda e, k3=k3: e.tensor_copy(rT[:, k3 * 256 + tt * 128:k3 * 256 + (tt + 1) * 128], pb[7][:, 0:128]), w=["rT", pbn[7]])
                            yield

                def gbuild(blk):
                    for tt in range(2):
                        for sb in range(8):
                            hb = sb % 2
                            tl0 = tt * 128 + sb * 16
                            o16 = hb * 2048
                            vv = [[128, 16], [1, 128]]
                            bn = ["scrB_h%d_t%d" % (hb, t_) for t_ in range(16)]
                            DVE(lambda e: e.tensor_tensor(fap(C16[:, o16:o16 + 1], vv), fap(iota128[:, 0:1], [[0, 16], [1, 128]]),
                                                          fap(rT[:, 256 + tl0:256 + tl0 + 1], [[1, 16], [0, 128]]), ALU.is_equal),
                                r=["iota128", "rT"], w=SCRC[hb * 8:(hb + 1) * 8])
                            for t_ in range(16):
                                DVE(lambda e, t_=t_: e.tensor_scalar(B16[:, o16 + t_ * 128:o16 + (t_ + 1) * 128], iota128b[:],
                                                                     rT[:, tl0 + t_:tl0 + t_ + 1], rT[:, 512 + tl0 + t_:512 + tl0 + t_ + 1],
                                                                     ALU.is_equal, ALU.mult), r=["iota128b", "rT"], w=[bn[t_], SCRB[hb]] if False else [bn[t_]])
                            for t4 in range(4):
                                bk = 4 + (sb * 4 + t4) % 2
                                for tq in range(4):
                                    tl_ = t4 * 4 + tq
                                    PE(lambda e, tl_=tl_, tq=tq, bk=bk: e.matmul(pb[bk][:, tq * 128:(tq + 1) * 128],
                                                                                 lhsT=B16[:, o16 + tl_ * 128:o16 + (tl_ + 1) * 128],
                                                                                 rhs=C16[:, o16 + tl_ * 128:o16 + (tl_ + 1) * 128], start=True, stop=True),
                                       r=[bn[tl_]] + SCRC[hb * 8:(hb + 1) * 8], w=[pbn[bk]])
                                ACT(lambda e, t4=t4, bk=bk: e.copy(G_sb[:, (tl0 + t4 * 4) * 128:(tl0 + t4 * 4 + 4) * 128], pb[bk][:, :]),
                                    w=["G_sb", pbn[bk]])

                def dense(blk, gen):
                    par = blk % 2
                    hnT = hnT2[par]
                    hnTn = "hnTe%d" % par
                    tok0 = blk * 256
                    junk32 = junk[:].bitcast(F32)
                    S.dma("sp", fap(junk32[:, 0:1], [[256, 2], [1, 256]]),
                          AP(I["p"].tensor, layer * T * 256 + tok0 * 256, [[256, 128], [128 * 256, 2], [1, 256]]), writes=["junk"])
                    DVE(lambda e: e.tensor_copy(ptb[:], junk32), r=["junk"], w=["ptb"])

                    def act_part(i2):
                        cg, ci = divmod(i2, NG)
                        idx = blk * NCG + cg
                        sl = idx % NSL
                        ba = 4 + i2 % 2
                        for kc in range(8):
                            PE(lambda e, kc=kc: e.matmul(pb[ba][:, 0:256], lhsT=ubuf[sl][:, kc * NG * 128 + ci * 128:kc * NG * 128 + (ci + 1) * 128],
                                                         rhs=hnT[:, kc * 256:(kc + 1) * 256], start=(kc == 0), stop=(kc == 7)),
                               r=["ubuf%d" % sl, hnTn], w=[pbn[ba]])
                        ACT(lambda e: e.activation(a_sb[i2 % 2][:], pb[ba][:, 0:256], AF.Gelu_apprx_tanh), w=["a_sb%d" % (i2 % 2), pbn[ba]])
                        DVE(lambda e: e.tensor_tensor(W_sb[i2 % 2][:], a_sb[i2 % 2][:], fap(G_sb[:, i2:i2 + 1], [[128, 256]]), ALU.mult),
                            r=["a_sb%d" % (i2 % 2), "G_sb"], w=["W_sb%d" % (i2 % 2)])

                    def out_part(i2):
                        cg, ci = divmod(i2, NG)
                        sl = (blk * NCG + cg) % NSL
                        for tt in range(2):
                            for dh in range(2):
                                PE(lambda e, tt=tt, dh=dh: e.matmul(pb[tt * 2 + dh][:, :], lhsT=W_sb[i2 % 2][:, tt * 128:(tt + 1) * 128],
                                                                    rhs=vbuf[sl][:, ci * D + dh * 512:ci * D + (dh + 1) * 512],
                                                                    start=(i2 == 0), stop=(i2 == 127)),
                                   r=["W_sb%d" % (i2 % 2), "vbuf%d" % sl], w=[pbn[tt * 2 + dh]])
                        if ci == NG - 1:
                            issue_uv(blk * NCG + cg + NSL)

                    act_part(0)
                    for i2 in range(128):
                        if i2 + 1 < 128:
                            act_part(i2 + 1)
                        out_part(i2)
                        if gen is not None and i2 >= 2:
                            next(gen, None)
                    if gen is not None:
                        for _ in gen:
                            pass

                def tail(blk):
                    par = blk % 2
                    hnT = hnT2[par]
                    hnTn = "hnTe%d" % par
                    tok0 = blk * 256
                    for tt in range(2):
                        hv, hn_ = HT(blk, tt)
                        for dh in range(2):
                            hvv = hv[:, dh * 512:(dh + 1) * 512]
                            DVE(lambda e, tt=tt, dh=dh, hvv=hvv: e.tensor_tensor(hvv, hvv, pb[tt * 2 + dh][:, :], ALU.add), w=[hn_, pbn[tt * 2 + dh]])
                    if debug_stop in ("P%d" % layer, "E%d" % layer):
                        for tt in range(2):
                            hv, hn_ = HT(blk, tt)
                            S.dma("sp", dbg[T + tok0 + tt * 128:T + tok0 + (tt + 1) * 128, :], hv, reads=[hn_])
                    for tt in range(2):
                        hv, hn_ = HT(blk, tt)
                        transpose_to(ptb[:, tt * 256:(tt + 1) * 256], "ptb", 6, fap(pTs[:, tt * 128:tt * 128 + 1], [[256, 2], [1, 128]]), "pTs", nk=2, eng="dve")
                        rmsnorm_tile(hv, hn_, gple[:], "gple", hnb[:, tt * D:(tt + 1) * D], "hnb%d" % tt, tt)
                        transpose_to(hnb[:, tt * D:(tt + 1) * D], "hnb%d" % tt, 7, fap(hnT[:, tt * 128:tt * 128 + 1], [[256, 8], [1, 128]]), hnTn)
                    for dh in range(2):
                        S.dma("sp", fap(wbuf[:, 0:1], [[512, 8], [1, 512]]),
                              AP(gw16.tensor, layer * D * D + dh * 512, [[D, 128], [128 * D, 8], [1, 512]]),
                              reads=["gw16_%d" % layer], writes=["wbuf"])
                        for tt in range(2):
                            hv, hn_ = HT(blk, tt)
                            bg, bp = (tt * 2 + dh) % 4, 4 + (tt * 2 + dh) % 2
                            for kc in range(8):
                                PE(lambda e, kc=kc, tt=tt, dh=dh, bg=bg: e.matmul(pb[bg][:, :], lhsT=hnT[:, kc * 256 + tt * 128:kc * 256 + (tt + 1) * 128],
                                                                                  rhs=wbuf[:, kc * 512:(kc + 1) * 512],
                                                                                  start=(kc == 0), stop=(kc == 7)), r=[hnTn, "wbuf"], w=[pbn[bg]])
                            for kc in range(2):
                                PE(lambda e, kc=kc, tt=tt, dh=dh, bp=bp: e.matmul(pb[bp][:, :], lhsT=pTs[:, kc * 256 + tt * 128:kc * 256 + (tt + 1) * 128],
                                                                                  rhs=pproj[:, kc * D + dh * 512:kc * D + (dh + 1) * 512],
                                                                                  start=(kc == 0), stop=(kc == 1)), r=["pTs", "pproj"], w=[pbn[bp]])
                            ACT(lambda e, bg=bg: e.activation(scrA[:, 0:512], pb[bg][:, :], AF.Sigmoid), w=[SCRA, pbn[bg]])
                            DVE(lambda e, bp=bp: e.tensor_tensor(scrB[:, 0:512], pb[bp][:, :], scrA[:, 0:512], ALU.mult), r=[SCRA], w=[SCRB, pbn[bp]])
                            hvv = hv[:, dh * 512:(dh + 1) * 512]
                            DVE(lambda e, hvv=hvv: e.tensor_tensor(hvv, hvv, scrB[:, 0:512], ALU.add), r=[SCRB], w=[hn_])
                    for tt in range(2):
                        hv, hn_ = HT(blk, tt)
                        rows = slice(tok0 + tt * 128, tok0 + (tt + 1) * 128)
                        if final:
                            rmsnorm_tile(hv, hn_, gfin[:], "gfin", scrC[:, 0:D], SCRC, tt)
                            S.dma("sp", out[rows, :], scrC[:, 0:D], reads=SCRC)
                        else:
                            S.dma("sp", h_dram[rows, :], hv, reads=[hn_], writes=["h_dram"])
                        if debug_stop == "E%d" % layer:
                            S.dma("sp", dbg[2 * T + tok0 + tt * 128:2 * T + tok0 + (tt + 1) * 128, :],
                                  (scrC[:, 0:D] if final else hv), reads=SCRC + [hn_])

                for _ in front(0):
                    pass
                gbuild(0)
                for blk in range(16):
                    gen = front(blk + 1) if blk + 1 < 16 else None
                    dense(blk, gen)
                    tail(blk)
                    if blk + 1 < 16:
                        gbuild(blk + 1)
            S.barrier()

        peer_phase(0, False, False)
        if debug_stop in ("P0", "E0"):
            S.finish("sp")
            return nc
        peer_phase(1, True, True)
        S.finish("sp")
        return nc


def _host_inputs(inputs, b):
    f = lambda a: np.ascontiguousarray(np.asarray(a, dtype=np.float32))
    m = {}
    m["x"] = f(inputs["x"][b])
    m["p"] = f(inputs["p"][:, b])
    m["mix_norm"] = f(inputs["mix_norm"])
    m["ab_w_in"] = f(inputs["ab_w_in"][0])
    m["ab_conv_w"] = f(inputs["ab_conv_w"][0]).reshape(124, 128)
    m["ab_conv_b"] = f(inputs["ab_conv_b"][0]).reshape(4, 128)
    m["ab_conv_ln_g"] = f(inputs["ab_conv_ln_g"][0]).reshape(4, 128)
    m["ab_conv_ln_b"] = f(inputs["ab_conv_ln_b"][0]).reshape(4, 128)
    m["ab_cmp_pos"] = f(inputs["ab_cmp_pos"][0])
    m["ab_cmp_w1"] = f(inputs["ab_cmp_w1"][0])
    m["ab_cmp_w2"] = f(inputs["ab_cmp_w2"][0])
    m["ab_w_out"] = f(inputs["ab_w_out"][0])
    m["pool_w"] = f(inputs["pool_w"][0])
    m["pool_scale"] = f(inputs["pool_scale"][0])
    for k in ("ffn_norm", "peer_wq", "peer_subkeys", "ple_norm", "ple_gate_w", "ple_proj", "final_norm"):
        m[k] = f(inputs[k])
    return m


def _run(inputs, debug_stop=None, ncores=8):
    shared = _host_inputs(inputs, 0)
    u = np.asarray(inputs["peer_u"], dtype=np.float32).reshape(2, 128, 128, D)
    u6 = u.reshape(2, 128, 128 // NG_, NG_, 8, 128)
    shared["uP"] = np.ascontiguousarray(u6.transpose(0, 2, 5, 4, 3, 1)).reshape(2, D, 16384)
    v = np.asarray(inputs["peer_v"], dtype=np.float32).reshape(2, 128, 128 // NG_, NG_, D)
    shared["vP"] = np.ascontiguousarray(v.transpose(0, 2, 1, 3, 4)).reshape(2, 16384, D)
    shared.update(make_consts())
    in_maps = []
    for b in range(ncores):
        m = dict(shared)
        m["x"] = np.ascontiguousarray(np.asarray(inputs["x"][b], dtype=np.float32))
        m["p"] = np.ascontiguousarray(np.asarray(inputs["p"][:, b], dtype=np.float32))
        in_maps.append(m)
    nc = build_program(debug_stop)
    res = run_bass_kernel_spmd(nc, in_maps, core_ids=list(range(ncores)))
    return res


def kernel(**inputs):
    res = _run(inputs, None)
    return np.stack([np.asarray(r["out"], dtype=np.float32) for r in res.results], axis=0)
```
